# Optimizing a Trainium2 kernel written in Bass

```python
import jax, jax.numpy as jnp
from jax import lax
import numpy as np

D_MODEL = 1024
BATCH = 8
SEQ = 4096
DEPTH = 4

N_GROUPS = 4
GROUP_WIDTH = D_MODEL // N_GROUPS
D_MIX = N_GROUPS * GROUP_WIDTH
CONV_KERNEL = 31
MLA_HEADS = 4
MLA_NOPE = 64
MLA_ROPE = 32
MLA_V = GROUP_WIDTH // MLA_HEADS
MLA_Q_RANK = 192
MLA_KV_RANK = 128
ROPE_BASE = 10000.0
ATTN_BLOCK = 128
POOL_WINDOWS = (2, 4, 8, 16)
POOL_CH = GROUP_WIDTH // len(POOL_WINDOWS)
GMLP_HEADS = 4
GMLP_HEAD_DIM = GROUP_WIDTH // GMLP_HEADS
GMLP_CHUNK = 128
D_FF = 4 * D_MODEL
NORM_EPS = 1e-6

D_CONV_IN = 2 * GROUP_WIDTH
D_MLA_IN = MLA_Q_RANK + MLA_KV_RANK + MLA_ROPE
D_POOL_IN = GROUP_WIDTH
D_GMLP_IN = 2 * GROUP_WIDTH
D_IN = D_CONV_IN + D_MLA_IN + D_POOL_IN + D_GMLP_IN

kernel_name = "hybrid_parallel_head_group_trunk"


def rms_norm(x, g):
    xf = x.astype(jnp.float32)
    y = xf * lax.rsqrt(jnp.mean(xf * xf, axis=-1, keepdims=True) + NORM_EPS)
    return (y * g.astype(jnp.float32)).astype(x.dtype)


def layer_norm(x, g, b):
    xf = x.astype(jnp.float32)
    mu = jnp.mean(xf, axis=-1, keepdims=True)
    var = jnp.mean(jnp.square(xf - mu), axis=-1, keepdims=True)
    y = (xf - mu) * lax.rsqrt(var + NORM_EPS)
    return (y * g.astype(jnp.float32) + b.astype(jnp.float32)).astype(x.dtype)


def rope_tables(positions):
    inv_freq = ROPE_BASE ** (-jnp.arange(0, MLA_ROPE, 2, dtype=jnp.float32) / MLA_ROPE)
    ang = positions.astype(jnp.float32)[..., None] * inv_freq
    return jnp.cos(ang), jnp.sin(ang)


def apply_rope(x, cos, sin):
    xf = x.astype(jnp.float32)
    x1, x2 = jnp.split(xf, 2, axis=-1)
    return jnp.concatenate([x1 * cos - x2 * sin, x2 * cos + x1 * sin], axis=-1).astype(x.dtype)


def conv_module(z, dw_w, dw_b, ln_g, ln_b, pw_w):
    a, g = jnp.split(z, 2, axis=-1)
    y = a * jax.nn.sigmoid(g)
    y = lax.conv_general_dilated(
        y, dw_w[:, None, :], window_strides=(1,), padding=[(CONV_KERNEL - 1, 0)],
        dimension_numbers=('NWC', 'WIO', 'NWC'), feature_group_count=GROUP_WIDTH) + dw_b
    y = jax.nn.silu(layer_norm(y, ln_g, ln_b))
    return y @ pw_w


def causal_block_attention(q, k, v):
    b, s, h, dk = q.shape
    n_blk = s // ATTN_BLOCK
    scale = dk ** -0.5
    q_blocks = q.reshape(b, n_blk, ATTN_BLOCK, h, dk).transpose(1, 0, 2, 3, 4)
    key_pos = jnp.arange(s)

    def attend(args):
        q_blk, blk = args
        scores = jnp.einsum('bqhd,bkhd->bhqk', q_blk, k).astype(jnp.float32) * scale
        query_pos = blk * ATTN_BLOCK + jnp.arange(ATTN_BLOCK)
        mask = key_pos[None, :] <= query_pos[:, None]
        probs = jax.nn.softmax(jnp.where(mask, scores, -jnp.inf), axis=-1).astype(v.dtype)
        return jnp.einsum('bhqk,bkhd->bqhd', probs, v)

    out = lax.map(attend, (q_blocks, jnp.arange(n_blk)))
    return out.transpose(1, 0, 2, 3, 4).reshape(b, s, h, v.shape[-1])


def latent_attention(z, cos, sin, q_norm_g, w_uq, kv_norm_g, w_ukv):
    b, s, _ = z.shape
    c_q, c_kv, k_rope = jnp.split(z, (MLA_Q_RANK, MLA_Q_RANK + MLA_KV_RANK), axis=-1)
    q = (rms_norm(c_q, q_norm_g) @ w_uq).reshape(b, s, MLA_HEADS, MLA_NOPE + MLA_ROPE)
    kv = (rms_norm(c_kv, kv_norm_g) @ w_ukv).reshape(b, s, MLA_HEADS, MLA_NOPE + MLA_V)
    q_nope, q_rope = jnp.split(q, (MLA_NOPE,), axis=-1)
    k_nope, v = jnp.split(kv, (MLA_NOPE,), axis=-1)
    q_rope = apply_rope(q_rope, cos[:, :, None, :], sin[:, :, None, :])
    k_rope = apply_rope(k_rope, cos, sin)[:, :, None, :]
    q = jnp.concatenate([q_nope, q_rope], axis=-1)
    k = jnp.concatenate([k_nope, jnp.broadcast_to(k_rope, (b, s, MLA_HEADS, MLA_ROPE))], axis=-1)
    return causal_block_attention(q, k, v).reshape(b, s, MLA_HEADS * MLA_V)


def multiscale_pool(z, pool_w, pool_scale):
    b, s, c = z.shape
    zf = z.astype(jnp.float32)
    csum = jnp.concatenate([jnp.zeros((b, 1, c), jnp.float32), jnp.cumsum(zf, axis=1)], axis=1)
    t = jnp.arange(s)
    outs = []
    for g, w in enumerate(POOL_WINDOWS):
        sl = slice(g * POOL_CH, (g + 1) * POOL_CH)
        lo = jnp.maximum(t + 1 - w, 0)
        win_sum = csum[:, 1:, sl] - csum[:, lo, sl]
        count = jnp.minimum(t + 1, w).astype(jnp.float32)
        y = win_sum / count[None, :, None] - zf[..., sl]
        outs.append(jnp.einsum('bsc,cd->bsd', y, pool_w[g].astype(jnp.float32)))
    return (jnp.concatenate(outs, axis=-1) * pool_scale.astype(jnp.float32)).astype(z.dtype)


def spatial_gating(z, norm_g, ws, bs):
    b, s, _ = z.shape
    u, v = jnp.split(z, 2, axis=-1)
    v = rms_norm(v, norm_g).reshape(b, s // GMLP_CHUNK, GMLP_CHUNK, GMLP_HEADS, GMLP_HEAD_DIM)
    causal = jnp.tril(jnp.ones((GMLP_CHUNK, GMLP_CHUNK), dtype=ws.dtype))
    gate = jnp.einsum('hij,bnjhd->bnihd', ws * causal, v) + bs.T[None, None, :, :, None]
    return u * gate.reshape(b, s, GROUP_WIDTH)


def setup_inputs(seed: int = 0) -> dict:
    key = jax.random.key(seed)
    ks = jax.random.split(key, 24)

    def normal(k, shape, scale):
        return jax.random.normal(k, shape, jnp.float32) * scale

    def gain(k, shape):
        return 1.0 + normal(k, shape, 0.05)

    positions = (jnp.arange(SEQ, dtype=jnp.int32)[None, :]
                 + jax.random.randint(ks[1], (BATCH, 1), 0, 1024, dtype=jnp.int32))
    return {
        'x': normal(ks[0], (BATCH, SEQ, D_MODEL), 1.0),
        'positions': positions,
        'mix_norm_g': gain(ks[2], (DEPTH, D_MODEL)),
        'w_in': normal(ks[3], (DEPTH, D_MODEL, D_IN), D_MODEL ** -0.5),
        'conv_dw_w': normal(ks[4], (DEPTH, CONV_KERNEL, GROUP_WIDTH), CONV_KERNEL ** -0.5),
        'conv_dw_b': normal(ks[5], (DEPTH, GROUP_WIDTH), 0.02),
        'conv_ln_g': gain(ks[6], (DEPTH, GROUP_WIDTH)),
        'conv_ln_b': normal(ks[7], (DEPTH, GROUP_WIDTH), 0.02),
        'conv_pw_w': normal(ks[8], (DEPTH, GROUP_WIDTH, GROUP_WIDTH), GROUP_WIDTH ** -0.5),
        'mla_q_norm_g': gain(ks[9], (DEPTH, MLA_Q_RANK)),
        'mla_w_uq': normal(ks[10], (DEPTH, MLA_Q_RANK, MLA_HEADS * (MLA_NOPE + MLA_ROPE)), MLA_Q_RANK ** -0.5),
        'mla_kv_norm_g': gain(ks[11], (DEPTH, MLA_KV_RANK)),
        'mla_w_ukv': normal(ks[12], (DEPTH, MLA_KV_RANK, MLA_HEADS * (MLA_NOPE + MLA_V)), MLA_KV_RANK ** -0.5),
        'pool_w': normal(ks[13], (DEPTH, len(POOL_WINDOWS), POOL_CH, POOL_CH), POOL_CH ** -0.5),
        'pool_scale': gain(ks[14], (DEPTH, GROUP_WIDTH)),
        'gmlp_norm_g': gain(ks[15], (DEPTH, GROUP_WIDTH)),
        'gmlp_ws': normal(ks[16], (DEPTH, GMLP_HEADS, GMLP_CHUNK, GMLP_CHUNK), GMLP_CHUNK ** -0.5),
        'gmlp_bs': 1.0 + normal(ks[17], (DEPTH, GMLP_HEADS, GMLP_CHUNK), 0.1),
        'group_norm_g': gain(ks[18], (DEPTH, N_GROUPS, GROUP_WIDTH)),
        'w_out': normal(ks[19], (DEPTH, D_MIX, D_MODEL), D_MIX ** -0.5),
        'ffn_norm_g': gain(ks[20], (DEPTH, D_MODEL)),
        'w_ff1': normal(ks[21], (DEPTH, D_MODEL, D_FF), D_MODEL ** -0.5),
        'w_ff2': normal(ks[22], (DEPTH, D_FF, D_MODEL), D_FF ** -0.5),
        'final_norm_g': gain(ks[23], (D_MODEL,)),
    }


def reference(x, positions, mix_norm_g, w_in, conv_dw_w, conv_dw_b, conv_ln_g, conv_ln_b, conv_pw_w,
              mla_q_norm_g, mla_w_uq, mla_kv_norm_g, mla_w_ukv, pool_w, pool_scale,
              gmlp_norm_g, gmlp_ws, gmlp_bs, group_norm_g, w_out, ffn_norm_g, w_ff1, w_ff2,
              final_norm_g):
    cos, sin = rope_tables(positions)
    split_at = (D_CONV_IN, D_CONV_IN + D_MLA_IN, D_CONV_IN + D_MLA_IN + D_POOL_IN)
    h = x
    for l in range(DEPTH):
        z = rms_norm(h, mix_norm_g[l]) @ w_in[l]
        z_conv, z_mla, z_pool, z_gmlp = jnp.split(z, split_at, axis=-1)
        o_conv = conv_module(z_conv, conv_dw_w[l], conv_dw_b[l], conv_ln_g[l], conv_ln_b[l], conv_pw_w[l])
        o_mla = latent_attention(z_mla, cos, sin, mla_q_norm_g[l], mla_w_uq[l], mla_kv_norm_g[l], mla_w_ukv[l])
        o_pool = multiscale_pool(z_pool, pool_w[l], pool_scale[l])
        o_gmlp = spatial_gating(z_gmlp, gmlp_norm_g[l], gmlp_ws[l], gmlp_bs[l])
        mixed = jnp.concatenate([rms_norm(o_conv, group_norm_g[l, 0]), rms_norm(o_mla, group_norm_g[l, 1]),
                                 rms_norm(o_pool, group_norm_g[l, 2]), rms_norm(o_gmlp, group_norm_g[l, 3])],
                                axis=-1)
        h = h + mixed @ w_out[l]
        f = jnp.square(jax.nn.relu(rms_norm(h, ffn_norm_g[l]) @ w_ff1[l]))
        h = h + f @ w_ff2[l]
    return rms_norm(h, final_norm_g)
```

```python
from contextlib import ExitStack
import math
import numpy as np
import concourse.bass as bass
import concourse.mybir as mybir
from concourse.bass_utils import run_bass_kernel_spmd

F32 = mybir.dt.float32
BF16 = mybir.dt.bfloat16
I32 = mybir.dt.int32
ALU = mybir.AluOpType
AF = mybir.ActivationFunctionType

D = 1024
DIN = 1632
DFF = 4096
EPS = 1e-6
NCOL = 83
NROW = 2304
NCST = 291
ENGS = ("pe", "act", "dve", "pool", "sp")


class Res:
    __slots__ = ("name", "w", "rs")

    def __init__(self, name):
        self.name = name
        self.w = None
        self.rs = []


class Slot:
    __slots__ = ("name", "n", "last", "sem")

    def __init__(self, name):
        self.name = name
        self.n = 0
        self.last = None
        self.sem = None


class Op:
    __slots__ = ("eng", "fn", "deps", "sig", "val", "slot")


class Sync:
    def __init__(self, nc, stack):
        self.nc = nc
        self.stack = stack
        self.esem = {e: stack.enter_context(nc.semaphore(f"s_{e}")) for e in ENGS}
        self.ecount = {e: 0 for e in ENGS}
        self.slots = {}


class Prog:
    def __init__(self, nc, name, sync):
        self.nc = nc
        self.name = name
        self.sync = sync
        self.ops = {e: [] for e in ENGS}
        self.slots = sync.slots
        for s in self.slots.values():
            s.last = None
        self.used = []
        self.res = {}
        self.cap = None

    def R(self, name):
        r = self.res.get(name)
        if r is None:
            r = self.res[name] = Res(name)
        return r

    def slot(self, name):
        s = self.slots.get(name)
        if s is None:
            s = self.slots[name] = Slot(name)
        return s

    def add(self, eng, fn, reads=(), writes=(), slot=None):
        if self.cap is not None:
            self.cap.append((eng, fn, list(reads), list(writes), slot))
            return None
        reads = [self.R(r) if isinstance(r, str) else r for r in reads]
        writes = [self.R(r) if isinstance(r, str) else r for r in writes]
        if isinstance(slot, str):
            slot = self.slot(slot)
        op = Op()
        op.eng = eng
        op.fn = fn
        op.sig = slot is not None
        op.val = None
        op.slot = slot
        deps = []
        xr = [r for r in reads if r.name.startswith("bank")]
        xw = [r for r in writes if r.name.startswith("bank")]
        reads = [r for r in reads if not r.name.startswith("bank")]
        writes = [r for r in writes if not r.name.startswith("bank")]
        for r, kind in [(r, "r") for r in xr] + [(r, "w") for r in xw]:
            if r.w is not None:
                pk = r.rs[0] if r.rs else "w"
                if not (r.w.eng == eng and pk == "r" and kind == "r"):
                    deps.append(r.w)
            r.w = op
            r.rs = [kind]
        for r in reads:
            if r.w is not None:
                deps.append(r.w)
        for r in writes:
            if r.w is not None:
                deps.append(r.w)
            deps.extend(r.rs)
        if slot is not None and slot.last is not None:
            deps.append(slot.last)
        for r in reads:
            r.rs.append(op)
        for r in writes:
            r.w = op
            r.rs = []
        if slot is not None:
            slot.last = op
            slot.n += 1
            op.val = 16 * slot.n
            if slot not in self.used:
                self.used.append(slot)
        dd = []
        seen = set()
        for d in deps:
            if d is op or id(d) in seen:
                continue
            seen.add(id(d))
            if d.eng == "pe" and eng == "pe" and d.slot is None and slot is None:
                continue
            d.sig = True
            dd.append(d)
        op.deps = dd
        self.ops[eng].append(op)
        return op

    def replay(self, items):
        for it in items:
            self.add(*it)

    def emit(self):
        nc = self.nc
        sync = self.sync
        for e in ENGS:
            c = sync.ecount[e]
            for op in self.ops[e]:
                if op.slot is None and op.sig:
                    c += 1
                    op.val = c
            sync.ecount[e] = c
        with ExitStack() as st:
            esem = sync.esem
            fin = list(self.used)
            for s in fin:
                if s.sem is None:
                    s.sem = sync.stack.enter_context(nc.semaphore(f"d_{s.name}"))
            block = st.enter_context(nc.Block())

            def run(eng_name, e):
                known = {}
                for op in self.ops[eng_name]:
                    for d in op.deps:
                        sem = d.slot.sem if d.slot is not None else esem[d.eng]
                        k = id(sem)
                        if known.get(k, 0) >= d.val:
                            continue
                        known[k] = d.val
                        e.wait_ge(sem, d.val)
                    ins = op.fn(e)
                    if op.slot is not None:
                        ins.then_inc(op.slot.sem, 16)
                    elif op.sig:
                        ins.then_inc(esem[eng_name], 1)
                if eng_name == "sp":
                    for s in fin:
                        e.wait_ge(s.sem, 16 * s.n)

            block.tensor(lambda e: run("pe", e))
            block.scalar(lambda e: run("act", e))
            block.vector(lambda e: run("dve", e))
            block.gpsimd(lambda e: run("pool", e))
            block.sync(lambda e: run("sp", e))


class Rot:
    def __init__(self, items):
        self.items = items
        self.i = 0

    def next(self):
        it = self.items[self.i % len(self.items)]
        self.i += 1
        return it


def build_program(T, n_layers, final):
    NB = T // 512
    NT = T // 128
    nc = bass.Bass("TRN2", target_bir_lowering=False)

    def din(name, shape, dt=F32):
        return nc.dram_tensor(name, list(shape), dt, kind="ExternalInput").ap()

    L = n_layers
    x = din("x", [T, D])
    pos = din("pos", [1, T], I32)
    w_in = din("w_in", [L, D, DIN])
    w_out = din("w_out", [L, D, D])
    w_uq = din("w_uq", [L, 192, 384])
    w_ukv = din("w_ukv", [L, 128, 512])
    conv_pw = din("conv_pw", [L, 256, 256])
    poolw = din("poolw", [L, 128, 2, 128])
    wsT = din("wsT", [L, 128, 4, 128])
    bsT = din("bsT", [L, 128, 2, 128])
    colp = din("colp", [L, 128, NCOL])
    rowp = din("rowp", [L, NROW])
    w_ff1 = din("w_ff1", [L, D, DFF])
    w_ff2 = din("w_ff2", [L, DFF, D])
    fng = din("fng", [1, D])
    cst = din("cst", [128, NCST])
    out = nc.dram_tensor("out", [T, D], F32, kind="ExternalOutput").ap()
    hbuf = nc.dram_tensor("hbuf", [T, D], F32).ap()
    ropetab = nc.dram_tensor("ropetab", [2, 32, T], F32).ap()

    with ExitStack() as top:
        banks = [top.enter_context(nc.psum_tensor(f"bank{i}", [128, 512], F32)) for i in range(8)]
        sync = Sync(nc, top)

        with ExitStack() as st:
            P = Prog(nc, "pr", sync)
            sb = lambda n, s, d: st.enter_context(nc.sbuf_tensor(n, s, d))
            posi = sb("posi", [32, T], I32)
            posf = sb("posf", [32, T], F32)
            ang = sb("ang", [32, T], F32)
            u = sb("u", [32, T], F32)
            ki = sb("ki", [32, T], I32)
            kf = sb("kf", [32, T], F32)
            ng = sb("ng", [32, T], F32)
            tab = [sb("tab0", [32, T], F32), sb("tab1", [32, T], F32)]
            c_sb = sb("c_sb", [128, NCST], F32)
            nbias = sb("nbias", [32, 1], F32)
            P.add("sp", lambda e: e.dma_start(out=posi[:], in_=pos[0, :].partition_broadcast(32)), writes=["posi"], slot="posi")
            P.add("sp", lambda e: e.dma_start(out=c_sb[:], in_=cst), writes=["c"], slot="c")
            P.add("dve", lambda e: e.tensor_copy(out=posf[:], in_=posi[:]), reads=["posi"], writes=["posf"])
            P.add("dve", lambda e: e.memset(nbias[:], -math.pi * (1 - 1e-6)), writes=["nbias"])
            P.add("dve", lambda e: e.tensor_scalar(out=ang[:], in0=posf[:], scalar1=c_sb[0:32, 290:291], scalar2=None, op0=ALU.mult),
                  reads=["posf", "c"], writes=["ang"])
            for i, shift in enumerate((0.75, 0.5)):
                P.add("dve", lambda e, shift=shift: e.tensor_scalar(out=u[:], in0=ang[:], scalar1=1.0 / (2 * math.pi), scalar2=shift, op0=ALU.mult, op1=ALU.add),
                      reads=["ang"], writes=["u"])
                P.add("dve", lambda e: e.tensor_copy(out=ki[:], in_=u[:]), reads=["u"], writes=["ki"])
                P.add("dve", lambda e: e.tensor_copy(out=kf[:], in_=ki[:]), reads=["ki"], writes=["kf"])
                P.add("dve", lambda e: e.tensor_tensor(out=u[:], in0=u[:], in1=kf[:], op=ALU.subtract), reads=["u", "kf"], writes=["u"])
                P.add("dve", lambda e: e.tensor_scalar(out=ng[:], in0=u[:], scalar1=0.0, scalar2=None, op0=ALU.is_lt), reads=["u"], writes=["ng"])
                P.add("dve", lambda e: e.tensor_tensor(out=u[:], in0=u[:], in1=ng[:], op=ALU.add), reads=["u", "ng"], writes=["u"])
                P.add("act", lambda e, i=i: e.activation(out=tab[i][:], in_=u[:], func=AF.Sin, scale=2 * math.pi * (1 - 1e-6), bias=nbias[:]),
                      reads=["u", "nbias"], writes=[f"tab{i}"])
                P.add("sp", lambda e, i=i: e.dma_start(out=ropetab[i], in_=tab[i][:]), reads=[f"tab{i}"], slot=f"tabo{i}")
            P.emit()

        for l in range(n_layers):
            hsrc = x if l == 0 else hbuf
            if True:
              emit_pass_a(nc, sync, l, T, NB, NT, banks, hsrc, hbuf, ropetab, pos,
                        dict(w_in=w_in, w_out=w_out, w_uq=w_uq, w_ukv=w_ukv, conv_pw=conv_pw, poolw=poolw,
                             wsT=wsT, bsT=bsT, colp=colp, rowp=rowp, cst=cst))
            is_last = (l == n_layers - 1)
            if True:
              emit_pass_b(nc, sync, l, T, NT, banks, hbuf, out, dict(w_ff1=w_ff1, w_ff2=w_ff2, rowp=rowp, fng=fng, cst=cst),
                        do_final=(is_last and final), to_out=is_last)
    return nc


def emit_pass_b(nc, sync, l, T, NT, banks, hbuf, out, W, do_final, to_out):
    NBB = T // 256
    with ExitStack() as st:
        P = Prog(nc, f"b{l}", sync)
        sb = lambda n, s, d: st.enter_context(nc.sbuf_tensor(f"B{l}_{n}", s, d))
        W1 = sb("W1", [128, 8, DFF], BF16)
        W2 = sb("W2", [128, 32, D], BF16)
        gff = sb("gff", [128, D], F32)
        gfin = sb("gfin", [128, D], F32)
        c_sb = sb("c_sb", [128, 128], F32)
        ident = sb("ident", [128, 128], BF16)
        hn = [sb(f"hn{i}", [128, D], F32) for i in range(4)]
        xn = [sb(f"xn{i}", [128, D], BF16) for i in range(4)]
        xnT = [sb(f"xnT{i}", [128, 8, 256], BF16) for i in range(2)]
        rr = [sb(f"r{i}", [128, 256], F32) for i in range(3)]
        fT = [sb(f"fT{i}", [128, 256], BF16) for i in range(3)]
        junk = sb("junk", [128, D], BF16)
        ss = [sb(f"ss{i}", [128, 1], F32) for i in range(4)]
        rs = [sb(f"rs{i}", [128, 1], F32) for i in range(4)]
        ss2 = [sb(f"ss2{i}", [128, 1], F32) for i in range(4)]
        rs2 = [sb(f"rs2{i}", [128, 1], F32) for i in range(4)]
        epsc = sb("epsc", [128, 1], F32)

        w1v = W["w_ff1"][l].rearrange("(k p) n -> p k n", p=128)
        w2v = W["w_ff2"][l].rearrange("(c p) n -> p c n", p=128)
        P.add("sp", lambda e: e.dma_start(out=c_sb[:], in_=W["cst"][:, 0:128]), writes=["c"], slot="c")
        P.add("sp", lambda e: e.dma_start(out=gff[:], in_=W["rowp"][l, 1024:2048].partition_broadcast(128)), writes=["gff"], slot="gff")
        if do_final:
            P.add("sp", lambda e: e.dma_start(out=gfin[:], in_=W["fng"][0, :].partition_broadcast(128)), writes=["gfin"], slot="gfin")
        P.add("dve", lambda e: e.tensor_copy(out=ident[:], in_=c_sb[:]), reads=["c"], writes=["ident"])
        P.add("dve", lambda e: e.memset(epsc[:], EPS), writes=["eps"])
        for g in range(8):
            P.add("pool", lambda e, g=g: e.dma_start(out=W1[:, :, g * 512:(g + 1) * 512], in_=w1v[:, :, g * 512:(g + 1) * 512]),
                  writes=[f"W1g{g}"], slot=f"w1_{g % 4}")
            P.add("pool", lambda e, g=g: e.dma_start(out=W2[:, 4 * g:4 * g + 4, :], in_=w2v[:, 4 * g:4 * g + 4, :]),
                  writes=[f"W2g{g}"], slot=f"w2_{g % 4}")

        trb7 = banks[7][:, :].bitcast(BF16)

        def prep_load(bb):
            for i in range(2):
                t = 2 * bb + i
                s = (bb % 2) * 2 + i
                P.add("sp", lambda e, t=t, s=s: e.dma_start(out=hn[s][:], in_=hbuf[t * 128:(t + 1) * 128, :]),
                      reads=[f"hd{t}"], writes=[f"hn{s}"], slot=f"hn{s}")
                P.add("act", lambda e, s=s: e.activation(out=junk[:], in_=hn[s][:], func=AF.Square, accum_out=ss[s][:]),
                      reads=[f"hn{s}"], writes=[f"ss{s}"])
                P.add("act", lambda e, s=s: e.activation(out=rs[s][:], in_=ss[s][:], func=AF.Ln, scale=1.0 / D, bias=epsc[:]),
                      reads=[f"ss{s}", "eps"], writes=[f"sd{s}"])
                P.add("act", lambda e, s=s: e.activation(out=rs[s][:], in_=rs[s][:], func=AF.Exp, scale=-0.5), reads=[f"sd{s}"], writes=[f"sd{s}"])
                P.add("dve", lambda e, s=s: e.scalar_tensor_tensor(out=xn[s][:], in0=hn[s][:], scalar=rs[s][:], in1=gff[:], op0=ALU.mult, op1=ALU.mult),
                      reads=[f"hn{s}", f"sd{s}", "gff"], writes=[f"xn{s}"])

        def prep_tr(bb):
            xt = xnT[bb % 2]
            for hb in range(2):
                for i in range(2):
                    s = (bb % 2) * 2 + i
                    for kk in range(4):
                        k = hb * 4 + kk
                        off = kk * 256 + i * 128
                        P.add("pe", lambda e, s=s, k=k, off=off: e.transpose(out=trb7[:, off:off + 128], in_=xn[s][:, k * 128:(k + 1) * 128], identity=ident[:]),
                              reads=[f"xn{s}", "ident"], writes=["bank7"])
                dst = xt[:, hb * 4:hb * 4 + 4, :].rearrange("p k t -> p (k t)")
                if hb == 0:
                    P.add("dve", lambda e, dst=dst: e.tensor_copy(out=dst, in_=trb7[:, :]), reads=["bank7"], writes=[f"xnT{bb % 2}a"])
                else:
                    P.add("act", lambda e, dst=dst: e.activation(out=dst, in_=trb7[:, :], func=AF.Copy), reads=["bank7"], writes=[f"xnT{bb % 2}b"])

        def ff1(bb, c):
            xt = xnT[bb % 2]
            q = c % 3
            pb = banks[4 + q][:, 0:256]
            for k in range(8):
                P.add("pe", lambda e, k=k, pb=pb, xt=xt, c=c: e.matmul(pb, lhsT=W1[:, k, c * 128:(c + 1) * 128], rhs=xt[:, k, :], start=(k == 0), stop=(k == 7)),
                      reads=[f"W1g{c // 4}", f"xnT{bb % 2}a", f"xnT{bb % 2}b"], writes=[f"bank{4 + q}"])
            j = c % 3
            P.add("act", lambda e, pb=pb, j=j: e.activation(out=rr[j][:], in_=pb, func=AF.Relu), reads=[f"bank{4 + q}"], writes=[f"r{j}"])
            P.add("pool", lambda e, j=j: e.tensor_tensor(out=fT[j][:], in0=rr[j][:], in1=rr[j][:], op=ALU.mult), reads=[f"r{j}"], writes=[f"fT{j}"])

        def ff2(bb, c):
            j = c % 3
            for i in range(2):
                for n in range(2):
                    P.add("pe", lambda e, i=i, n=n, j=j, c=c: e.matmul(banks[i * 2 + n][:, :], lhsT=fT[j][:, i * 128:(i + 1) * 128], rhs=W2[:, c, n * 512:(n + 1) * 512],
                                                                 start=(c == 0), stop=(c == 31)),
                          reads=[f"fT{j}", f"W2g{c // 4}"], writes=[f"bank{i * 2 + n}"])

        def epilogue(bb):
            for i in range(2):
                t = 2 * bb + i
                s = (bb % 2) * 2 + i
                for n in range(2):
                    P.add("dve", lambda e, i=i, n=n, s=s: e.tensor_tensor(out=hn[s][:, n * 512:(n + 1) * 512], in0=banks[i * 2 + n][:, :], in1=hn[s][:, n * 512:(n + 1) * 512], op=ALU.add),
                          reads=[f"bank{i * 2 + n}", f"hn{s}"], writes=[f"hn{s}"])
                if do_final:
                    P.add("act", lambda e, s=s: e.activation(out=junk[:], in_=hn[s][:], func=AF.Square, accum_out=ss2[s][:]),
                          reads=[f"hn{s}"], writes=[f"ss2{s}"])
                    P.add("act", lambda e, s=s: e.activation(out=rs2[s][:], in_=ss2[s][:], func=AF.Ln, scale=1.0 / D, bias=epsc[:]),
                          reads=[f"ss2{s}", "eps"], writes=[f"sd2{s}"])
                    P.add("act", lambda e, s=s: e.activation(out=rs2[s][:], in_=rs2[s][:], func=AF.Exp, scale=-0.5), reads=[f"sd2{s}"], writes=[f"sd2{s}"])
                    P.add("dve", lambda e, s=s: e.scalar_tensor_tensor(out=hn[s][:], in0=hn[s][:], scalar=rs2[s][:], in1=gfin[:], op0=ALU.mult, op1=ALU.mult),
                          reads=[f"hn{s}", f"sd2{s}", "gfin"], writes=[f"hn{s}"])
                dst = out if to_out else hbuf
                P.add("sp", lambda e, t=t, s=s, dst=dst: e.dma_start(out=dst[t * 128:(t + 1) * 128, :], in_=hn[s][:]),
                      reads=[f"hn{s}"], writes=[f"hd{t}"], slot=f"st{s}")

        prep_load(0)
        prep_tr(0)
        for bb in range(NBB):
            for c in range(32):
                ff1(bb, c)
                if c > 0:
                    ff2(bb, c - 1)
                if c == 4 and bb + 1 < NBB:
                    prep_load(bb + 1)
                if c == 20 and bb + 1 < NBB:
                    prep_tr(bb + 1)
            ff2(bb, 31)
            epilogue(bb)
        P.emit()


def emit_pass_a(nc, sync, l, T, NB, NT, banks, hsrc, hbuf, ropetab, pos, W):
    with ExitStack() as st:
        P = Prog(nc, f"a{l}", sync)
        sb = lambda n, s, d: st.enter_context(nc.sbuf_tensor(f"A{l}_{n}", s, d))
        w_in_sb = sb("w_in", [128, 8, DIN], BF16)
        w_krr = sb("w_krr", [128, 8, 128], BF16)
        w_out_sb = sb("w_out", [128, 8, D], BF16)
        w_uq_sb = sb("w_uq", [128, 2, 416], BF16)
        w_uqr = sb("w_uqr", [128, 2, 416], BF16)
        w_ukv_sb = sb("w_ukv", [128, 512], BF16)
        pw_sb = sb("pw", [128, 2, 256], BF16)
        w_kn = sb("w_kn", [128, 256], BF16)
        w_v = sb("w_v", [128, 256], BF16)
        poolw_sb = sb("poolw", [128, 2, 128], BF16)
        wsT_sb = sb("wsT", [128, 4, 128], BF16)
        dg = sb("dg", [128, 8, 128], BF16)
        colp = sb("colp", [128, NCOL], F32)
        gmix = sb("gmix", [128, D], F32)
        ggm = sb("ggm", [128, 256], F32)
        bsT = sb("bsT", [128, 2, 128], F32)
        c_sb = sb("c_sb", [128, NCST], F32)
        ident = sb("ident", [128, 128], BF16)
        maskT = sb("maskT", [128, 128], BF16)
        ones = sb("ones", [128, 128], BF16)
        epsc = sb("epsc", [128, 1], F32)
        onec = sb("onec", [128, 1], F32)
        KT = sb("KT", [128, 4, T], BF16)
        Vt = sb("Vt", [128, NT, 4, 65], BF16)
        ycv = sb("ycv", [128, 2, 542], BF16)
        zp = sb("zp", [128, 2, 527], F32)
        hn = [sb(f"hn{i}", [128, D], F32) for i in range(2)]
        xn = [sb(f"xn{i}", [128, D], BF16) for i in range(4)]
        hr = [sb(f"hr{i}", [128, D], F32) for i in range(2)]
        xnT = sb("xnT", [128, 8, 512], BF16)
        mixedT = sb("mixedT", [128, 8, 512], BF16)
        QT = sb("QT", [128, 4, 512], BF16)
        PT = [sb(f"PT{i}", [128, 512], BF16) for i in range(3)]
        cos2 = sb("cos2", [128, 512], F32)
        sin2 = sb("sin2", [128, 512], F32)
        junk = sb("junk", [128, 256], BF16)
        Fm = [sb(f"F{i}", [128, 512], F32) for i in range(10)]
        Hm = [sb(f"H{i}", [128, 512], BF16) for i in range(6)]
        PA = sb("PA", [128, 527], F32)
        PB = sb("PB", [128, 527], F32)
        omla = sb("omla", [128, 4, 256], F32)
        vn = sb("vn", [128, 4, 256], BF16)
        mtok = [sb(f"mtok{i}", [128, 256], BF16) for i in range(2)]
        sm = sb("sm", [128, 64], F32)

        def F(i):
            return Fm[i], f"F{i}"

        def H(i):
            return Hm[i], f"H{i}"

        P.add("sp", lambda e: e.dma_start(out=c_sb[:], in_=W["cst"]), writes=["c"], slot="c")
        P.add("sp", lambda e: e.dma_start(out=colp[:], in_=W["colp"][l]), writes=["colp"], slot="colp")
        P.add("sp", lambda e: e.dma_start(out=gmix[:], in_=W["rowp"][l, 0:1024].partition_broadcast(128)), writes=["gmix"], slot="gmix")
        P.add("sp", lambda e: e.dma_start(out=ggm[:], in_=W["rowp"][l, 2048:2304].partition_broadcast(128)), writes=["ggm"], slot="ggm")
        P.add("sp", lambda e: e.dma_start(out=bsT[:], in_=W["bsT"][l]), writes=["bsT"], slot="bsT")
        P.add("dve", lambda e: e.tensor_copy(out=ident[:], in_=c_sb[:, 0:128]), reads=["c"], writes=["ident"])
        P.add("dve", lambda e: e.tensor_copy(out=maskT[:], in_=c_sb[:, 128:256]), reads=["c"], writes=["maskT"])
        P.add("dve", lambda e: e.memset(ones[:], 1.0), writes=["ones"])
        P.add("dve", lambda e: e.memset(epsc[:], EPS), writes=["eps"])
        P.add("dve", lambda e: e.memset(onec[:], 1.0), writes=["eps"])
        P.add("pool", lambda e: e.memset(KT[96:128, :, :], 0.0), writes=["KTpad"])
        P.add("pool", lambda e: e.memset(QT[96:128, :, :], 0.0), writes=["QTpad"])
        P.add("pool", lambda e: e.memset(w_uq_sb[:, :, 384:416], 0.0), writes=["w_uq"])
        P.add("pool", lambda e: e.memset(w_uq_sb[64:128, 1, :], 0.0), writes=["w_uq"])
        P.add("pool", lambda e: e.memset(w_uqr[:], 0.0), writes=["w_uqr"])
        P.add("pool", lambda e: e.memset(ycv[:], 0.0), writes=["ycvh0", "ycvh1", "ycv0", "ycv1"])
        P.add("pool", lambda e: e.memset(zp[:], 0.0), writes=["zph0", "zph1", "zp0", "zp1"])
        P.add("pool", lambda e: e.memset(Vt[:], 1.0), writes=["Vinit"])
        P.add("pool", lambda e: e.memset(w_krr[:], 0.0), writes=["w_krr"])
        wv = W["w_in"][l].rearrange("(k p) n -> p k n", p=128)
        for g in range(4):
            P.add("pool", lambda e, g=g: e.dma_start(out=w_in_sb[:, 2 * g:2 * g + 2, :], in_=wv[:, 2 * g:2 * g + 2, :]), writes=[f"w_in{g}"], slot=f"wl{g}")
        P.add("pool", lambda e: e.dma_start(out=w_uq_sb[:, 0, 0:384], in_=W["w_uq"][l, 0:128, :]), writes=["w_uq"], slot="wl0")
        P.add("pool", lambda e: e.dma_start(out=w_uq_sb[0:64, 1, 0:384], in_=W["w_uq"][l, 128:192, :]), writes=["w_uq"], slot="wl1")
        P.add("pool", lambda e: e.dma_start(out=w_ukv_sb[:], in_=W["w_ukv"][l]), writes=["w_ukv"], slot="wl2")
        P.add("pool", lambda e: e.dma_start(out=pw_sb[:], in_=W["conv_pw"][l].rearrange("(k p) n -> p k n", p=128)), writes=["pw"], slot="wl3")
        P.add("pool", lambda e: e.dma_start(out=poolw_sb[:], in_=W["poolw"][l]), writes=["poolw"], slot="wl0")
        P.add("pool", lambda e: e.dma_start(out=wsT_sb[:], in_=W["wsT"][l]), writes=["wsT"], slot="wl1")
        wov = W["w_out"][l].rearrange("(k p) n -> p k n", p=128)
        for g in range(2):
            P.add("pool", lambda e, g=g: e.dma_start(out=w_out_sb[:, 4 * g:4 * g + 4, :], in_=wov[:, 4 * g:4 * g + 4, :]), writes=[f"w_out{g}"], slot=f"wl{2 + g}")
        WIN = [f"w_in{g}" for g in range(4)]
        WOUT = ["w_out0", "w_out1"]
        ukv4 = w_ukv_sb[:, :].rearrange("p (h d) -> p h d", h=4)
        P.add("act", lambda e: e.activation(out=w_kn[:, :].rearrange("p (h d) -> p h d", h=4), in_=ukv4[:, :, 0:64], func=AF.Copy), reads=["w_ukv"], writes=["w_kn"])
        P.add("act", lambda e: e.activation(out=w_v[:, :].rearrange("p (h d) -> p h d", h=4), in_=ukv4[:, :, 64:128], func=AF.Copy), reads=["w_ukv"], writes=["w_v"])
        for h in range(4):
            P.add("dve", lambda e, h=h: e.tensor_tensor(out=wsT_sb[:, h, :], in0=wsT_sb[:, h, :], in1=maskT[:], op=ALU.mult),
                  reads=["wsT", "maskT"], writes=["wsT"])
        for kc in range(2):
            pr = slice(0, 128) if kc == 0 else slice(0, 64)
            src = w_uq_sb[pr, kc, 0:384].rearrange("p (h d) -> p h d", h=4)
            dst = w_uqr[pr, kc, 0:384].rearrange("p (h d) -> p h d", h=4)
            P.add("act", lambda e, src=src, dst=dst: e.mul(out=dst[:, :, 64:80], in_=src[:, :, 80:96], mul=-1.0), reads=["w_uq", "w_uqr"], writes=["w_uqr"])
            P.add("act", lambda e, src=src, dst=dst: e.activation(out=dst[:, :, 80:96], in_=src[:, :, 64:80], func=AF.Copy), reads=["w_uq", "w_uqr"], writes=["w_uqr"])
            P.add("act", lambda e, src=src, dst=dst: e.activation(out=dst[:, :, 0:64], in_=src[:, :, 0:64], func=AF.Copy), reads=["w_uq", "w_uqr"], writes=["w_uqr"])
        P.add("act", lambda e: e.mul(out=w_krr[:, :, 64:80], in_=w_in_sb[:, :, 848:864], mul=-1.0), reads=WIN + ["w_krr"], writes=["w_krr"])
        P.add("act", lambda e: e.activation(out=w_krr[:, :, 80:96], in_=w_in_sb[:, :, 832:848], func=AF.Copy), reads=WIN + ["w_krr"], writes=["w_krr"])
        gen3 = Rot([(banks[i], f"bank{i}") for i in range(3)])
        gen1 = [Rot([(banks[i], f"bank{i}")]) for i in range(3)]
        genh = [gen3]
        stb = Rot([(banks[i], f"bank{i}") for i in range(3, 6)])
        pvb = Rot([(banks[i], f"bank{i}") for i in range(6, 8)])
        ptr = Rot([(PT[i], f"PT{i}") for i in range(3)])
        SCALE = 96.0 ** -0.5

        def mm_group(pout, pres, parts, extra_reads=()):
            n = len(parts)
            for i, (lt, rh, rd) in enumerate(parts):
                P.add("pe", lambda e, lt=lt, rh=rh, i=i: e.matmul(pout, lhsT=lt, rhs=rh, start=(i == 0), stop=(i == n - 1)),
                      reads=list(rd) + list(extra_reads), writes=[pres])

        def rstd_bcast(sq_parts, scale, dst, dres):
            bk, br = genh[0].next()
            mm_group(bk[:, :], br, [(ones[0:sq.shape[0], :], sq, rd + ["ones"]) for sq, rd in sq_parts])
            P.add("act", lambda e: e.activation(out=dst, in_=bk[:, :], func=AF.Ln, scale=scale, bias=epsc[:]), reads=[br, "eps"], writes=[dres])
            P.add("act", lambda e: e.activation(out=dst, in_=dst, func=AF.Exp, scale=-0.5), reads=[dres], writes=[dres])

        def group_norm_fm(g, of, ofres, sqt, rtt):
            sqs = []
            for c in range(2):
                hq, hres = sqt[c]
                hres = hres if isinstance(hres, list) else [hres]
                P.add("pool", lambda e, c=c, hq=hq: e.tensor_tensor(out=hq, in0=of[c], in1=of[c], op=ALU.mult), reads=[ofres[c]], writes=hres)
                sqs.append((hq, hres))
            rt, rres = rtt
            rstd_bcast(sqs, 1.0 / 256, rt, rres)
            for c in range(2):
                P.add("dve", lambda e, c=c: e.scalar_tensor_tensor(out=mixedT[:, 2 * g + c, :], in0=of[c], scalar=colp[:, 75 + 2 * g + c:76 + 2 * g + c], in1=rt,
                                                                  op0=ALU.mult, op1=ALU.mult),
                      reads=[ofres[c], rres, "colp"], writes=[f"mixedT{2 * g + c}"])

        genpre = Rot([(banks[i], f"bank{i}") for i in (1, 2, 3, 4)])
        genpost = Rot([(banks[i], f"bank{i}") for i in (0, 5)])
        pre_ops, main_ops, post_ops, prenorm_ops = [], [], [], []
        for b in range(NB):
            tb = b * 512
            P.cap = []
            genh[0] = genpre
            P.add("sp", lambda e, tb=tb: e.dma_start(out=cos2[64:96, :], in_=ropetab[0, :, tb:tb + 512]), writes=["cos2"], slot="cos2")
            P.add("sp", lambda e, tb=tb: e.dma_start(out=sin2[64:96, :], in_=ropetab[1, :, tb:tb + 512]), writes=["sin2"], slot="sin2")
            prenorm_cap = P.cap
            P.cap = []
            for ip in range(2):
                for i in (2 * ip, 2 * ip + 1):
                    t = 4 * b + i
                    s = i % 2
                    P.add("sp", lambda e, t=t, s=s: e.dma_start(out=hn[s][:], in_=hsrc[t * 128:(t + 1) * 128, :]), reads=[f"hd{t}"], writes=[f"hn{s}"], slot=f"hn{s}")
                    P.add("act", lambda e, s=s, i=i: e.activation(out=xn[i][:], in_=hn[s][:], func=AF.Square, accum_out=sm[:, 40 + i:41 + i]), reads=[f"hn{s}"], writes=[f"ss{i}", f"xn{i}"])
                lo = 2 * ip
                P.add("act", lambda e, lo=lo: e.activation(out=sm[:, 44 + lo:46 + lo], in_=sm[:, 40 + lo:42 + lo], func=AF.Ln, scale=1.0 / D, bias=epsc[:]),
                      reads=[f"ss{lo}", f"ss{lo + 1}", "eps"], writes=[f"sdp{ip}"])
                P.add("act", lambda e, lo=lo: e.activation(out=sm[:, 44 + lo:46 + lo], in_=sm[:, 44 + lo:46 + lo], func=AF.Exp, scale=-0.5), reads=[f"sdp{ip}"], writes=[f"sdp{ip}"])
                for i in (2 * ip, 2 * ip + 1):
                    s = i % 2
                    P.add("dve", lambda e, s=s, i=i: e.scalar_tensor_tensor(out=xn[i][:], in0=hn[s][:], scalar=sm[:, 44 + i:45 + i], in1=gmix[:], op0=ALU.mult, op1=ALU.mult),
                          reads=[f"hn{s}", f"sdp{ip}", "gmix"], writes=[f"xn{i}"])
            prenorm_ops.append(P.cap)
            P.cap = prenorm_cap
            for i in range(4):
                for hb in range(2):
                    bk, br = genh[0].next()
                    bv = bk[:, :].bitcast(BF16)
                    for kk in range(4):
                        k = hb * 4 + kk
                        P.add("pe", lambda e, i=i, k=k, kk=kk, bv=bv: e.transpose(out=bv[:, kk * 128:(kk + 1) * 128], in_=xn[i][:, k * 128:(k + 1) * 128], identity=ident[:]),
                              reads=[f"xn{i}", "ident"], writes=[br])
                    eng = "dve" if hb == 0 else "act"
                    dst = xnT[:, hb * 4:hb * 4 + 4, i * 128:(i + 1) * 128]
                    src = bv[:, 0:512].rearrange("p (k t) -> p k t", k=4)
                    if eng == "dve":
                        P.add("dve", lambda e, dst=dst, src=src: e.tensor_copy(out=dst, in_=src), reads=[br], writes=[f"xnT{i}"])
                    else:
                        P.add("act", lambda e, dst=dst, src=src: e.activation(out=dst, in_=src, func=AF.Copy), reads=[br], writes=[f"xnT{i}"])
            XNT = [f"xnT{i}" for i in range(4)]

            def inproj(c0, c1, lw=None):
                bk, br = genh[0].next()
                m = c1 - c0
                parts = []
                for k in range(8):
                    lt = (w_in_sb[:, k, c0:c1] if lw is None else lw[:, k, :])
                    parts.append((lt, xnT[:, k, :], XNT + WIN + (["w_krr"] if lw is not None else [])))
                mm_group(bk[0:m, :], br, parts)
                return bk, br

            craw = [F(0), F(1), F(2)]
            csq = [H(0), H(1), H(2)]
            spans = [(512, 640), (640, 704), (704, 832)]
            for j, (c0, c1) in enumerate(spans):
                m = c1 - c0
                bk, br = inproj(c0, c1)
                ft, fr = craw[j]
                P.add("act", lambda e, bk=bk, ft=ft, m=m: e.activation(out=ft[0:m, :], in_=bk[0:m, :], func=AF.Copy), reads=[br], writes=[fr])
                ht, hres = csq[j]
                P.add("pool", lambda e, ft=ft, ht=ht, m=m: e.tensor_tensor(out=ht[0:m, :], in0=ft[0:m, :], in1=ft[0:m, :], op=ALU.mult), reads=[fr], writes=[hres])
            rq, rqres = F(3)
            rstd_bcast([(csq[0][0][:, :], [csq[0][1]]), (csq[1][0][0:64, :], [csq[1][1]])], 1.0 / 192, rq[:], rqres)
            rkv, rkvres = F(4)
            rstd_bcast([(csq[2][0][:, :], [csq[2][1]])], 1.0 / 128, rkv[:], rkvres)
            cqn = [H(3), H(4), H(5)]
            gcols = [68, 69, 70]
            for j in range(3):
                m = spans[j][1] - spans[j][0]
                ft, fr = craw[j]
                ht, hres = cqn[j]
                rt, rres = (rq, rqres) if j < 2 else (rkv, rkvres)
                P.add("dve", lambda e, ft=ft, ht=ht, rt=rt, m=m, gc=gcols[j]: e.scalar_tensor_tensor(out=ht[0:m, :], in0=ft[0:m, :], scalar=colp[0:m, gc:gc + 1], in1=rt[0:m, :],
                                                                                                   op0=ALU.mult, op1=ALU.mult),
                      reads=[fr, rres, "colp"], writes=[hres])
            for h in range(4):
                pa, par = genh[0].next()
                mm_group(pa[:, :], par, [(w_uq_sb[:, 0, h * 96:h * 96 + 128], cqn[0][0][:, :], [cqn[0][1], "w_uq"]),
                                         (w_uq_sb[0:64, 1, h * 96:h * 96 + 128], cqn[1][0][0:64, :], [cqn[1][1], "w_uq"])])
                pbk, pbr = genh[0].next()
                mm_group(pbk[:, :], pbr, [(w_uqr[:, 0, h * 96:h * 96 + 128], cqn[0][0][:, :], [cqn[0][1], "w_uqr"]),
                                          (w_uqr[0:64, 1, h * 96:h * 96 + 128], cqn[1][0][0:64, :], [cqn[1][1], "w_uqr"])])
                P.add("act", lambda e, pa=pa, h=h: e.activation(out=QT[0:64, h, :], in_=pa[0:64, :], func=AF.Copy), reads=[par], writes=[f"QT{h}"])
                t1, t1r = F(5)
                t2, t2r = F(6)
                P.add("dve", lambda e, pa=pa, t1=t1: e.tensor_tensor(out=t1[64:96, :], in0=pa[64:96, :], in1=cos2[64:96, :], op=ALU.mult), reads=[par, "cos2"], writes=[t1r])
                P.add("dve", lambda e, pbk=pbk, t2=t2: e.tensor_tensor(out=t2[64:96, :], in0=pbk[64:96, :], in1=sin2[64:96, :], op=ALU.mult), reads=[pbr, "sin2"], writes=[t2r])
                P.add("pool", lambda e, t1=t1, t2=t2, h=h: e.tensor_tensor(out=QT[64:96, h, :], in0=t1[64:96, :], in1=t2[64:96, :], op=ALU.add), reads=[t1r, t2r], writes=[f"QT{h}r"])
            for hp in range(2):
                bk, br = genh[0].next()
                lt = w_kn[:, hp * 128:(hp + 1) * 128]
                mm_group(bk[:, :], br, [(lt, cqn[2][0][:, :], [cqn[2][1], "w_kn"])])
                P.add("act", lambda e, bk=bk, hp=hp, tb=tb: e.activation(out=KT[0:64, 2 * hp, tb:tb + 512], in_=bk[0:64, :], func=AF.Copy), reads=[br], writes=[f"KT{b}_{2 * hp}"])
                P.add("dve", lambda e, bk=bk, hp=hp, tb=tb: e.tensor_copy(out=KT[0:64, 2 * hp + 1, tb:tb + 512], in_=bk[64:128, :]), reads=[br], writes=[f"KT{b}_{2 * hp + 1}"])
            ka, kar = inproj(768, 896)
            kb, kbr = inproj(0, 128, lw=w_krr)
            t1, t1r = F(5)
            t2, t2r = F(6)
            P.add("dve", lambda e, ka=ka, t1=t1: e.tensor_tensor(out=t1[64:96, :], in0=ka[64:96, :], in1=cos2[64:96, :], op=ALU.mult), reads=[kar, "cos2"], writes=[t1r])
            P.add("dve", lambda e, kb=kb, t2=t2: e.tensor_tensor(out=t2[64:96, :], in0=kb[64:96, :], in1=sin2[64:96, :], op=ALU.mult), reads=[kbr, "sin2"], writes=[t2r])
            for h in range(4):
                P.add("pool", lambda e, t1=t1, t2=t2, h=h, tb=tb: e.tensor_tensor(out=KT[64:96, h, tb:tb + 512], in0=t1[64:96, :], in1=t2[64:96, :], op=ALU.add),
                      reads=[t1r, t2r], writes=[f"KT{b}_{h}r"])
            for i2 in range(2):
                bk, br = genh[0].next()
                for ii in range(2):
                    i = 2 * i2 + ii
                    rh = w_v[:, :]
                    P.add("pe", lambda e, bk=bk, ii=ii, i=i, rh=rh: e.matmul(bk[:, ii * 256:(ii + 1) * 256], lhsT=cqn[2][0][:, i * 128:(i + 1) * 128], rhs=rh, start=True, stop=True),
                          reads=[cqn[2][1], "w_v"], writes=[br])
                for ii in range(2):
                    i = 2 * i2 + ii
                    dst = Vt[:, 4 * b + i, :, 0:64]
                    src = bk[:, ii * 256:(ii + 1) * 256].rearrange("p (h d) -> p h d", h=4)
                    P.add("act", lambda e, dst=dst, src=src: e.activation(out=dst, in_=src, func=AF.Copy), reads=[br, "Vinit"], writes=[f"V{4 * b + i}"])

            pre_ops.append(P.cap)
            P.cap = None
            nk = 4 * b + 4
            units = [(h, kt) for h in range(4) for kt in range(nk)]
            accs = {}
            pts = {}
            LA = 2

            def emit_s(h, kt):
                j0 = max(0, kt - 4 * b)
                q0 = j0 * 128
                sbk, sbr = stb.next()
                P.add("pe", lambda e, sbk=sbk, h=h, kt=kt, q0=q0: e.matmul(sbk[:, q0:512], lhsT=KT[:, h, kt * 128:(kt + 1) * 128], rhs=QT[:, h, q0:512], start=True, stop=True),
                      reads=[f"KT{kt // 4}_{h}", f"KT{kt // 4}_{h}r", f"QT{h}", f"QT{h}r", "KTpad", "QTpad"], writes=[sbr])
                pt, ptres = ptr.next()
                P.add("act", lambda e, sbk=sbk, pt=pt, q0=q0: e.activation(out=pt[:, q0:512], in_=sbk[:, q0:512], func=AF.Exp, scale=SCALE), reads=[sbr], writes=[ptres])
                if kt >= 4 * b:
                    P.add("pool", lambda e, pt=pt, q0=q0: e.tensor_tensor(out=pt[:, q0:q0 + 128], in0=pt[:, q0:q0 + 128], in1=maskT[:], op=ALU.mult),
                          reads=[ptres, "maskT"], writes=[ptres])
                pts[(h, kt)] = (pt, ptres, j0)

            def emit_pv(h, kt):
                if kt == 0:
                    accs[h] = pvb.next()
                acc, accr = accs[h]
                accv = acc[:, 0:260].rearrange("p (j d) -> p j d", j=4)
                pt, ptres, j0 = pts.pop((h, kt))
                for j in range(j0, 4):
                    first = (kt == 0 and j == j0)
                    P.add("pe", lambda e, pt=pt, j=j, kt=kt, h=h, first=first, accv=accv: e.matmul(accv[:, j, :], lhsT=pt[:, j * 128:(j + 1) * 128], rhs=Vt[:, kt, h, :],
                                                                                                start=first, stop=(kt == nk - 1 and j == 3), skip_group_check=True),
                          reads=[ptres, f"V{kt}", "Vinit"], writes=[accr])
                if kt == nk - 1:
                    P.add("dve", lambda e, accv=accv, h=h: e.reciprocal(out=sm[:, 8 + 4 * h:12 + 4 * h], in_=accv[:, :, 64]), reads=[accr], writes=[f"rec{h}"])
                    for j in range(4):
                        P.add("dve", lambda e, accv=accv, h=h, j=j: e.tensor_scalar(out=omla[:, j, h * 64:(h + 1) * 64], in0=accv[:, j, 0:64], scalar1=sm[:, 8 + 4 * h + j:9 + 4 * h + j],
                                                                                    scalar2=None, op0=ALU.mult),
                              reads=[accr, f"rec{h}"], writes=[f"omla{j}"])

            att_units = []
            for idx in range(len(units) + LA):
                P.cap = []
                if idx < len(units):
                    emit_s(*units[idx])
                if idx - LA >= 0:
                    emit_pv(*units[idx - LA])
                att_units.append(P.cap)
                P.cap = None
            streams = []
            def emit_mla_out():
                bk, br = genh[0].next()
                bv = bk[:, :].bitcast(BF16)
                for j in range(4):
                    P.add("act", lambda e, j=j: e.activation(out=junk[:, 0:256], in_=omla[:, j, :], func=AF.Square, accum_out=sm[:, 24 + j:25 + j]), reads=[f"omla{j}"], writes=[f"oss{j}"])
                P.add("act", lambda e: e.activation(out=sm[:, 28:32], in_=sm[:, 24:28], func=AF.Ln, scale=1.0 / 256, bias=epsc[:]), reads=[f"oss{j}" for j in range(4)] + ["eps"], writes=["osd"])
                P.add("act", lambda e: e.activation(out=sm[:, 28:32], in_=sm[:, 28:32], func=AF.Exp, scale=-0.5), reads=["osd"], writes=["osd"])
                for j in range(4):
                    mt = mtok[j % 2]
                    P.add("dve", lambda e, j=j, mt=mt: e.scalar_tensor_tensor(out=mt[:], in0=omla[:, j, :], scalar=sm[:, 28 + j:29 + j], in1=ggm[:], op0=ALU.mult, op1=ALU.mult),
                          reads=[f"omla{j}", "osd", "ggm"], writes=[f"mtok{j % 2}"])
                    for c in range(2):
                        P.add("pe", lambda e, mt=mt, c=c, j=j, bv=bv: e.transpose(out=bv[:, c * 512 + j * 128:c * 512 + (j + 1) * 128], in_=mt[:, c * 128:(c + 1) * 128], identity=ident[:]),
                              reads=[f"mtok{j % 2}", "ident"], writes=[br])
                P.add("act", lambda e, bv=bv: e.activation(out=mixedT[:, 2:4, :].rearrange("p c t -> p (c t)"), in_=bv[:, :], func=AF.Copy), reads=[br], writes=["mixedT2", "mixedT3"])

            P.cap = []
            genh[0] = gen1[0]
            for c in range(2):
                gb, gbr = inproj(256 + c * 128, 256 + (c + 1) * 128)
                sg, sgr = F(0)
                P.add("act", lambda e, gb=gb, sg=sg: e.activation(out=sg[:], in_=gb[:, :], func=AF.Exp, scale=-1.0), reads=[gbr], writes=[sgr])
                P.add("act", lambda e, sg=sg: e.activation(out=sg[:], in_=sg[:], func=AF.Ln, bias=onec[:]), reads=[sgr, "eps"], writes=[sgr])
                P.add("act", lambda e, sg=sg: e.activation(out=sg[:], in_=sg[:], func=AF.Exp, scale=-1.0), reads=[sgr], writes=[sgr])
                ab, abr = inproj(c * 128, (c + 1) * 128)
                P.add("dve", lambda e, ab=ab, sg=sg, c=c: e.tensor_tensor(out=ycv[:, c, 30:542], in0=ab[:, :], in1=sg[:], op=ALU.mult), reads=[abr, sgr], writes=[f"ycv{c}"])
            ycf = [F(1), F(2)]
            ycb = [H(0), H(1)]
            ysq = [H(2), H(3)]
            for c in range(2):
                bk, br = genh[0].next()
                for k in range(31):
                    ds = (c * 31 + k) % 8
                    P.add("dve", lambda e, c=c, k=k, ds=ds: e.tensor_scalar(out=dg[:, ds, :], in0=ident[:], scalar1=colp[:, c * 31 + k:c * 31 + k + 1], scalar2=None, op0=ALU.mult),
                          reads=["ident", "colp"], writes=[f"dg{ds}"])
                    P.add("pe", lambda e, bk=bk, c=c, k=k, ds=ds: e.matmul(bk[:, :], lhsT=dg[:, ds, :], rhs=ycv[:, c, k:k + 512], start=(k == 0), stop=(k == 30)),
                          reads=[f"dg{ds}", f"ycv{c}", f"ycvh{c}"], writes=[br])
                ft, fr = ycf[c]
                P.add("dve", lambda e, bk=bk, ft=ft, c=c: e.tensor_scalar(out=ft[:], in0=bk[:, :], scalar1=colp[:, 62 + c:63 + c], scalar2=None, op0=ALU.add), reads=[br, "colp"], writes=[fr])
                P.add("pool", lambda e, c=c: e.tensor_copy(out=ycv[:, c, 0:30], in_=ycv[:, c, 512:542]), reads=[f"ycv{c}"], writes=[f"ycvh{c}"])
                hb_, hbr = ycb[c]
                P.add("pool", lambda e, ft=ft, hb_=hb_: e.tensor_copy(out=hb_[:], in_=ft[:]), reads=[fr], writes=[hbr])
                hq, hqr = ysq[c]
                P.add("pool", lambda e, ft=ft, hq=hq: e.tensor_tensor(out=hq[:], in0=ft[:], in1=ft[:], op=ALU.mult), reads=[fr], writes=[hqr])
            mt_, mtr = F(3)
            m2, m2r = F(4)
            vr, vrr = F(5)
            mb, mbr = genh[0].next()
            mm_group(mb[:, :], mbr, [(ones[:, :], ycb[c][0][:, :], [ycb[c][1], "ones"]) for c in range(2)])
            P.add("dve", lambda e, mb=mb, mt_=mt_: e.tensor_scalar(out=mt_[:], in0=mb[:, :], scalar1=1.0 / 256, scalar2=None, op0=ALU.mult), reads=[mbr], writes=[mtr])
            qb, qbr = genh[0].next()
            mm_group(qb[:, :], qbr, [(ones[:, :], ysq[c][0][:, :], [ysq[c][1], "ones"]) for c in range(2)])
            P.add("pool", lambda e, mt_=mt_, m2=m2: e.tensor_tensor(out=m2[:], in0=mt_[:], in1=mt_[:], op=ALU.mult), reads=[mtr], writes=[m2r])
            P.add("dve", lambda e, qb=qb, m2=m2, vr=vr: e.scalar_tensor_tensor(out=vr[:], in0=qb[:, :], scalar=1.0 / 256, in1=m2[:], op0=ALU.mult, op1=ALU.subtract),
                  reads=[qbr, m2r], writes=[vrr])
            P.add("act", lambda e, vr=vr: e.activation(out=vr[:], in_=vr[:], func=AF.Ln, bias=epsc[:]), reads=[vrr, "eps"], writes=[vrr])
            P.add("act", lambda e, vr=vr: e.activation(out=vr[:], in_=vr[:], func=AF.Exp, scale=-0.5), reads=[vrr], writes=[vrr])
            sact = [H(0), H(1)]
            for c in range(2):
                ft, fr = ycf[c]
                P.add("dve", lambda e, ft=ft, mt_=mt_: e.tensor_tensor(out=ft[:], in0=ft[:], in1=mt_[:], op=ALU.subtract), reads=[fr, mtr], writes=[fr])
                P.add("pool", lambda e, ft=ft, vr=vr: e.tensor_tensor(out=ft[:], in0=ft[:], in1=vr[:], op=ALU.mult), reads=[fr, vrr], writes=[fr])
                ht, hres = sact[c]
                sg, sgr = F(0)
                P.add("dve", lambda e, ft=ft, c=c: e.tensor_scalar(out=ft[:], in0=ft[:], scalar1=colp[:, 64 + c:65 + c], scalar2=colp[:, 66 + c:67 + c], op0=ALU.mult, op1=ALU.add),
                      reads=[fr, "colp"], writes=[fr])
                P.add("act", lambda e, ft=ft, sg=sg: e.activation(out=sg[:], in_=ft[:], func=AF.Exp, scale=-1.0), reads=[fr], writes=[sgr])
                P.add("act", lambda e, sg=sg: e.activation(out=sg[:], in_=sg[:], func=AF.Ln, bias=onec[:]), reads=[sgr, "eps"], writes=[sgr])
                P.add("act", lambda e, sg=sg: e.activation(out=sg[:], in_=sg[:], func=AF.Exp, scale=-1.0), reads=[sgr], writes=[sgr])
                P.add("pool", lambda e, ft=ft, sg=sg, ht=ht: e.tensor_tensor(out=ht[:], in0=ft[:], in1=sg[:], op=ALU.mult), reads=[fr, sgr], writes=[hres])
            of = [F(3), F(4)]
            for co in range(2):
                bk, br = genh[0].next()
                mm_group(bk[:, :], br, [(pw_sb[:, ci, co * 128:(co + 1) * 128], sact[ci][0][:, :], [sact[ci][1], "pw"]) for ci in range(2)])
                ft, fr = of[co]
                P.add("dve", lambda e, bk=bk, ft=ft: e.tensor_copy(out=ft[:], in_=bk[:, :]), reads=[br], writes=[fr])
            group_norm_fm(0, [of[0][0][:], of[1][0][:]], [of[0][1], of[1][1]], [(Hm[2][:], "H2"), (Hm[3][:], "H3")], (Fm[5][:], "F5"))

            streams.append(P.cap)
            P.cap = []
            genh[0] = gen1[1]
            yp = [H(4), H(5)]
            for c in range(2):
                pbk, pbr = inproj(864 + c * 128, 864 + (c + 1) * 128)
                P.add("dve", lambda e, pbk=pbk, c=c: e.tensor_copy(out=zp[:, c, 15:527], in_=pbk[:, :]), reads=[pbr], writes=[f"zp{c}"])
                zc = zp[:, c, :]
                P.add("pool", lambda e, zc=zc: e.tensor_tensor(out=PA[:, 1:527], in0=zc[:, 1:527], in1=zc[:, 0:526], op=ALU.add), reads=[f"zp{c}", f"zph{c}"], writes=["PA"])
                P.add("pool", lambda e: e.tensor_tensor(out=PB[:, 3:527], in0=PA[:, 3:527], in1=PA[:, 1:525], op=ALU.add), reads=["PA"], writes=["PB"])
                if c == 0:
                    lo, hi = PA, PB
                    lor, hir = "PA", "PB"
                else:
                    P.add("pool", lambda e: e.tensor_tensor(out=PA[:, 7:527], in0=PB[:, 7:527], in1=PB[:, 3:523], op=ALU.add), reads=["PB", "PA"], writes=["PA"])
                    P.add("pool", lambda e: e.tensor_tensor(out=PB[:, 15:527], in0=PA[:, 15:527], in1=PA[:, 7:519], op=ALU.add), reads=["PA", "PB"], writes=["PB"])
                    lo, hi = PA, PB
                    lor, hir = "PA", "PB"
                ht, hres = yp[c]
                for (pr, srcT, srcr) in ((slice(0, 64), lo, lor), (slice(64, 128), hi, hir)):
                    P.add("dve", lambda e, pr=pr, srcT=srcT, c=c, ht=ht, zc=zc: e.scalar_tensor_tensor(out=ht[pr, :], in0=srcT[pr, 15:527], scalar=c_sb[pr, 288 + c:289 + c], in1=zc[pr, 15:527],
                                                                                                     op0=ALU.mult, op1=ALU.subtract),
                          reads=[srcr, f"zp{c}", "c"], writes=[hres])
                    if b == 0:
                        tt, ttr = F(6)
                        P.add("dve", lambda e, pr=pr, srcT=srcT, c=c, tt=tt: e.tensor_tensor(out=tt[pr, 0:16], in0=srcT[pr, 15:31], in1=c_sb[pr, 256 + c * 16:272 + c * 16], op=ALU.mult),
                              reads=[srcr, "c"], writes=[ttr])
                        P.add("dve", lambda e, pr=pr, c=c, tt=tt, ht=ht, zc=zc: e.tensor_tensor(out=ht[pr, 0:16], in0=tt[pr, 0:16], in1=zc[pr, 15:31], op=ALU.subtract),
                              reads=[ttr, f"zp{c}", hres], writes=[hres])
                P.add("pool", lambda e, c=c: e.tensor_copy(out=zp[:, c, 0:15], in_=zp[:, c, 512:527]), reads=[f"zp{c}"], writes=[f"zph{c}"])
            of = [(PA[:, 0:512], "PA"), (PB[:, 0:512], "PB")]
            for c in range(2):
                bk, br = genh[0].next()
                mm_group(bk[:, :], br, [(poolw_sb[:, c, :], yp[c][0][:, :], [yp[c][1], "poolw"])])
                ft, fr = of[c]
                P.add("dve", lambda e, bk=bk, ft=ft, c=c: e.tensor_scalar(out=ft, in0=bk[:, :], scalar1=colp[:, 71 + c:72 + c], scalar2=None, op0=ALU.mult), reads=[br, "colp"], writes=[fr])
            group_norm_fm(2, [of[0][0], of[1][0]], [of[0][1], of[1][1]], [(Hm[4][:], "H4"), (Hm[5][:], "H5")], (Fm[6][:], "F6"))

            streams.append(P.cap)
            P.cap = []
            genh[0] = gen1[2]
            for i2 in range(2):
                bk, br = genh[0].next()
                for ii in range(2):
                    i = 2 * i2 + ii
                    for k in range(8):
                        P.add("pe", lambda e, bk=bk, ii=ii, i=i, k=k: e.matmul(bk[:, ii * 256:(ii + 1) * 256], lhsT=xnT[:, k, i * 128:(i + 1) * 128], rhs=w_in_sb[:, k, 1376:1632],
                                                                            start=(k == 0), stop=(k == 7)),
                              reads=XNT + WIN, writes=[br])
                for ii in range(2):
                    i = 2 * i2 + ii
                    src = bk[:, ii * 256:(ii + 1) * 256]
                    P.add("act", lambda e, src=src, i=i: e.activation(out=junk[:, 0:256], in_=src, func=AF.Square, accum_out=sm[:, 32 + i:33 + i]), reads=[br], writes=[f"vss{i}"])
                lo = 2 * i2
                P.add("act", lambda e, lo=lo: e.activation(out=sm[:, 36 + lo:38 + lo], in_=sm[:, 32 + lo:34 + lo], func=AF.Ln, scale=1.0 / 256, bias=epsc[:]), reads=[f"vss{lo}", f"vss{lo + 1}", "eps"], writes=[f"vsdp{i2}"])
                P.add("act", lambda e, lo=lo: e.activation(out=sm[:, 36 + lo:38 + lo], in_=sm[:, 36 + lo:38 + lo], func=AF.Exp, scale=-0.5), reads=[f"vsdp{i2}"], writes=[f"vsdp{i2}"])
                for ii in range(2):
                    i = 2 * i2 + ii
                    src = bk[:, ii * 256:(ii + 1) * 256]
                    P.add("dve", lambda e, src=src, i=i: e.tensor_scalar(out=vn[:, i, :], in0=src, scalar1=sm[:, 36 + i:37 + i], scalar2=None, op0=ALU.mult), reads=[br, f"vsdp{i2}"], writes=[f"vn{i}"])
            of = [F(7), F(8)]
            gate = of
            for c in range(2):
                gb, gbr = genh[0].next()
                for i in range(4):
                    for hh in range(2):
                        h = 2 * c + hh
                        P.add("pe", lambda e, gb=gb, i=i, hh=hh, h=h: e.matmul(gb[hh * 64:(hh + 1) * 64, i * 128:(i + 1) * 128], lhsT=vn[:, i, h * 64:(h + 1) * 64], rhs=wsT_sb[:, h, :],
                                                                            start=True, stop=True),
                              reads=[f"vn{i}", "wsT"], writes=[gbr])
                gt, gtr = gate[c]
                bsb = bsT[:, c, :].unsqueeze(1).broadcast_to([128, 4, 128])
                P.add("dve", lambda e, gb=gb, gt=gt, c=c, bsb=bsb: e.scalar_tensor_tensor(out=gt[:].rearrange("p (i t) -> p i t", i=4), in0=gb[:, :].rearrange("p (i t) -> p i t", i=4),
                                                                                        scalar=colp[:, 73 + c:74 + c], in1=bsb, op0=ALU.mult, op1=ALU.add),
                      reads=[gbr, "colp", "bsT"], writes=[gtr])
                ub, ubr = inproj(1120 + c * 128, 1120 + (c + 1) * 128)
                ft, fr = of[c]
                P.add("dve", lambda e, ub=ub, gt=gt, ft=ft: e.tensor_tensor(out=ft[:], in0=ub[:, :], in1=gt[:], op=ALU.mult), reads=[ubr, gtr], writes=[fr])
            group_norm_fm(3, [of[0][0][:], of[1][0][:]], [of[0][1], of[1][1]],
                          [(vn[:, 0:2, :].rearrange("p a b -> p (a b)"), ["vn0", "vn1"]), (vn[:, 2:4, :].rearrange("p a b -> p (a b)"), ["vn2", "vn3"])], (Fm[9][:], "F9"))

            streams.append(P.cap)
            P.cap = None
            genh[0] = gen3
            main_ops.append((att_units, streams))
            P.cap = []
            genh[0] = genpost
            emit_mla_out()
            MX = [f"mixedT{k}" for k in range(8)]
            for i in range(4):
                t = 4 * b + i
                s = i % 2
                P.add("sp", lambda e, t=t, s=s: e.dma_start(out=hr[s][:], in_=hsrc[t * 128:(t + 1) * 128, :]), reads=[f"hd{t}"], writes=[f"hr{s}"], slot=f"hr{s}")
                for n in range(2):
                    bk, br = genh[0].next()
                    mm_group(bk[:, :], br, [(mixedT[:, k, i * 128:(i + 1) * 128], w_out_sb[:, k, n * 512:(n + 1) * 512], MX + WOUT) for k in range(8)])
                    P.add("dve", lambda e, bk=bk, s=s, n=n: e.tensor_tensor(out=hr[s][:, n * 512:(n + 1) * 512], in0=bk[:, :], in1=hr[s][:, n * 512:(n + 1) * 512], op=ALU.add),
                          reads=[br, f"hr{s}"], writes=[f"hr{s}"])
                P.add("sp", lambda e, t=t, s=s: e.dma_start(out=hbuf[t * 128:(t + 1) * 128, :], in_=hr[s][:]), reads=[f"hr{s}"], writes=[f"hd{t}"], slot=f"st{s}")
            post_ops.append(P.cap)
            P.cap = None
        def merge_main(att_units, streams, extra):
            strs = list(streams) + [extra]
            order = [0, 1, 0, 2, 3]
            totY = sum(len(x) for x in strs)
            per = max(3, -(-totY // max(1, len(att_units))))
            posn = [0] * len(strs)
            st = {"rr": 0}
            out = []

            def emit_y(n):
                done = 0
                while done < n and any(posn[i] < len(strs[i]) for i in range(len(strs))):
                    i = order[st["rr"] % len(order)]
                    st["rr"] += 1
                    if posn[i] < len(strs[i]):
                        out.append(strs[i][posn[i]])
                        posn[i] += 1
                        done += 1

            for u in att_units:
                out.extend(u)
                emit_y(per)
            emit_y(10 ** 9)
            return out

        P.replay(prenorm_ops[0])
        P.replay(pre_ops[0])
        for b in range(NB):
            att_units, streams = main_ops[b]
            P.replay(merge_main(att_units, streams, prenorm_ops[b + 1] if b + 1 < NB else []))
            A_ = post_ops[b]
            B_ = pre_ops[b + 1] if b + 1 < NB else []
            ia = ib = 0
            ra = max(1, len(A_))
            rb = max(1, len(B_))
            while ia < len(A_) or ib < len(B_):
                if ib >= len(B_) or (ia < len(A_) and ia * rb <= ib * ra):
                    P.replay([A_[ia]])
                    ia += 1
                else:
                    P.replay([B_[ib]])
                    ib += 1
        P.emit()


POOL_WINDOWS = (2, 4, 8, 16)


def make_consts():
    c = np.zeros((128, NCST), np.float32)
    c[:, 0:128] = np.eye(128, dtype=np.float32)
    k = np.arange(128)[:, None]
    q = np.arange(128)[None, :]
    c[:, 128:256] = (q >= k).astype(np.float32)
    for ch in range(2):
        for p in range(128):
            w = POOL_WINDOWS[2 * ch + p // 64]
            c[p, 288 + ch] = 1.0 / w
            for t in range(16):
                c[p, 256 + ch * 16 + t] = 1.0 / min(t + 1, w)
    inv_freq = (10000.0 ** (-np.arange(0, 32, 2, dtype=np.float32) / 32)).astype(np.float32)
    for p in range(32):
        c[p, 290] = inv_freq[p % 16]
    return c


def col128(v):
    v = np.asarray(v, np.float32)
    return np.ascontiguousarray(v.reshape(-1, 128).T)


def pack_layer_inputs(inp, layers):
    L = len(layers)
    colp = np.zeros((L, 128, NCOL), np.float32)
    rowp = np.zeros((L, NROW), np.float32)
    poolw = np.zeros((L, 128, 2, 128), np.float32)
    wsT = np.zeros((L, 128, 4, 128), np.float32)
    bsT = np.zeros((L, 128, 2, 128), np.float32)
    for li, l in enumerate(layers):
        dw = inp["conv_dw_w"][l]
        for c in range(2):
            colp[li, :, c * 31:(c + 1) * 31] = dw[:, c * 128:(c + 1) * 128].T
        colp[li, :, 62:64] = col128(inp["conv_dw_b"][l])
        colp[li, :, 64:66] = col128(inp["conv_ln_g"][l])
        colp[li, :, 66:68] = col128(inp["conv_ln_b"][l])
        qg = np.zeros(256, np.float32)
        qg[0:192] = inp["mla_q_norm_g"][l]
        colp[li, :, 68:70] = col128(qg)
        colp[li, :, 70:71] = col128(inp["mla_kv_norm_g"][l])
        colp[li, :, 71:73] = col128(inp["pool_scale"][l])
        colp[li, :, 73:75] = col128(inp["gmlp_norm_g"][l])
        colp[li, :, 75:83] = col128(inp["group_norm_g"][l].reshape(-1))
        rowp[li, 0:1024] = inp["mix_norm_g"][l]
        rowp[li, 1024:2048] = inp["ffn_norm_g"][l]
        rowp[li, 2048:2304] = inp["group_norm_g"][l, 1]
        pw_ = inp["pool_w"][l]
        for c in range(2):
            for hh in range(2):
                poolw[li, hh * 64:(hh + 1) * 64, c, hh * 64:(hh + 1) * 64] = pw_[2 * c + hh]
        wsT[li] = np.transpose(inp["gmlp_ws"][l], (2, 0, 1))
        bs = inp["gmlp_bs"][l]
        for c in range(2):
            for hh in range(2):
                bsT[li, hh * 64:(hh + 1) * 64, c, :] = bs[2 * c + hh][None, :]
    sl = list(layers)
    d = dict(
        w_in=np.ascontiguousarray(inp["w_in"][sl]), w_out=np.ascontiguousarray(inp["w_out"][sl]),
        w_uq=np.ascontiguousarray(inp["mla_w_uq"][sl]), w_ukv=np.ascontiguousarray(inp["mla_w_ukv"][sl]),
        conv_pw=np.ascontiguousarray(inp["conv_pw_w"][sl]), poolw=poolw, wsT=wsT, bsT=bsT, colp=colp, rowp=rowp,
        w_ff1=np.ascontiguousarray(inp["w_ff1"][sl]), w_ff2=np.ascontiguousarray(inp["w_ff2"][sl]),
        fng=np.ascontiguousarray(np.asarray(inp["final_norm_g"], np.float32).reshape(1, D)),
        cst=make_consts(),
    )
    return d


_PROGS = {}


def get_prog(T, n_layers, final):
    key = (T, n_layers, final)
    if key not in _PROGS:
        _PROGS[key] = build_program(T, n_layers, final)
    return _PROGS[key]


def run_layers(inp, hs, positions, layers, final, T):
    shared = pack_layer_inputs(inp, layers)
    nc = get_prog(T, len(layers), final)
    in_maps = []
    for ci in range(len(hs)):
        m = dict(shared)
        m["x"] = np.ascontiguousarray(hs[ci], dtype=np.float32)
        m["pos"] = np.ascontiguousarray(positions[ci].reshape(1, T).astype(np.int32))
        in_maps.append(m)
    res = run_bass_kernel_spmd(nc, in_maps, core_ids=list(range(len(hs))))
    return [np.asarray(r["out"]) for r in res.results]


FUSED = True


def kernel(**inputs):
    inp = {k: np.asarray(v) for k, v in inputs.items()}
    x = inp["x"].astype(np.float32)
    B, T, _ = x.shape
    depth = inp["w_in"].shape[0]
    positions = inp["positions"]
    hs = [x[b] for b in range(B)]
    if FUSED:
        outs = run_layers(inp, hs, positions, list(range(depth)), True, T)
    else:
        for l in range(depth):
            hs = run_layers(inp, hs, positions, [l], l == depth - 1, T)
        outs = hs
    return np.stack(outs, axis=0).astype(np.float32)
```

```python
from contextlib import ExitStack
import math
import numpy as np
import concourse.bass as bass
import concourse.mybir as mybir
from concourse.bass_utils import run_bass_kernel_spmd

F32 = mybir.dt.float32
BF16 = mybir.dt.bfloat16
I32 = mybir.dt.int32
ALU = mybir.AluOpType
AF = mybir.ActivationFunctionType

D = 1024
DIN = 1632
DFF = 4096
EPS = 1e-6
NCOL = 83
NROW = 2304
NCST = 291
ENGS = ("pe", "act", "dve", "pool", "sp")


class Res:
    __slots__ = ("name", "w", "rs")

    def __init__(self, name):
        self.name = name
        self.w = None
        self.rs = []


class Slot:
    __slots__ = ("name", "n", "last", "sem")

    def __init__(self, name):
        self.name = name
        self.n = 0
        self.last = None
        self.sem = None


class Op:
    __slots__ = ("eng", "fn", "deps", "sig", "val", "slot")


class Sync:
    def __init__(self, nc, stack):
        self.nc = nc
        self.stack = stack
        self.esem = {e: stack.enter_context(nc.semaphore(f"s_{e}")) for e in ENGS}
        self.ecount = {e: 0 for e in ENGS}
        self.slots = {}


class Prog:
    def __init__(self, nc, name, sync):
        self.nc = nc
        self.name = name
        self.sync = sync
        self.ops = {e: [] for e in ENGS}
        self.slots = sync.slots
        for s in self.slots.values():
            s.last = None
        self.used = []
        self.res = {}
        self.cap = None

    def R(self, name):
        r = self.res.get(name)
        if r is None:
            r = self.res[name] = Res(name)
        return r

    def slot(self, name):
        s = self.slots.get(name)
        if s is None:
            s = self.slots[name] = Slot(name)
        return s

    def add(self, eng, fn, reads=(), writes=(), slot=None):
        if self.cap is not None:
            self.cap.append((eng, fn, list(reads), list(writes), slot))
            return None
        reads = [self.R(r) if isinstance(r, str) else r for r in reads]
        writes = [self.R(r) if isinstance(r, str) else r for r in writes]
        if isinstance(slot, str):
            slot = self.slot(slot)
        op = Op()
        op.eng = eng
        op.fn = fn
        op.sig = slot is not None
        op.val = None
        op.slot = slot
        deps = []
        xr = [r for r in reads if r.name.startswith("bank")]
        xw = [r for r in writes if r.name.startswith("bank")]
        reads = [r for r in reads if not r.name.startswith("bank")]
        writes = [r for r in writes if not r.name.startswith("bank")]
        for r, kind in [(r, "r") for r in xr] + [(r, "w") for r in xw]:
            if r.w is not None:
                pk = r.rs[0] if r.rs else "w"
                if not (r.w.eng == eng and pk == "r" and kind == "r"):
                    deps.append(r.w)
            r.w = op
            r.rs = [kind]
        for r in reads:
            if r.w is not None:
                deps.append(r.w)
        for r in writes:
            if r.w is not None:
                deps.append(r.w)
            deps.extend(r.rs)
        if slot is not None and slot.last is not None:
            deps.append(slot.last)
        for r in reads:
            r.rs.append(op)
        for r in writes:
            r.w = op
            r.rs = []
        if slot is not None:
            slot.last = op
            slot.n += 1
            op.val = 16 * slot.n
            if slot not in self.used:
                self.used.append(slot)
        dd = []
        seen = set()
        for d in deps:
            if d is op or id(d) in seen:
                continue
            seen.add(id(d))
            if d.eng == "pe" and eng == "pe" and d.slot is None and slot is None:
                continue
            d.sig = True
            dd.append(d)
        op.deps = dd
        self.ops[eng].append(op)
        return op

    def replay(self, items):
        for it in items:
            self.add(*it)

    def emit(self):
        nc = self.nc
        sync = self.sync
        for e in ENGS:
            c = sync.ecount[e]
            for op in self.ops[e]:
                if op.slot is None and op.sig:
                    c += 1
                    op.val = c
            sync.ecount[e] = c
        with ExitStack() as st:
            esem = sync.esem
            fin = list(self.used)
            for s in fin:
                if s.sem is None:
                    s.sem = sync.stack.enter_context(nc.semaphore(f"d_{s.name}"))
            block = st.enter_context(nc.Block())

            def run(eng_name, e):
                known = {}
                for op in self.ops[eng_name]:
                    for d in op.deps:
                        sem = d.slot.sem if d.slot is not None else esem[d.eng]
                        k = id(sem)
                        if known.get(k, 0) >= d.val:
                            continue
                        known[k] = d.val
                        e.wait_ge(sem, d.val)
                    ins = op.fn(e)
                    if op.slot is not None:
                        ins.then_inc(op.slot.sem, 16)
                    elif op.sig:
                        ins.then_inc(esem[eng_name], 1)
                if eng_name == "sp":
                    for s in fin:
                        e.wait_ge(s.sem, 16 * s.n)

            block.tensor(lambda e: run("pe", e))
            block.scalar(lambda e: run("act", e))
            block.vector(lambda e: run("dve", e))
            block.gpsimd(lambda e: run("pool", e))
            block.sync(lambda e: run("sp", e))


class Rot:
    def __init__(self, items):
        self.items = items
        self.i = 0

    def next(self):
        it = self.items[self.i % len(self.items)]
        self.i += 1
        return it


def build_program(T, n_layers, final):
    NB = T // 512
    NT = T // 128
    nc = bass.Bass("TRN2", target_bir_lowering=False)

    def din(name, shape, dt=F32):
        return nc.dram_tensor(name, list(shape), dt, kind="ExternalInput").ap()

    L = n_layers
    x = din("x", [T, D])
    pos = din("pos", [1, T], I32)
    w_in = din("w_in", [L, D, DIN])
    w_out = din("w_out", [L, D, D])
    w_uq = din("w_uq", [L, 192, 384])
    w_ukv = din("w_ukv", [L, 128, 512])
    conv_pw = din("conv_pw", [L, 256, 256])
    poolw = din("poolw", [L, 128, 2, 128])
    wsT = din("wsT", [L, 128, 4, 128])
    bsT = din("bsT", [L, 128, 2, 128])
    colp = din("colp", [L, 128, NCOL])
    rowp = din("rowp", [L, NROW])
    w_ff1 = din("w_ff1", [L, D, DFF])
    w_ff2 = din("w_ff2", [L, DFF, D])
    fng = din("fng", [1, D])
    cst = din("cst", [128, NCST])
    out = nc.dram_tensor("out", [T, D], F32, kind="ExternalOutput").ap()
    hbuf = nc.dram_tensor("hbuf", [T, D], F32).ap()
    ropetab = nc.dram_tensor("ropetab", [2, 32, T], F32).ap()

    with ExitStack() as top:
        banks = [top.enter_context(nc.psum_tensor(f"bank{i}", [128, 512], F32)) for i in range(8)]
        sync = Sync(nc, top)
        w_in_sb = top.enter_context(nc.sbuf_tensor("w_in_sb", [128, 8, DIN], BF16))

        with ExitStack() as st:
            P = Prog(nc, "pr", sync)
            sb = lambda n, s, d: st.enter_context(nc.sbuf_tensor(n, s, d))
            posi = sb("posi", [32, T], I32)
            posf = sb("posf", [32, T], F32)
            ang = sb("ang", [32, T], F32)
            u = sb("u", [32, T], F32)
            ki = sb("ki", [32, T], I32)
            kf = sb("kf", [32, T], F32)
            ng = sb("ng", [32, T], F32)
            tab = [sb("tab0", [32, T], F32), sb("tab1", [32, T], F32)]
            c_sb = sb("c_sb", [128, NCST], F32)
            nbias = sb("nbias", [32, 1], F32)
            P.add("sp", lambda e: e.dma_start(out=posi[:], in_=pos[0, :].partition_broadcast(32)), writes=["posi"], slot="posi")
            P.add("sp", lambda e: e.dma_start(out=c_sb[:], in_=cst), writes=["c"], slot="c")
            P.add("dve", lambda e: e.tensor_copy(out=posf[:], in_=posi[:]), reads=["posi"], writes=["posf"])
            P.add("dve", lambda e: e.memset(nbias[:], -math.pi * (1 - 1e-6)), writes=["nbias"])
            P.add("dve", lambda e: e.tensor_scalar(out=ang[:], in0=posf[:], scalar1=c_sb[0:32, 290:291], scalar2=None, op0=ALU.mult),
                  reads=["posf", "c"], writes=["ang"])
            for i, shift in enumerate((0.75, 0.5)):
                P.add("dve", lambda e, shift=shift: e.tensor_scalar(out=u[:], in0=ang[:], scalar1=1.0 / (2 * math.pi), scalar2=shift, op0=ALU.mult, op1=ALU.add),
                      reads=["ang"], writes=["u"])
                P.add("dve", lambda e: e.tensor_copy(out=ki[:], in_=u[:]), reads=["u"], writes=["ki"])
                P.add("dve", lambda e: e.tensor_copy(out=kf[:], in_=ki[:]), reads=["ki"], writes=["kf"])
                P.add("dve", lambda e: e.tensor_tensor(out=u[:], in0=u[:], in1=kf[:], op=ALU.subtract), reads=["u", "kf"], writes=["u"])
                P.add("dve", lambda e: e.tensor_scalar(out=ng[:], in0=u[:], scalar1=0.0, scalar2=None, op0=ALU.is_lt), reads=["u"], writes=["ng"])
                P.add("dve", lambda e: e.tensor_tensor(out=u[:], in0=u[:], in1=ng[:], op=ALU.add), reads=["u", "ng"], writes=["u"])
                P.add("act", lambda e, i=i: e.activation(out=tab[i][:], in_=u[:], func=AF.Sin, scale=2 * math.pi * (1 - 1e-6), bias=nbias[:]),
                      reads=["u", "nbias"], writes=[f"tab{i}"])
                P.add("sp", lambda e, i=i: e.dma_start(out=ropetab[i], in_=tab[i][:]), reads=[f"tab{i}"], slot=f"tabo{i}")
            P.emit()

        for l in range(n_layers):
            hsrc = x if l == 0 else hbuf
            if True:
              emit_pass_a(nc, sync, l, T, NB, NT, banks, hsrc, hbuf, ropetab, pos, w_in_sb,
                        dict(w_in=w_in, w_out=w_out, w_uq=w_uq, w_ukv=w_ukv, conv_pw=conv_pw, poolw=poolw,
                             wsT=wsT, bsT=bsT, colp=colp, rowp=rowp, cst=cst))
            is_last = (l == n_layers - 1)
            if True:
              emit_pass_b(nc, sync, l, T, NT, banks, hbuf, out, dict(w_ff1=w_ff1, w_ff2=w_ff2, rowp=rowp, fng=fng, cst=cst, w_in=w_in, w_in_sb=w_in_sb, n_layers=n_layers),
                        do_final=(is_last and final), to_out=is_last)
    return nc


def emit_pass_b(nc, sync, l, T, NT, banks, hbuf, out, W, do_final, to_out):
    NBB = T // 256
    with ExitStack() as st:
        P = Prog(nc, f"b{l}", sync)
        sb = lambda n, s, d: st.enter_context(nc.sbuf_tensor(f"B{l}_{n}", s, d))
        W1 = sb("W1", [128, 8, DFF], BF16)
        W2 = sb("W2", [128, 32, D], BF16)
        gff = sb("gff", [128, D], F32)
        gfin = sb("gfin", [128, D], F32)
        c_sb = sb("c_sb", [128, 128], F32)
        ident = sb("ident", [128, 128], BF16)
        hn = [sb(f"hn{i}", [128, D], F32) for i in range(4)]
        xn = [sb(f"xn{i}", [128, D], BF16) for i in range(4)]
        xnT = [sb(f"xnT{i}", [128, 8, 256], BF16) for i in range(2)]
        rr = [sb(f"r{i}", [128, 256], F32) for i in range(3)]
        fT = [sb(f"fT{i}", [128, 256], BF16) for i in range(3)]
        junk = sb("junk", [128, D], BF16)
        ss = [sb(f"ss{i}", [128, 1], F32) for i in range(4)]
        rs = [sb(f"rs{i}", [128, 1], F32) for i in range(4)]
        ss2 = [sb(f"ss2{i}", [128, 1], F32) for i in range(4)]
        rs2 = [sb(f"rs2{i}", [128, 1], F32) for i in range(4)]
        epsc = sb("epsc", [128, 1], F32)

        w1v = W["w_ff1"][l].rearrange("(k p) n -> p k n", p=128)
        w2v = W["w_ff2"][l].rearrange("(c p) n -> p c n", p=128)
        P.add("sp", lambda e: e.dma_start(out=c_sb[:], in_=W["cst"][:, 0:128]), writes=["c"], slot="c")
        P.add("sp", lambda e: e.dma_start(out=gff[:], in_=W["rowp"][l, 1024:2048].partition_broadcast(128)), writes=["gff"], slot="gff")
        if do_final:
            P.add("sp", lambda e: e.dma_start(out=gfin[:], in_=W["fng"][0, :].partition_broadcast(128)), writes=["gfin"], slot="gfin")
        P.add("dve", lambda e: e.tensor_copy(out=ident[:], in_=c_sb[:]), reads=["c"], writes=["ident"])
        P.add("dve", lambda e: e.memset(epsc[:], EPS), writes=["eps"])
        for g in range(8):
            P.add("pool", lambda e, g=g: e.dma_start(out=W1[:, :, g * 512:(g + 1) * 512], in_=w1v[:, :, g * 512:(g + 1) * 512]),
                  writes=[f"W1g{g}"], slot=f"w1_{g % 4}")
            P.add("pool", lambda e, g=g: e.dma_start(out=W2[:, 4 * g:4 * g + 4, :], in_=w2v[:, 4 * g:4 * g + 4, :]),
                  writes=[f"W2g{g}"], slot=f"w2_{g % 4}")

        trb7 = banks[7][:, :].bitcast(BF16)

        def prep_load(bb):
            for i in range(2):
                t = 2 * bb + i
                s = (bb % 2) * 2 + i
                P.add("sp", lambda e, t=t, s=s: e.dma_start(out=hn[s][:], in_=hbuf[t * 128:(t + 1) * 128, :]),
                      reads=[f"hd{t}"], writes=[f"hn{s}"], slot=f"hn{s}")
                P.add("act", lambda e, s=s: e.activation(out=junk[:], in_=hn[s][:], func=AF.Square, accum_out=ss[s][:]),
                      reads=[f"hn{s}"], writes=[f"ss{s}"])
                P.add("act", lambda e, s=s: e.activation(out=rs[s][:], in_=ss[s][:], func=AF.Ln, scale=1.0 / D, bias=epsc[:]),
                      reads=[f"ss{s}", "eps"], writes=[f"sd{s}"])
                P.add("act", lambda e, s=s: e.activation(out=rs[s][:], in_=rs[s][:], func=AF.Exp, scale=-0.5), reads=[f"sd{s}"], writes=[f"sd{s}"])
                P.add("dve", lambda e, s=s: e.scalar_tensor_tensor(out=xn[s][:], in0=hn[s][:], scalar=rs[s][:], in1=gff[:], op0=ALU.mult, op1=ALU.mult),
                      reads=[f"hn{s}", f"sd{s}", "gff"], writes=[f"xn{s}"])

        def prep_tr(bb):
            xt = xnT[bb % 2]
            for hb in range(2):
                for i in range(2):
                    s = (bb % 2) * 2 + i
                    for kk in range(4):
                        k = hb * 4 + kk
                        off = kk * 256 + i * 128
                        P.add("pe", lambda e, s=s, k=k, off=off: e.transpose(out=trb7[:, off:off + 128], in_=xn[s][:, k * 128:(k + 1) * 128], identity=ident[:]),
                              reads=[f"xn{s}", "ident"], writes=["bank7"])
                dst = xt[:, hb * 4:hb * 4 + 4, :].rearrange("p k t -> p (k t)")
                if hb == 0:
                    P.add("dve", lambda e, dst=dst: e.tensor_copy(out=dst, in_=trb7[:, :]), reads=["bank7"], writes=[f"xnT{bb % 2}a"])
                else:
                    P.add("act", lambda e, dst=dst: e.activation(out=dst, in_=trb7[:, :], func=AF.Copy), reads=["bank7"], writes=[f"xnT{bb % 2}b"])

        def ff1(bb, c):
            xt = xnT[bb % 2]
            q = c % 3
            pb = banks[4 + q][:, 0:256]
            for k in range(8):
                P.add("pe", lambda e, k=k, pb=pb, xt=xt, c=c: e.matmul(pb, lhsT=W1[:, k, c * 128:(c + 1) * 128], rhs=xt[:, k, :], start=(k == 0), stop=(k == 7)),
                      reads=[f"W1g{c // 4}", f"xnT{bb % 2}a", f"xnT{bb % 2}b"], writes=[f"bank{4 + q}"])
            j = c % 3
            P.add("act", lambda e, pb=pb, j=j: e.activation(out=rr[j][:], in_=pb, func=AF.Relu), reads=[f"bank{4 + q}"], writes=[f"r{j}"])
            P.add("pool", lambda e, j=j: e.tensor_tensor(out=fT[j][:], in0=rr[j][:], in1=rr[j][:], op=ALU.mult), reads=[f"r{j}"], writes=[f"fT{j}"])

        def ff2(bb, c):
            j = c % 3
            for i in range(2):
                for n in range(2):
                    P.add("pe", lambda e, i=i, n=n, j=j, c=c: e.matmul(banks[i * 2 + n][:, :], lhsT=fT[j][:, i * 128:(i + 1) * 128], rhs=W2[:, c, n * 512:(n + 1) * 512],
                                                                 start=(c == 0), stop=(c == 31)),
                          reads=[f"fT{j}", f"W2g{c // 4}"], writes=[f"bank{i * 2 + n}"])

        def epilogue(bb):
            for i in range(2):
                t = 2 * bb + i
                s = (bb % 2) * 2 + i
                for n in range(2):
                    P.add("dve", lambda e, i=i, n=n, s=s: e.tensor_tensor(out=hn[s][:, n * 512:(n + 1) * 512], in0=banks[i * 2 + n][:, :], in1=hn[s][:, n * 512:(n + 1) * 512], op=ALU.add),
                          reads=[f"bank{i * 2 + n}", f"hn{s}"], writes=[f"hn{s}"])
                if do_final:
                    P.add("act", lambda e, s=s: e.activation(out=junk[:], in_=hn[s][:], func=AF.Square, accum_out=ss2[s][:]),
                          reads=[f"hn{s}"], writes=[f"ss2{s}"])
                    P.add("act", lambda e, s=s: e.activation(out=rs2[s][:], in_=ss2[s][:], func=AF.Ln, scale=1.0 / D, bias=epsc[:]),
                          reads=[f"ss2{s}", "eps"], writes=[f"sd2{s}"])
                    P.add("act", lambda e, s=s: e.activation(out=rs2[s][:], in_=rs2[s][:], func=AF.Exp, scale=-0.5), reads=[f"sd2{s}"], writes=[f"sd2{s}"])
                    P.add("dve", lambda e, s=s: e.scalar_tensor_tensor(out=hn[s][:], in0=hn[s][:], scalar=rs2[s][:], in1=gfin[:], op0=ALU.mult, op1=ALU.mult),
                          reads=[f"hn{s}", f"sd2{s}", "gfin"], writes=[f"hn{s}"])
                dst = out if to_out else hbuf
                P.add("sp", lambda e, t=t, s=s, dst=dst: e.dma_start(out=dst[t * 128:(t + 1) * 128, :], in_=hn[s][:]),
                      reads=[f"hn{s}"], writes=[f"hd{t}"], slot=f"st{s}")

        prep_load(0)
        prep_tr(0)
        for bb in range(NBB):
            if bb == NBB // 2 and l + 1 < W["n_layers"]:
                wvn = W["w_in"][l + 1].rearrange("(k p) n -> p k n", p=128)
                for g in range(4):
                    P.add("pool", lambda e, g=g: e.dma_start(out=W["w_in_sb"][:, 2 * g:2 * g + 2, :], in_=wvn[:, 2 * g:2 * g + 2, :]), writes=[f"w_in{g}"], slot=f"wl{g}")
            for c in range(32):
                ff1(bb, c)
                if c > 0:
                    ff2(bb, c - 1)
                if c == 4 and bb + 1 < NBB:
                    prep_load(bb + 1)
                if c == 20 and bb + 1 < NBB:
                    prep_tr(bb + 1)
            ff2(bb, 31)
            epilogue(bb)
        P.emit()


def emit_pass_a(nc, sync, l, T, NB, NT, banks, hsrc, hbuf, ropetab, pos, w_in_sb, W):
    with ExitStack() as st:
        P = Prog(nc, f"a{l}", sync)
        sb = lambda n, s, d: st.enter_context(nc.sbuf_tensor(f"A{l}_{n}", s, d))
        w_krr = sb("w_krr", [128, 8, 128], BF16)
        w_out_sb = sb("w_out", [128, 8, D], BF16)
        w_uq_sb = sb("w_uq", [128, 2, 416], BF16)
        w_uqr = sb("w_uqr", [128, 2, 416], BF16)
        w_ukv_sb = sb("w_ukv", [128, 512], BF16)
        pw_sb = sb("pw", [128, 2, 256], BF16)
        w_kn = sb("w_kn", [128, 256], BF16)
        w_v = sb("w_v", [128, 256], BF16)
        poolw_sb = sb("poolw", [128, 2, 128], BF16)
        wsT_sb = sb("wsT", [128, 4, 128], BF16)
        dg = sb("dg", [128, 8, 128], BF16)
        colp = sb("colp", [128, NCOL], F32)
        gmix = sb("gmix", [128, D], F32)
        ggm = sb("ggm", [128, 256], F32)
        bsT = sb("bsT", [128, 2, 128], F32)
        c_sb = sb("c_sb", [128, NCST], F32)
        ident = sb("ident", [128, 128], BF16)
        maskT = sb("maskT", [128, 128], BF16)
        ones = sb("ones", [128, 128], BF16)
        epsc = sb("epsc", [128, 1], F32)
        onec = sb("onec", [128, 1], F32)
        KT = sb("KT", [128, 4, T], BF16)
        Vt = sb("Vt", [128, NT, 4, 65], BF16)
        ycv = sb("ycv", [128, 2, 542], BF16)
        zp = sb("zp", [128, 2, 527], F32)
        hn = [sb(f"hn{i}", [128, D], F32) for i in range(2)]
        xn = [sb(f"xn{i}", [128, D], BF16) for i in range(4)]
        hr = [sb(f"hr{i}", [128, D], F32) for i in range(2)]
        xnT = sb("xnT", [128, 8, 512], BF16)
        mixedT = sb("mixedT", [128, 8, 512], BF16)
        QT = sb("QT", [128, 4, 512], BF16)
        PT = [sb(f"PT{i}", [128, 512], BF16) for i in range(3)]
        cos2 = sb("cos2", [128, 512], F32)
        sin2 = sb("sin2", [128, 512], F32)
        junk = sb("junk", [128, 256], BF16)
        Fm = [sb(f"F{i}", [128, 512], F32) for i in range(10)]
        Hm = [sb(f"H{i}", [128, 512], BF16) for i in range(6)]
        PA = sb("PA", [128, 527], F32)
        PB = sb("PB", [128, 527], F32)
        omla = sb("omla", [128, 4, 256], F32)
        vn = sb("vn", [128, 4, 256], BF16)
        mtok = [sb(f"mtok{i}", [128, 256], BF16) for i in range(2)]
        sm = sb("sm", [128, 64], F32)

        def F(i):
            return Fm[i], f"F{i}"

        def H(i):
            return Hm[i], f"H{i}"

        P.add("sp", lambda e: e.dma_start(out=c_sb[:], in_=W["cst"]), writes=["c"], slot="c")
        P.add("sp", lambda e: e.dma_start(out=colp[:], in_=W["colp"][l]), writes=["colp"], slot="colp")
        P.add("sp", lambda e: e.dma_start(out=gmix[:], in_=W["rowp"][l, 0:1024].partition_broadcast(128)), writes=["gmix"], slot="gmix")
        P.add("sp", lambda e: e.dma_start(out=ggm[:], in_=W["rowp"][l, 2048:2304].partition_broadcast(128)), writes=["ggm"], slot="ggm")
        P.add("sp", lambda e: e.dma_start(out=bsT[:], in_=W["bsT"][l]), writes=["bsT"], slot="bsT")
        P.add("dve", lambda e: e.tensor_copy(out=ident[:], in_=c_sb[:, 0:128]), reads=["c"], writes=["ident"])
        P.add("dve", lambda e: e.tensor_copy(out=maskT[:], in_=c_sb[:, 128:256]), reads=["c"], writes=["maskT"])
        P.add("dve", lambda e: e.memset(ones[:], 1.0), writes=["ones"])
        P.add("dve", lambda e: e.memset(epsc[:], EPS), writes=["eps"])
        P.add("dve", lambda e: e.memset(onec[:], 1.0), writes=["eps"])
        P.add("pool", lambda e: e.memset(KT[96:128, :, :], 0.0), writes=["KTpad"])
        P.add("pool", lambda e: e.memset(QT[96:128, :, :], 0.0), writes=["QTpad"])
        P.add("pool", lambda e: e.memset(w_uq_sb[:, :, 384:416], 0.0), writes=["w_uq"])
        P.add("pool", lambda e: e.memset(w_uq_sb[64:128, 1, :], 0.0), writes=["w_uq"])
        P.add("pool", lambda e: e.memset(w_uqr[:], 0.0), writes=["w_uqr"])
        P.add("pool", lambda e: e.memset(ycv[:], 0.0), writes=["ycvh0", "ycvh1", "ycv0", "ycv1"])
        P.add("pool", lambda e: e.memset(zp[:], 0.0), writes=["zph0", "zph1", "zp0", "zp1"])
        P.add("pool", lambda e: e.memset(Vt[:], 1.0), writes=["Vinit"])
        P.add("pool", lambda e: e.memset(w_krr[:], 0.0), writes=["w_krr"])
        wv = W["w_in"][l].rearrange("(k p) n -> p k n", p=128)
        for g in range(4 if l == 0 else 0):
            P.add("pool", lambda e, g=g: e.dma_start(out=w_in_sb[:, 2 * g:2 * g + 2, :], in_=wv[:, 2 * g:2 * g + 2, :]), writes=[f"w_in{g}"], slot=f"wl{g}")
        P.add("pool", lambda e: e.dma_start(out=w_uq_sb[:, 0, 0:384], in_=W["w_uq"][l, 0:128, :]), writes=["w_uq"], slot="wl0")
        P.add("pool", lambda e: e.dma_start(out=w_uq_sb[0:64, 1, 0:384], in_=W["w_uq"][l, 128:192, :]), writes=["w_uq"], slot="wl1")
        P.add("pool", lambda e: e.dma_start(out=w_ukv_sb[:], in_=W["w_ukv"][l]), writes=["w_ukv"], slot="wl2")
        P.add("pool", lambda e: e.dma_start(out=pw_sb[:], in_=W["conv_pw"][l].rearrange("(k p) n -> p k n", p=128)), writes=["pw"], slot="wl3")
        P.add("pool", lambda e: e.dma_start(out=poolw_sb[:], in_=W["poolw"][l]), writes=["poolw"], slot="wl0")
        P.add("pool", lambda e: e.dma_start(out=wsT_sb[:], in_=W["wsT"][l]), writes=["wsT"], slot="wl1")
        wov = W["w_out"][l].rearrange("(k p) n -> p k n", p=128)
        for g in range(2):
            P.add("pool", lambda e, g=g: e.dma_start(out=w_out_sb[:, 4 * g:4 * g + 4, :], in_=wov[:, 4 * g:4 * g + 4, :]), writes=[f"w_out{g}"], slot=f"wl{2 + g}")
        WIN = [f"w_in{g}" for g in range(4)]
        WOUT = ["w_out0", "w_out1"]
        ukv4 = w_ukv_sb[:, :].rearrange("p (h d) -> p h d", h=4)
        P.add("act", lambda e: e.activation(out=w_kn[:, :].rearrange("p (h d) -> p h d", h=4), in_=ukv4[:, :, 0:64], func=AF.Copy), reads=["w_ukv"], writes=["w_kn"])
        P.add("act", lambda e: e.activation(out=w_v[:, :].rearrange("p (h d) -> p h d", h=4), in_=ukv4[:, :, 64:128], func=AF.Copy), reads=["w_ukv"], writes=["w_v"])
        for h in range(4):
            P.add("dve", lambda e, h=h: e.tensor_tensor(out=wsT_sb[:, h, :], in0=wsT_sb[:, h, :], in1=maskT[:], op=ALU.mult),
                  reads=["wsT", "maskT"], writes=["wsT"])
        for kc in range(2):
            pr = slice(0, 128) if kc == 0 else slice(0, 64)
            src = w_uq_sb[pr, kc, 0:384].rearrange("p (h d) -> p h d", h=4)
            dst = w_uqr[pr, kc, 0:384].rearrange("p (h d) -> p h d", h=4)
            P.add("act", lambda e, src=src, dst=dst: e.mul(out=dst[:, :, 64:80], in_=src[:, :, 80:96], mul=-1.0), reads=["w_uq", "w_uqr"], writes=["w_uqr"])
            P.add("act", lambda e, src=src, dst=dst: e.activation(out=dst[:, :, 80:96], in_=src[:, :, 64:80], func=AF.Copy), reads=["w_uq", "w_uqr"], writes=["w_uqr"])
            P.add("act", lambda e, src=src, dst=dst: e.activation(out=dst[:, :, 0:64], in_=src[:, :, 0:64], func=AF.Copy), reads=["w_uq", "w_uqr"], writes=["w_uqr"])
        P.add("act", lambda e: e.mul(out=w_krr[:, :, 64:80], in_=w_in_sb[:, :, 848:864], mul=-1.0), reads=WIN + ["w_krr"], writes=["w_krr"])
        P.add("act", lambda e: e.activation(out=w_krr[:, :, 80:96], in_=w_in_sb[:, :, 832:848], func=AF.Copy), reads=WIN + ["w_krr"], writes=["w_krr"])
        gen3 = Rot([(banks[i], f"bank{i}") for i in range(3)])
        gen1 = [Rot([(banks[i], f"bank{i}")]) for i in range(3)]
        genh = [gen3]
        stb = Rot([(banks[i], f"bank{i}") for i in range(3, 6)])
        pvb = Rot([(banks[i], f"bank{i}") for i in range(6, 8)])
        ptr = Rot([(PT[i], f"PT{i}") for i in range(3)])
        SCALE = 96.0 ** -0.5

        def mm_group(pout, pres, parts, extra_reads=()):
            n = len(parts)
            for i, (lt, rh, rd) in enumerate(parts):
                P.add("pe", lambda e, lt=lt, rh=rh, i=i: e.matmul(pout, lhsT=lt, rhs=rh, start=(i == 0), stop=(i == n - 1)),
                      reads=list(rd) + list(extra_reads), writes=[pres])

        def rstd_bcast(sq_parts, scale, dst, dres):
            bk, br = genh[0].next()
            mm_group(bk[:, :], br, [(ones[0:sq.shape[0], :], sq, rd + ["ones"]) for sq, rd in sq_parts])
            P.add("act", lambda e: e.activation(out=dst, in_=bk[:, :], func=AF.Ln, scale=scale, bias=epsc[:]), reads=[br, "eps"], writes=[dres])
            P.add("act", lambda e: e.activation(out=dst, in_=dst, func=AF.Exp, scale=-0.5), reads=[dres], writes=[dres])

        def group_norm_fm(g, of, ofres, sqt, rtt):
            sqs = []
            for c in range(2):
                hq, hres = sqt[c]
                hres = hres if isinstance(hres, list) else [hres]
                P.add("pool", lambda e, c=c, hq=hq: e.tensor_tensor(out=hq, in0=of[c], in1=of[c], op=ALU.mult), reads=[ofres[c]], writes=hres)
                sqs.append((hq, hres))
            rt, rres = rtt
            rstd_bcast(sqs, 1.0 / 256, rt, rres)
            for c in range(2):
                P.add("dve", lambda e, c=c: e.scalar_tensor_tensor(out=mixedT[:, 2 * g + c, :], in0=of[c], scalar=colp[:, 75 + 2 * g + c:76 + 2 * g + c], in1=rt,
                                                                  op0=ALU.mult, op1=ALU.mult),
                      reads=[ofres[c], rres, "colp"], writes=[f"mixedT{2 * g + c}"])

        genpre = Rot([(banks[i], f"bank{i}") for i in (1, 2, 3, 4)])
        genpost = Rot([(banks[i], f"bank{i}") for i in (0, 5)])
        pre_ops, main_ops, post_ops, prenorm_ops = [], [], [], []
        for b in range(NB):
            tb = b * 512
            P.cap = []
            genh[0] = genpre
            P.add("sp", lambda e, tb=tb: e.dma_start(out=cos2[64:96, :], in_=ropetab[0, :, tb:tb + 512]), writes=["cos2"], slot="cos2")
            P.add("sp", lambda e, tb=tb: e.dma_start(out=sin2[64:96, :], in_=ropetab[1, :, tb:tb + 512]), writes=["sin2"], slot="sin2")
            prenorm_cap = P.cap
            P.cap = []
            for ip in range(2):
                for i in (2 * ip, 2 * ip + 1):
                    t = 4 * b + i
                    s = i % 2
                    P.add("sp", lambda e, t=t, s=s: e.dma_start(out=hn[s][:], in_=hsrc[t * 128:(t + 1) * 128, :]), reads=[f"hd{t}"], writes=[f"hn{s}"], slot=f"hn{s}")
                    P.add("act", lambda e, s=s, i=i: e.activation(out=xn[i][:], in_=hn[s][:], func=AF.Square, accum_out=sm[:, 40 + i:41 + i]), reads=[f"hn{s}"], writes=[f"ss{i}", f"xn{i}"])
                lo = 2 * ip
                P.add("act", lambda e, lo=lo: e.activation(out=sm[:, 44 + lo:46 + lo], in_=sm[:, 40 + lo:42 + lo], func=AF.Ln, scale=1.0 / D, bias=epsc[:]),
                      reads=[f"ss{lo}", f"ss{lo + 1}", "eps"], writes=[f"sdp{ip}"])
                P.add("act", lambda e, lo=lo: e.activation(out=sm[:, 44 + lo:46 + lo], in_=sm[:, 44 + lo:46 + lo], func=AF.Exp, scale=-0.5), reads=[f"sdp{ip}"], writes=[f"sdp{ip}"])
                for i in (2 * ip, 2 * ip + 1):
                    s = i % 2
                    P.add("dve", lambda e, s=s, i=i: e.scalar_tensor_tensor(out=xn[i][:], in0=hn[s][:], scalar=sm[:, 44 + i:45 + i], in1=gmix[:], op0=ALU.mult, op1=ALU.mult),
                          reads=[f"hn{s}", f"sdp{ip}", "gmix"], writes=[f"xn{i}"])
            prenorm_ops.append(P.cap)
            P.cap = prenorm_cap
            for i in range(4):
                for hb in range(2):
                    bk, br = genh[0].next()
                    bv = bk[:, :].bitcast(BF16)
                    for kk in range(4):
                        k = hb * 4 + kk
                        P.add("pe", lambda e, i=i, k=k, kk=kk, bv=bv: e.transpose(out=bv[:, kk * 128:(kk + 1) * 128], in_=xn[i][:, k * 128:(k + 1) * 128], identity=ident[:]),
                              reads=[f"xn{i}", "ident"], writes=[br])
                    eng = "dve" if hb == 0 else "act"
                    dst = xnT[:, hb * 4:hb * 4 + 4, i * 128:(i + 1) * 128]
                    src = bv[:, 0:512].rearrange("p (k t) -> p k t", k=4)
                    if eng == "dve":
                        P.add("dve", lambda e, dst=dst, src=src: e.tensor_copy(out=dst, in_=src), reads=[br], writes=[f"xnT{i}"])
                    else:
                        P.add("act", lambda e, dst=dst, src=src: e.activation(out=dst, in_=src, func=AF.Copy), reads=[br], writes=[f"xnT{i}"])
            XNT = [f"xnT{i}" for i in range(4)]

            def inproj(c0, c1, lw=None):
                bk, br = genh[0].next()
                m = c1 - c0
                parts = []
                for k in range(8):
                    lt = (w_in_sb[:, k, c0:c1] if lw is None else lw[:, k, :])
                    parts.append((lt, xnT[:, k, :], XNT + WIN + (["w_krr"] if lw is not None else [])))
                mm_group(bk[0:m, :], br, parts)
                return bk, br

            craw = [F(0), F(1), F(2)]
            csq = [H(0), H(1), H(2)]
            spans = [(512, 640), (640, 704), (704, 832)]
            for j, (c0, c1) in enumerate(spans):
                m = c1 - c0
                bk, br = inproj(c0, c1)
                ft, fr = craw[j]
                P.add("act", lambda e, bk=bk, ft=ft, m=m: e.activation(out=ft[0:m, :], in_=bk[0:m, :], func=AF.Copy), reads=[br], writes=[fr])
                ht, hres = csq[j]
                P.add("pool", lambda e, ft=ft, ht=ht, m=m: e.tensor_tensor(out=ht[0:m, :], in0=ft[0:m, :], in1=ft[0:m, :], op=ALU.mult), reads=[fr], writes=[hres])
            rq, rqres = F(3)
            rstd_bcast([(csq[0][0][:, :], [csq[0][1]]), (csq[1][0][0:64, :], [csq[1][1]])], 1.0 / 192, rq[:], rqres)
            rkv, rkvres = F(4)
            rstd_bcast([(csq[2][0][:, :], [csq[2][1]])], 1.0 / 128, rkv[:], rkvres)
            cqn = [H(3), H(4), H(5)]
            gcols = [68, 69, 70]
            for j in range(3):
                m = spans[j][1] - spans[j][0]
                ft, fr = craw[j]
                ht, hres = cqn[j]
                rt, rres = (rq, rqres) if j < 2 else (rkv, rkvres)
                P.add("dve", lambda e, ft=ft, ht=ht, rt=rt, m=m, gc=gcols[j]: e.scalar_tensor_tensor(out=ht[0:m, :], in0=ft[0:m, :], scalar=colp[0:m, gc:gc + 1], in1=rt[0:m, :],
                                                                                                   op0=ALU.mult, op1=ALU.mult),
                      reads=[fr, rres, "colp"], writes=[hres])
            for h in range(4):
                pa, par = genh[0].next()
                mm_group(pa[:, :], par, [(w_uq_sb[:, 0, h * 96:h * 96 + 128], cqn[0][0][:, :], [cqn[0][1], "w_uq"]),
                                         (w_uq_sb[0:64, 1, h * 96:h * 96 + 128], cqn[1][0][0:64, :], [cqn[1][1], "w_uq"])])
                pbk, pbr = genh[0].next()
                mm_group(pbk[:, :], pbr, [(w_uqr[:, 0, h * 96:h * 96 + 128], cqn[0][0][:, :], [cqn[0][1], "w_uqr"]),
                                          (w_uqr[0:64, 1, h * 96:h * 96 + 128], cqn[1][0][0:64, :], [cqn[1][1], "w_uqr"])])
                P.add("act", lambda e, pa=pa, h=h: e.activation(out=QT[0:64, h, :], in_=pa[0:64, :], func=AF.Copy), reads=[par], writes=[f"QT{h}"])
                t1, t1r = F(5)
                t2, t2r = F(6)
                P.add("dve", lambda e, pa=pa, t1=t1: e.tensor_tensor(out=t1[64:96, :], in0=pa[64:96, :], in1=cos2[64:96, :], op=ALU.mult), reads=[par, "cos2"], writes=[t1r])
                P.add("dve", lambda e, pbk=pbk, t2=t2: e.tensor_tensor(out=t2[64:96, :], in0=pbk[64:96, :], in1=sin2[64:96, :], op=ALU.mult), reads=[pbr, "sin2"], writes=[t2r])
                P.add("pool", lambda e, t1=t1, t2=t2, h=h: e.tensor_tensor(out=QT[64:96, h, :], in0=t1[64:96, :], in1=t2[64:96, :], op=ALU.add), reads=[t1r, t2r], writes=[f"QT{h}r"])
            for hp in range(2):
                bk, br = genh[0].next()
                lt = w_kn[:, hp * 128:(hp + 1) * 128]
                mm_group(bk[:, :], br, [(lt, cqn[2][0][:, :], [cqn[2][1], "w_kn"])])
                P.add("act", lambda e, bk=bk, hp=hp, tb=tb: e.activation(out=KT[0:64, 2 * hp, tb:tb + 512], in_=bk[0:64, :], func=AF.Copy), reads=[br], writes=[f"KT{b}_{2 * hp}"])
                P.add("dve", lambda e, bk=bk, hp=hp, tb=tb: e.tensor_copy(out=KT[0:64, 2 * hp + 1, tb:tb + 512], in_=bk[64:128, :]), reads=[br], writes=[f"KT{b}_{2 * hp + 1}"])
            ka, kar = inproj(768, 896)
            kb, kbr = inproj(0, 128, lw=w_krr)
            t1, t1r = F(5)
            t2, t2r = F(6)
            P.add("dve", lambda e, ka=ka, t1=t1: e.tensor_tensor(out=t1[64:96, :], in0=ka[64:96, :], in1=cos2[64:96, :], op=ALU.mult), reads=[kar, "cos2"], writes=[t1r])
            P.add("dve", lambda e, kb=kb, t2=t2: e.tensor_tensor(out=t2[64:96, :], in0=kb[64:96, :], in1=sin2[64:96, :], op=ALU.mult), reads=[kbr, "sin2"], writes=[t2r])
            for h in range(4):
                P.add("pool", lambda e, t1=t1, t2=t2, h=h, tb=tb: e.tensor_tensor(out=KT[64:96, h, tb:tb + 512], in0=t1[64:96, :], in1=t2[64:96, :], op=ALU.add),
                      reads=[t1r, t2r], writes=[f"KT{b}_{h}r"])
            for i2 in range(2):
                bk, br = genh[0].next()
                for ii in range(2):
                    i = 2 * i2 + ii
                    rh = w_v[:, :]
                    P.add("pe", lambda e, bk=bk, ii=ii, i=i, rh=rh: e.matmul(bk[:, ii * 256:(ii + 1) * 256], lhsT=cqn[2][0][:, i * 128:(i + 1) * 128], rhs=rh, start=True, stop=True),
                          reads=[cqn[2][1], "w_v"], writes=[br])
                for ii in range(2):
                    i = 2 * i2 + ii
                    dst = Vt[:, 4 * b + i, :, 0:64]
                    src = bk[:, ii * 256:(ii + 1) * 256].rearrange("p (h d) -> p h d", h=4)
                    P.add("act", lambda e, dst=dst, src=src: e.activation(out=dst, in_=src, func=AF.Copy), reads=[br, "Vinit"], writes=[f"V{4 * b + i}"])

            pre_ops.append(P.cap)
            P.cap = None
            nk = 4 * b + 4
            units = [(h, kt) for h in range(4) for kt in range(nk)]
            accs = {}
            pts = {}
            LA = 2

            def emit_s(h, kt):
                j0 = max(0, kt - 4 * b)
                q0 = j0 * 128
                sbk, sbr = stb.next()
                P.add("pe", lambda e, sbk=sbk, h=h, kt=kt, q0=q0: e.matmul(sbk[:, q0:512], lhsT=KT[:, h, kt * 128:(kt + 1) * 128], rhs=QT[:, h, q0:512], start=True, stop=True),
                      reads=[f"KT{kt // 4}_{h}", f"KT{kt // 4}_{h}r", f"QT{h}", f"QT{h}r", "KTpad", "QTpad"], writes=[sbr])
                pt, ptres = ptr.next()
                P.add("act", lambda e, sbk=sbk, pt=pt, q0=q0: e.activation(out=pt[:, q0:512], in_=sbk[:, q0:512], func=AF.Exp, scale=SCALE), reads=[sbr], writes=[ptres])
                if kt >= 4 * b:
                    P.add("pool", lambda e, pt=pt, q0=q0: e.tensor_tensor(out=pt[:, q0:q0 + 128], in0=pt[:, q0:q0 + 128], in1=maskT[:], op=ALU.mult),
                          reads=[ptres, "maskT"], writes=[ptres])
                pts[(h, kt)] = (pt, ptres, j0)

            def emit_pv(h, kt):
                if kt == 0:
                    accs[h] = pvb.next()
                acc, accr = accs[h]
                accv = acc[:, 0:260].rearrange("p (j d) -> p j d", j=4)
                pt, ptres, j0 = pts.pop((h, kt))
                for j in range(j0, 4):
                    first = (kt == 0 and j == j0)
                    P.add("pe", lambda e, pt=pt, j=j, kt=kt, h=h, first=first, accv=accv: e.matmul(accv[:, j, :], lhsT=pt[:, j * 128:(j + 1) * 128], rhs=Vt[:, kt, h, :],
                                                                                                start=first, stop=(kt == nk - 1 and j == 3), skip_group_check=True),
                          reads=[ptres, f"V{kt}", "Vinit"], writes=[accr])
                if kt == nk - 1:
                    P.add("dve", lambda e, accv=accv, h=h: e.reciprocal(out=sm[:, 8 + 4 * h:12 + 4 * h], in_=accv[:, :, 64]), reads=[accr], writes=[f"rec{h}"])
                    for j in range(4):
                        P.add("dve", lambda e, accv=accv, h=h, j=j: e.tensor_scalar(out=omla[:, j, h * 64:(h + 1) * 64], in0=accv[:, j, 0:64], scalar1=sm[:, 8 + 4 * h + j:9 + 4 * h + j],
                                                                                    scalar2=None, op0=ALU.mult),
                              reads=[accr, f"rec{h}"], writes=[f"omla{j}"])

            att_units = []
            for idx in range(len(units) + LA):
                P.cap = []
                if idx < len(units):
                    emit_s(*units[idx])
                if idx - LA >= 0:
                    emit_pv(*units[idx - LA])
                att_units.append(P.cap)
                P.cap = None
            streams = []
            def emit_mla_out():
                bk, br = genh[0].next()
                bv = bk[:, :].bitcast(BF16)
                for j in range(4):
                    P.add("act", lambda e, j=j: e.activation(out=junk[:, 0:256], in_=omla[:, j, :], func=AF.Square, accum_out=sm[:, 24 + j:25 + j]), reads=[f"omla{j}"], writes=[f"oss{j}"])
                P.add("act", lambda e: e.activation(out=sm[:, 28:32], in_=sm[:, 24:28], func=AF.Ln, scale=1.0 / 256, bias=epsc[:]), reads=[f"oss{j}" for j in range(4)] + ["eps"], writes=["osd"])
                P.add("act", lambda e: e.activation(out=sm[:, 28:32], in_=sm[:, 28:32], func=AF.Exp, scale=-0.5), reads=["osd"], writes=["osd"])
                for j in range(4):
                    mt = mtok[j % 2]
                    P.add("dve", lambda e, j=j, mt=mt: e.scalar_tensor_tensor(out=mt[:], in0=omla[:, j, :], scalar=sm[:, 28 + j:29 + j], in1=ggm[:], op0=ALU.mult, op1=ALU.mult),
                          reads=[f"omla{j}", "osd", "ggm"], writes=[f"mtok{j % 2}"])
                    for c in range(2):
                        P.add("pe", lambda e, mt=mt, c=c, j=j, bv=bv: e.transpose(out=bv[:, c * 512 + j * 128:c * 512 + (j + 1) * 128], in_=mt[:, c * 128:(c + 1) * 128], identity=ident[:]),
                              reads=[f"mtok{j % 2}", "ident"], writes=[br])
                P.add("act", lambda e, bv=bv: e.activation(out=mixedT[:, 2:4, :].rearrange("p c t -> p (c t)"), in_=bv[:, :], func=AF.Copy), reads=[br], writes=["mixedT2", "mixedT3"])

            P.cap = []
            genh[0] = gen1[0]
            for c in range(2):
                gb, gbr = inproj(256 + c * 128, 256 + (c + 1) * 128)
                sg, sgr = F(0)
                P.add("act", lambda e, gb=gb, sg=sg: e.activation(out=sg[:], in_=gb[:, :], func=AF.Exp, scale=-1.0), reads=[gbr], writes=[sgr])
                P.add("act", lambda e, sg=sg: e.activation(out=sg[:], in_=sg[:], func=AF.Ln, bias=onec[:]), reads=[sgr, "eps"], writes=[sgr])
                P.add("act", lambda e, sg=sg: e.activation(out=sg[:], in_=sg[:], func=AF.Exp, scale=-1.0), reads=[sgr], writes=[sgr])
                ab, abr = inproj(c * 128, (c + 1) * 128)
                P.add("dve", lambda e, ab=ab, sg=sg, c=c: e.tensor_tensor(out=ycv[:, c, 30:542], in0=ab[:, :], in1=sg[:], op=ALU.mult), reads=[abr, sgr], writes=[f"ycv{c}"])
            ycf = [F(1), F(2)]
            ycb = [H(0), H(1)]
            ysq = [H(2), H(3)]
            for c in range(2):
                bk, br = genh[0].next()
                for k in range(31):
                    ds = (c * 31 + k) % 8
                    P.add("dve", lambda e, c=c, k=k, ds=ds: e.tensor_scalar(out=dg[:, ds, :], in0=ident[:], scalar1=colp[:, c * 31 + k:c * 31 + k + 1], scalar2=None, op0=ALU.mult),
                          reads=["ident", "colp"], writes=[f"dg{ds}"])
                    P.add("pe", lambda e, bk=bk, c=c, k=k, ds=ds: e.matmul(bk[:, :], lhsT=dg[:, ds, :], rhs=ycv[:, c, k:k + 512], start=(k == 0), stop=(k == 30)),
                          reads=[f"dg{ds}", f"ycv{c}", f"ycvh{c}"], writes=[br])
                ft, fr = ycf[c]
                P.add("dve", lambda e, bk=bk, ft=ft, c=c: e.tensor_scalar(out=ft[:], in0=bk[:, :], scalar1=colp[:, 62 + c:63 + c], scalar2=None, op0=ALU.add), reads=[br, "colp"], writes=[fr])
                P.add("pool", lambda e, c=c: e.tensor_copy(out=ycv[:, c, 0:30], in_=ycv[:, c, 512:542]), reads=[f"ycv{c}"], writes=[f"ycvh{c}"])
                hb_, hbr = ycb[c]
                P.add("pool", lambda e, ft=ft, hb_=hb_: e.tensor_copy(out=hb_[:], in_=ft[:]), reads=[fr], writes=[hbr])
                hq, hqr = ysq[c]
                P.add("pool", lambda e, ft=ft, hq=hq: e.tensor_tensor(out=hq[:], in0=ft[:], in1=ft[:], op=ALU.mult), reads=[fr], writes=[hqr])
            mt_, mtr = F(3)
            m2, m2r = F(4)
            vr, vrr = F(5)
            mb, mbr = genh[0].next()
            mm_group(mb[:, :], mbr, [(ones[:, :], ycb[c][0][:, :], [ycb[c][1], "ones"]) for c in range(2)])
            P.add("dve", lambda e, mb=mb, mt_=mt_: e.tensor_scalar(out=mt_[:], in0=mb[:, :], scalar1=1.0 / 256, scalar2=None, op0=ALU.mult), reads=[mbr], writes=[mtr])
            qb, qbr = genh[0].next()
            mm_group(qb[:, :], qbr, [(ones[:, :], ysq[c][0][:, :], [ysq[c][1], "ones"]) for c in range(2)])
            P.add("pool", lambda e, mt_=mt_, m2=m2: e.tensor_tensor(out=m2[:], in0=mt_[:], in1=mt_[:], op=ALU.mult), reads=[mtr], writes=[m2r])
            P.add("dve", lambda e, qb=qb, m2=m2, vr=vr: e.scalar_tensor_tensor(out=vr[:], in0=qb[:, :], scalar=1.0 / 256, in1=m2[:], op0=ALU.mult, op1=ALU.subtract),
                  reads=[qbr, m2r], writes=[vrr])
            P.add("act", lambda e, vr=vr: e.activation(out=vr[:], in_=vr[:], func=AF.Ln, bias=epsc[:]), reads=[vrr, "eps"], writes=[vrr])
            P.add("act", lambda e, vr=vr: e.activation(out=vr[:], in_=vr[:], func=AF.Exp, scale=-0.5), reads=[vrr], writes=[vrr])
            sact = [H(0), H(1)]
            for c in range(2):
                ft, fr = ycf[c]
                P.add("dve", lambda e, ft=ft, mt_=mt_: e.tensor_tensor(out=ft[:], in0=ft[:], in1=mt_[:], op=ALU.subtract), reads=[fr, mtr], writes=[fr])
                P.add("pool", lambda e, ft=ft, vr=vr: e.tensor_tensor(out=ft[:], in0=ft[:], in1=vr[:], op=ALU.mult), reads=[fr, vrr], writes=[fr])
                ht, hres = sact[c]
                sg, sgr = F(0)
                P.add("dve", lambda e, ft=ft, c=c: e.tensor_scalar(out=ft[:], in0=ft[:], scalar1=colp[:, 64 + c:65 + c], scalar2=colp[:, 66 + c:67 + c], op0=ALU.mult, op1=ALU.add),
                      reads=[fr, "colp"], writes=[fr])
                P.add("act", lambda e, ft=ft, sg=sg: e.activation(out=sg[:], in_=ft[:], func=AF.Exp, scale=-1.0), reads=[fr], writes=[sgr])
                P.add("act", lambda e, sg=sg: e.activation(out=sg[:], in_=sg[:], func=AF.Ln, bias=onec[:]), reads=[sgr, "eps"], writes=[sgr])
                P.add("act", lambda e, sg=sg: e.activation(out=sg[:], in_=sg[:], func=AF.Exp, scale=-1.0), reads=[sgr], writes=[sgr])
                P.add("pool", lambda e, ft=ft, sg=sg, ht=ht: e.tensor_tensor(out=ht[:], in0=ft[:], in1=sg[:], op=ALU.mult), reads=[fr, sgr], writes=[hres])
            of = [F(3), F(4)]
            for co in range(2):
                bk, br = genh[0].next()
                mm_group(bk[:, :], br, [(pw_sb[:, ci, co * 128:(co + 1) * 128], sact[ci][0][:, :], [sact[ci][1], "pw"]) for ci in range(2)])
                ft, fr = of[co]
                P.add("dve", lambda e, bk=bk, ft=ft: e.tensor_copy(out=ft[:], in_=bk[:, :]), reads=[br], writes=[fr])
            group_norm_fm(0, [of[0][0][:], of[1][0][:]], [of[0][1], of[1][1]], [(Hm[2][:], "H2"), (Hm[3][:], "H3")], (Fm[5][:], "F5"))

            streams.append(P.cap)
            P.cap = []
            genh[0] = gen1[1]
            yp = [H(4), H(5)]
            for c in range(2):
                pbk, pbr = inproj(864 + c * 128, 864 + (c + 1) * 128)
                P.add("dve", lambda e, pbk=pbk, c=c: e.tensor_copy(out=zp[:, c, 15:527], in_=pbk[:, :]), reads=[pbr], writes=[f"zp{c}"])
                zc = zp[:, c, :]
                P.add("pool", lambda e, zc=zc: e.tensor_tensor(out=PA[:, 1:527], in0=zc[:, 1:527], in1=zc[:, 0:526], op=ALU.add), reads=[f"zp{c}", f"zph{c}"], writes=["PA"])
                P.add("pool", lambda e: e.tensor_tensor(out=PB[:, 3:527], in0=PA[:, 3:527], in1=PA[:, 1:525], op=ALU.add), reads=["PA"], writes=["PB"])
                if c == 0:
                    lo, hi = PA, PB
                    lor, hir = "PA", "PB"
                else:
                    P.add("pool", lambda e: e.tensor_tensor(out=PA[:, 7:527], in0=PB[:, 7:527], in1=PB[:, 3:523], op=ALU.add), reads=["PB", "PA"], writes=["PA"])
                    P.add("pool", lambda e: e.tensor_tensor(out=PB[:, 15:527], in0=PA[:, 15:527], in1=PA[:, 7:519], op=ALU.add), reads=["PA", "PB"], writes=["PB"])
                    lo, hi = PA, PB
                    lor, hir = "PA", "PB"
                ht, hres = yp[c]
                for (pr, srcT, srcr) in ((slice(0, 64), lo, lor), (slice(64, 128), hi, hir)):
                    P.add("dve", lambda e, pr=pr, srcT=srcT, c=c, ht=ht, zc=zc: e.scalar_tensor_tensor(out=ht[pr, :], in0=srcT[pr, 15:527], scalar=c_sb[pr, 288 + c:289 + c], in1=zc[pr, 15:527],
                                                                                                     op0=ALU.mult, op1=ALU.subtract),
                          reads=[srcr, f"zp{c}", "c"], writes=[hres])
                    if b == 0:
                        tt, ttr = F(6)
                        P.add("dve", lambda e, pr=pr, srcT=srcT, c=c, tt=tt: e.tensor_tensor(out=tt[pr, 0:16], in0=srcT[pr, 15:31], in1=c_sb[pr, 256 + c * 16:272 + c * 16], op=ALU.mult),
                              reads=[srcr, "c"], writes=[ttr])
                        P.add("dve", lambda e, pr=pr, c=c, tt=tt, ht=ht, zc=zc: e.tensor_tensor(out=ht[pr, 0:16], in0=tt[pr, 0:16], in1=zc[pr, 15:31], op=ALU.subtract),
                              reads=[ttr, f"zp{c}", hres], writes=[hres])
                P.add("pool", lambda e, c=c: e.tensor_copy(out=zp[:, c, 0:15], in_=zp[:, c, 512:527]), reads=[f"zp{c}"], writes=[f"zph{c}"])
            of = [(PA[:, 0:512], "PA"), (PB[:, 0:512], "PB")]
            for c in range(2):
                bk, br = genh[0].next()
                mm_group(bk[:, :], br, [(poolw_sb[:, c, :], yp[c][0][:, :], [yp[c][1], "poolw"])])
                ft, fr = of[c]
                P.add("dve", lambda e, bk=bk, ft=ft, c=c: e.tensor_scalar(out=ft, in0=bk[:, :], scalar1=colp[:, 71 + c:72 + c], scalar2=None, op0=ALU.mult), reads=[br, "colp"], writes=[fr])
            group_norm_fm(2, [of[0][0], of[1][0]], [of[0][1], of[1][1]], [(Hm[4][:], "H4"), (Hm[5][:], "H5")], (Fm[6][:], "F6"))

            streams.append(P.cap)
            P.cap = []
            genh[0] = gen1[2]
            for i2 in range(2):
                bk, br = genh[0].next()
                for ii in range(2):
                    i = 2 * i2 + ii
                    for k in range(8):
                        P.add("pe", lambda e, bk=bk, ii=ii, i=i, k=k: e.matmul(bk[:, ii * 256:(ii + 1) * 256], lhsT=xnT[:, k, i * 128:(i + 1) * 128], rhs=w_in_sb[:, k, 1376:1632],
                                                                            start=(k == 0), stop=(k == 7)),
                              reads=XNT + WIN, writes=[br])
                for ii in range(2):
                    i = 2 * i2 + ii
                    src = bk[:, ii * 256:(ii + 1) * 256]
                    P.add("act", lambda e, src=src, i=i: e.activation(out=junk[:, 0:256], in_=src, func=AF.Square, accum_out=sm[:, 32 + i:33 + i]), reads=[br], writes=[f"vss{i}"])
                lo = 2 * i2
                P.add("act", lambda e, lo=lo: e.activation(out=sm[:, 36 + lo:38 + lo], in_=sm[:, 32 + lo:34 + lo], func=AF.Ln, scale=1.0 / 256, bias=epsc[:]), reads=[f"vss{lo}", f"vss{lo + 1}", "eps"], writes=[f"vsdp{i2}"])
                P.add("act", lambda e, lo=lo: e.activation(out=sm[:, 36 + lo:38 + lo], in_=sm[:, 36 + lo:38 + lo], func=AF.Exp, scale=-0.5), reads=[f"vsdp{i2}"], writes=[f"vsdp{i2}"])
                for ii in range(2):
                    i = 2 * i2 + ii
                    src = bk[:, ii * 256:(ii + 1) * 256]
                    P.add("dve", lambda e, src=src, i=i: e.tensor_scalar(out=vn[:, i, :], in0=src, scalar1=sm[:, 36 + i:37 + i], scalar2=None, op0=ALU.mult), reads=[br, f"vsdp{i2}"], writes=[f"vn{i}"])
            of = [F(7), F(8)]
            gate = of
            for c in range(2):
                gb, gbr = genh[0].next()
                for i in range(4):
                    for hh in range(2):
                        h = 2 * c + hh
                        P.add("pe", lambda e, gb=gb, i=i, hh=hh, h=h: e.matmul(gb[hh * 64:(hh + 1) * 64, i * 128:(i + 1) * 128], lhsT=vn[:, i, h * 64:(h + 1) * 64], rhs=wsT_sb[:, h, :],
                                                                            start=True, stop=True),
                              reads=[f"vn{i}", "wsT"], writes=[gbr])
                gt, gtr = gate[c]
                bsb = bsT[:, c, :].unsqueeze(1).broadcast_to([128, 4, 128])
                P.add("dve", lambda e, gb=gb, gt=gt, c=c, bsb=bsb: e.scalar_tensor_tensor(out=gt[:].rearrange("p (i t) -> p i t", i=4), in0=gb[:, :].rearrange("p (i t) -> p i t", i=4),
                                                                                        scalar=colp[:, 73 + c:74 + c], in1=bsb, op0=ALU.mult, op1=ALU.add),
                      reads=[gbr, "colp", "bsT"], writes=[gtr])
                ub, ubr = inproj(1120 + c * 128, 1120 + (c + 1) * 128)
                ft, fr = of[c]
                P.add("dve", lambda e, ub=ub, gt=gt, ft=ft: e.tensor_tensor(out=ft[:], in0=ub[:, :], in1=gt[:], op=ALU.mult), reads=[ubr, gtr], writes=[fr])
            group_norm_fm(3, [of[0][0][:], of[1][0][:]], [of[0][1], of[1][1]],
                          [(vn[:, 0:2, :].rearrange("p a b -> p (a b)"), ["vn0", "vn1"]), (vn[:, 2:4, :].rearrange("p a b -> p (a b)"), ["vn2", "vn3"])], (Fm[9][:], "F9"))

            streams.append(P.cap)
            P.cap = None
            genh[0] = gen3
            main_ops.append((att_units, streams))
            P.cap = []
            genh[0] = genpost
            emit_mla_out()
            MX = [f"mixedT{k}" for k in range(8)]
            for i in range(4):
                t = 4 * b + i
                s = i % 2
                P.add("sp", lambda e, t=t, s=s: e.dma_start(out=hr[s][:], in_=hsrc[t * 128:(t + 1) * 128, :]), reads=[f"hd{t}"], writes=[f"hr{s}"], slot=f"hr{s}")
                for n in range(2):
                    bk, br = genh[0].next()
                    mm_group(bk[:, :], br, [(mixedT[:, k, i * 128:(i + 1) * 128], w_out_sb[:, k, n * 512:(n + 1) * 512], MX + WOUT) for k in range(8)])
                    P.add("dve", lambda e, bk=bk, s=s, n=n: e.tensor_tensor(out=hr[s][:, n * 512:(n + 1) * 512], in0=bk[:, :], in1=hr[s][:, n * 512:(n + 1) * 512], op=ALU.add),
                          reads=[br, f"hr{s}"], writes=[f"hr{s}"])
                P.add("sp", lambda e, t=t, s=s: e.dma_start(out=hbuf[t * 128:(t + 1) * 128, :], in_=hr[s][:]), reads=[f"hr{s}"], writes=[f"hd{t}"], slot=f"st{s}")
            post_ops.append(P.cap)
            P.cap = None
        def merge_main(att_units, streams, extra):
            strs = list(streams) + [extra]
            order = [0, 1, 0, 2, 3]
            totY = sum(len(x) for x in strs)
            per = max(3, -(-totY // max(1, len(att_units))))
            posn = [0] * len(strs)
            st = {"rr": 0}
            out = []

            def emit_y(n):
                done = 0
                while done < n and any(posn[i] < len(strs[i]) for i in range(len(strs))):
                    i = order[st["rr"] % len(order)]
                    st["rr"] += 1
                    if posn[i] < len(strs[i]):
                        out.append(strs[i][posn[i]])
                        posn[i] += 1
                        done += 1

            for u in att_units:
                out.extend(u)
                emit_y(per)
            emit_y(10 ** 9)
            return out

        P.replay(prenorm_ops[0])
        P.replay(pre_ops[0])
        for b in range(NB):
            att_units, streams = main_ops[b]
            P.replay(merge_main(att_units, streams, prenorm_ops[b + 1] if b + 1 < NB else []))
            A_ = post_ops[b]
            B_ = pre_ops[b + 1] if b + 1 < NB else []
            ia = ib = 0
            ra = max(1, len(A_))
            rb = max(1, len(B_))
            while ia < len(A_) or ib < len(B_):
                if ib >= len(B_) or (ia < len(A_) and ia * rb <= ib * ra):
                    P.replay([A_[ia]])
                    ia += 1
                else:
                    P.replay([B_[ib]])
                    ib += 1
        P.emit()


POOL_WINDOWS = (2, 4, 8, 16)


def make_consts():
    c = np.zeros((128, NCST), np.float32)
    c[:, 0:128] = np.eye(128, dtype=np.float32)
    k = np.arange(128)[:, None]
    q = np.arange(128)[None, :]
    c[:, 128:256] = (q >= k).astype(np.float32)
    for ch in range(2):
        for p in range(128):
            w = POOL_WINDOWS[2 * ch + p // 64]
            c[p, 288 + ch] = 1.0 / w
            for t in range(16):
                c[p, 256 + ch * 16 + t] = 1.0 / min(t + 1, w)
    inv_freq = (10000.0 ** (-np.arange(0, 32, 2, dtype=np.float32) / 32)).astype(np.float32)
    for p in range(32):
        c[p, 290] = inv_freq[p % 16]
    return c


def col128(v):
    v = np.asarray(v, np.float32)
    return np.ascontiguousarray(v.reshape(-1, 128).T)


def pack_layer_inputs(inp, layers):
    L = len(layers)
    colp = np.zeros((L, 128, NCOL), np.float32)
    rowp = np.zeros((L, NROW), np.float32)
    poolw = np.zeros((L, 128, 2, 128), np.float32)
    wsT = np.zeros((L, 128, 4, 128), np.float32)
    bsT = np.zeros((L, 128, 2, 128), np.float32)
    for li, l in enumerate(layers):
        dw = inp["conv_dw_w"][l]
        for c in range(2):
            colp[li, :, c * 31:(c + 1) * 31] = dw[:, c * 128:(c + 1) * 128].T
        colp[li, :, 62:64] = col128(inp["conv_dw_b"][l])
        colp[li, :, 64:66] = col128(inp["conv_ln_g"][l])
        colp[li, :, 66:68] = col128(inp["conv_ln_b"][l])
        qg = np.zeros(256, np.float32)
        qg[0:192] = inp["mla_q_norm_g"][l]
        colp[li, :, 68:70] = col128(qg)
        colp[li, :, 70:71] = col128(inp["mla_kv_norm_g"][l])
        colp[li, :, 71:73] = col128(inp["pool_scale"][l])
        colp[li, :, 73:75] = col128(inp["gmlp_norm_g"][l])
        colp[li, :, 75:83] = col128(inp["group_norm_g"][l].reshape(-1))
        rowp[li, 0:1024] = inp["mix_norm_g"][l]
        rowp[li, 1024:2048] = inp["ffn_norm_g"][l]
        rowp[li, 2048:2304] = inp["group_norm_g"][l, 1]
        pw_ = inp["pool_w"][l]
        for c in range(2):
            for hh in range(2):
                poolw[li, hh * 64:(hh + 1) * 64, c, hh * 64:(hh + 1) * 64] = pw_[2 * c + hh]
        wsT[li] = np.transpose(inp["gmlp_ws"][l], (2, 0, 1))
        bs = inp["gmlp_bs"][l]
        for c in range(2):
            for hh in range(2):
                bsT[li, hh * 64:(hh + 1) * 64, c, :] = bs[2 * c + hh][None, :]
    sl = list(layers)
    d = dict(
        w_in=np.ascontiguousarray(inp["w_in"][sl]), w_out=np.ascontiguousarray(inp["w_out"][sl]),
        w_uq=np.ascontiguousarray(inp["mla_w_uq"][sl]), w_ukv=np.ascontiguousarray(inp["mla_w_ukv"][sl]),
        conv_pw=np.ascontiguousarray(inp["conv_pw_w"][sl]), poolw=poolw, wsT=wsT, bsT=bsT, colp=colp, rowp=rowp,
        w_ff1=np.ascontiguousarray(inp["w_ff1"][sl]), w_ff2=np.ascontiguousarray(inp["w_ff2"][sl]),
        fng=np.ascontiguousarray(np.asarray(inp["final_norm_g"], np.float32).reshape(1, D)),
        cst=make_consts(),
    )
    return d


_PROGS = {}


def get_prog(T, n_layers, final):
    key = (T, n_layers, final)
    if key not in _PROGS:
        _PROGS[key] = build_program(T, n_layers, final)
    return _PROGS[key]


def run_layers(inp, hs, positions, layers, final, T):
    shared = pack_layer_inputs(inp, layers)
    nc = get_prog(T, len(layers), final)
    in_maps = []
    for ci in range(len(hs)):
        m = dict(shared)
        m["x"] = np.ascontiguousarray(hs[ci], dtype=np.float32)
        m["pos"] = np.ascontiguousarray(positions[ci].reshape(1, T).astype(np.int32))
        in_maps.append(m)
    res = run_bass_kernel_spmd(nc, in_maps, core_ids=list(range(len(hs))))
    return [np.asarray(r["out"]) for r in res.results]


FUSED = True


def kernel(**inputs):
    inp = {k: np.asarray(v) for k, v in inputs.items()}
    x = inp["x"].astype(np.float32)
    B, T, _ = x.shape
    depth = inp["w_in"].shape[0]
    positions = inp["positions"]
    hs = [x[b] for b in range(B)]
    if FUSED:
        outs = run_layers(inp, hs, positions, list(range(depth)), True, T)
    else:
        for l in range(depth):
            hs = run_layers(inp, hs, positions, [l], l == depth - 1, T)
        outs = hs
    return np.stack(outs, axis=0).astype(np.float32)
```

```python
from contextlib import ExitStack
import math
import numpy as np
import concourse.bass as bass
import concourse.mybir as mybir
from concourse.bass_utils import run_bass_kernel_spmd

F32 = mybir.dt.float32
BF16 = mybir.dt.bfloat16
I32 = mybir.dt.int32
ALU = mybir.AluOpType
AF = mybir.ActivationFunctionType

D = 1024
DIN = 1632
DFF = 4096
EPS = 1e-6
NCOL = 83
NROW = 2304
NCST = 291
ENGS = ("pe", "act", "dve", "pool", "sp")


class Res:
    __slots__ = ("name", "w", "rs")

    def __init__(self, name):
        self.name = name
        self.w = None
        self.rs = []


class Slot:
    __slots__ = ("name", "n", "last", "sem")

    def __init__(self, name):
        self.name = name
        self.n = 0
        self.last = None
        self.sem = None


class Op:
    __slots__ = ("eng", "fn", "deps", "sig", "val", "slot")


class Sync:
    def __init__(self, nc, stack):
        self.nc = nc
        self.stack = stack
        self.esem = {e: stack.enter_context(nc.semaphore(f"s_{e}")) for e in ENGS}
        self.ecount = {e: 0 for e in ENGS}
        self.slots = {}


class Prog:
    def __init__(self, nc, name, sync):
        self.nc = nc
        self.name = name
        self.sync = sync
        self.ops = {e: [] for e in ENGS}
        self.slots = sync.slots
        for s in self.slots.values():
            s.last = None
        self.used = []
        self.res = {}
        self.cap = None

    def R(self, name):
        r = self.res.get(name)
        if r is None:
            r = self.res[name] = Res(name)
        return r

    def slot(self, name):
        s = self.slots.get(name)
        if s is None:
            s = self.slots[name] = Slot(name)
        return s

    def add(self, eng, fn, reads=(), writes=(), slot=None):
        if self.cap is not None:
            self.cap.append((eng, fn, list(reads), list(writes), slot))
            return None
        reads = [self.R(r) if isinstance(r, str) else r for r in reads]
        writes = [self.R(r) if isinstance(r, str) else r for r in writes]
        if isinstance(slot, str):
            slot = self.slot(slot)
        op = Op()
        op.eng = eng
        op.fn = fn
        op.sig = slot is not None
        op.val = None
        op.slot = slot
        deps = []
        xr = [r for r in reads if r.name.startswith("bank")]
        xw = [r for r in writes if r.name.startswith("bank")]
        reads = [r for r in reads if not r.name.startswith("bank")]
        writes = [r for r in writes if not r.name.startswith("bank")]
        for r, kind in [(r, "r") for r in xr] + [(r, "w") for r in xw]:
            if r.w is not None:
                pk = r.rs[0] if r.rs else "w"
                if not (r.w.eng == eng and pk == "r" and kind == "r"):
                    deps.append(r.w)
            r.w = op
            r.rs = [kind]
        for r in reads:
            if r.w is not None:
                deps.append(r.w)
        for r in writes:
            if r.w is not None:
                deps.append(r.w)
            deps.extend(r.rs)
        if slot is not None and slot.last is not None:
            deps.append(slot.last)
        for r in reads:
            r.rs.append(op)
        for r in writes:
            r.w = op
            r.rs = []
        if slot is not None:
            slot.last = op
            slot.n += 1
            op.val = 16 * slot.n
            if slot not in self.used:
                self.used.append(slot)
        dd = []
        seen = set()
        for d in deps:
            if d is op or id(d) in seen:
                continue
            seen.add(id(d))
            if d.eng == "pe" and eng == "pe" and d.slot is None and slot is None:
                continue
            d.sig = True
            dd.append(d)
        op.deps = dd
        self.ops[eng].append(op)
        return op

    def replay(self, items):
        for it in items:
            self.add(*it)

    def emit(self):
        nc = self.nc
        sync = self.sync
        for e in ENGS:
            c = sync.ecount[e]
            for op in self.ops[e]:
                if op.slot is None and op.sig:
                    c += 1
                    op.val = c
            sync.ecount[e] = c
        with ExitStack() as st:
            esem = sync.esem
            fin = list(self.used)
            for s in fin:
                if s.sem is None:
                    s.sem = sync.stack.enter_context(nc.semaphore(f"d_{s.name}"))
            block = st.enter_context(nc.Block())

            def run(eng_name, e):
                known = {}
                for op in self.ops[eng_name]:
                    for d in op.deps:
                        sem = d.slot.sem if d.slot is not None else esem[d.eng]
                        k = id(sem)
                        if known.get(k, 0) >= d.val:
                            continue
                        known[k] = d.val
                        e.wait_ge(sem, d.val)
                    ins = op.fn(e)
                    if op.slot is not None:
                        ins.then_inc(op.slot.sem, 16)
                    elif op.sig:
                        ins.then_inc(esem[eng_name], 1)
                if eng_name == "sp":
                    for s in fin:
                        e.wait_ge(s.sem, 16 * s.n)

            block.tensor(lambda e: run("pe", e))
            block.scalar(lambda e: run("act", e))
            block.vector(lambda e: run("dve", e))
            block.gpsimd(lambda e: run("pool", e))
            block.sync(lambda e: run("sp", e))


class Rot:
    def __init__(self, items):
        self.items = items
        self.i = 0

    def next(self):
        it = self.items[self.i % len(self.items)]
        self.i += 1
        return it


def build_program(T, n_layers, final):
    NB = T // 512
    NT = T // 128
    nc = bass.Bass("TRN2", target_bir_lowering=False)

    def din(name, shape, dt=F32):
        return nc.dram_tensor(name, list(shape), dt, kind="ExternalInput").ap()

    L = n_layers
    x = din("x", [T, D])
    pos = din("pos", [1, T], I32)
    w_in = din("w_in", [L, D, DIN])
    w_out = din("w_out", [L, D, D])
    w_uq = din("w_uq", [L, 192, 384])
    w_ukv = din("w_ukv", [L, 128, 512])
    conv_pw = din("conv_pw", [L, 256, 256])
    poolw = din("poolw", [L, 128, 2, 128])
    wsT = din("wsT", [L, 128, 4, 128])
    bsT = din("bsT", [L, 128, 2, 128])
    colp = din("colp", [L, 128, NCOL])
    rowp = din("rowp", [L, NROW])
    w_ff1 = din("w_ff1", [L, D, DFF])
    w_ff2 = din("w_ff2", [L, DFF, D])
    fng = din("fng", [1, D])
    cst = din("cst", [128, NCST])
    out = nc.dram_tensor("out", [T, D], F32, kind="ExternalOutput").ap()
    hbuf = nc.dram_tensor("hbuf", [T, D], F32).ap()
    ropetab = nc.dram_tensor("ropetab", [2, 32, T], F32).ap()

    with ExitStack() as top:
        banks = [top.enter_context(nc.psum_tensor(f"bank{i}", [128, 512], F32)) for i in range(8)]
        sync = Sync(nc, top)
        w_in_sb = top.enter_context(nc.sbuf_tensor("w_in_sb", [128, 8, DIN], BF16))

        with ExitStack() as st:
            P = Prog(nc, "pr", sync)
            sb = lambda n, s, d: st.enter_context(nc.sbuf_tensor(n, s, d))
            TQ = T // 4
            posi = sb("posi", [128, TQ], I32)
            posf = sb("posf", [128, TQ], F32)
            ang = sb("ang", [128, TQ], F32)
            u = sb("u", [128, TQ], F32)
            ki = sb("ki", [128, TQ], I32)
            kf = sb("kf", [128, TQ], F32)
            ng = sb("ng", [128, TQ], F32)
            tab = [sb("tab0", [128, TQ], F32), sb("tab1", [128, TQ], F32)]
            c_sb = sb("c_sb", [128, NCST], F32)
            nbias = sb("nbias", [128, 1], F32)
            wv0 = w_in[0].rearrange("(k p) n -> p k n", p=128)
            for g in range(4):
                P.add("pool", lambda e, g=g: e.dma_start(out=w_in_sb[:, 2 * g:2 * g + 2, :], in_=wv0[:, 2 * g:2 * g + 2, :]), writes=[f"w_in{g}"], slot=f"wl{g}")
            for g in range(4):
                P.add("sp", lambda e, g=g: e.dma_start(out=posi[32 * g:32 * g + 32, :], in_=pos[0, g * TQ:(g + 1) * TQ].partition_broadcast(32)), writes=["posi"], slot=f"posi{g}")
            P.add("sp", lambda e: e.dma_start(out=c_sb[:], in_=cst), writes=["c"], slot="c")
            P.add("dve", lambda e: e.tensor_copy(out=posf[:], in_=posi[:]), reads=["posi"], writes=["posf"])
            P.add("dve", lambda e: e.memset(nbias[:], -math.pi * (1 - 1e-6)), writes=["nbias"])
            P.add("dve", lambda e: e.tensor_scalar(out=ang[:], in0=posf[:], scalar1=c_sb[:, 290:291], scalar2=None, op0=ALU.mult),
                  reads=["posf", "c"], writes=["ang"])
            for i, shift in enumerate((0.75, 0.5)):
                P.add("dve", lambda e, shift=shift: e.tensor_scalar(out=u[:], in0=ang[:], scalar1=1.0 / (2 * math.pi), scalar2=shift, op0=ALU.mult, op1=ALU.add),
                      reads=["ang"], writes=["u"])
                P.add("dve", lambda e: e.tensor_copy(out=ki[:], in_=u[:]), reads=["u"], writes=["ki"])
                P.add("dve", lambda e: e.tensor_copy(out=kf[:], in_=ki[:]), reads=["ki"], writes=["kf"])
                P.add("dve", lambda e: e.tensor_tensor(out=u[:], in0=u[:], in1=kf[:], op=ALU.subtract), reads=["u", "kf"], writes=["u"])
                P.add("dve", lambda e: e.tensor_scalar(out=ng[:], in0=u[:], scalar1=0.0, scalar2=None, op0=ALU.is_lt), reads=["u"], writes=["ng"])
                P.add("dve", lambda e: e.tensor_tensor(out=u[:], in0=u[:], in1=ng[:], op=ALU.add), reads=["u", "ng"], writes=["u"])
                P.add("act", lambda e, i=i: e.activation(out=tab[i][:], in_=u[:], func=AF.Sin, scale=2 * math.pi * (1 - 1e-6), bias=nbias[:]),
                      reads=["u", "nbias"], writes=[f"tab{i}"])
                for g in range(4):
                    P.add("sp", lambda e, i=i, g=g: e.dma_start(out=ropetab[i, :, g * TQ:(g + 1) * TQ], in_=tab[i][32 * g:32 * g + 32, :]), reads=[f"tab{i}"], slot=f"tabo{i}{g}")
            P.emit()

        for l in range(n_layers):
            hsrc = x if l == 0 else hbuf
            if True:
              emit_pass_a(nc, sync, l, T, NB, NT, banks, hsrc, hbuf, ropetab, pos, w_in_sb,
                        dict(w_in=w_in, w_out=w_out, w_uq=w_uq, w_ukv=w_ukv, conv_pw=conv_pw, poolw=poolw,
                             wsT=wsT, bsT=bsT, colp=colp, rowp=rowp, cst=cst))
            is_last = (l == n_layers - 1)
            if True:
              emit_pass_b(nc, sync, l, T, NT, banks, hbuf, out, dict(w_ff1=w_ff1, w_ff2=w_ff2, rowp=rowp, fng=fng, cst=cst, w_in=w_in, w_in_sb=w_in_sb, n_layers=n_layers),
                        do_final=(is_last and final), to_out=is_last)
    return nc


def emit_pass_b(nc, sync, l, T, NT, banks, hbuf, out, W, do_final, to_out):
    NBB = T // 256
    with ExitStack() as st:
        P = Prog(nc, f"b{l}", sync)
        sb = lambda n, s, d: st.enter_context(nc.sbuf_tensor(f"B{l}_{n}", s, d))
        W1 = sb("W1", [128, 8, DFF], BF16)
        W2 = sb("W2", [128, 32, D], BF16)
        gff = sb("gff", [128, D], F32)
        gfin = sb("gfin", [128, D], F32)
        c_sb = sb("c_sb", [128, 128], F32)
        ident = sb("ident", [128, 128], BF16)
        hn = [sb(f"hn{i}", [128, D], F32) for i in range(4)]
        xn = [sb(f"xn{i}", [128, D], BF16) for i in range(4)]
        xnT = [sb(f"xnT{i}", [128, 8, 256], BF16) for i in range(2)]
        rr = [sb(f"r{i}", [128, 256], F32) for i in range(3)]
        fT = [sb(f"fT{i}", [128, 256], BF16) for i in range(3)]
        junk = sb("junk", [128, D], BF16)
        ss = [sb(f"ss{i}", [128, 1], F32) for i in range(4)]
        rs = [sb(f"rs{i}", [128, 1], F32) for i in range(4)]
        ss2 = [sb(f"ss2{i}", [128, 1], F32) for i in range(4)]
        rs2 = [sb(f"rs2{i}", [128, 1], F32) for i in range(4)]
        epsc = sb("epsc", [128, 1], F32)

        w1v = W["w_ff1"][l].rearrange("(k p) n -> p k n", p=128)
        w2v = W["w_ff2"][l].rearrange("(c p) n -> p c n", p=128)
        P.add("sp", lambda e: e.dma_start(out=c_sb[:], in_=W["cst"][:, 0:128]), writes=["c"], slot="c")
        P.add("sp", lambda e: e.dma_start(out=gff[:], in_=W["rowp"][l, 1024:2048].partition_broadcast(128)), writes=["gff"], slot="gff")
        if do_final:
            P.add("sp", lambda e: e.dma_start(out=gfin[:], in_=W["fng"][0, :].partition_broadcast(128)), writes=["gfin"], slot="gfin")
        P.add("dve", lambda e: e.tensor_copy(out=ident[:], in_=c_sb[:]), reads=["c"], writes=["ident"])
        P.add("dve", lambda e: e.memset(epsc[:], EPS), writes=["eps"])
        for g in range(8):
            P.add("pool", lambda e, g=g: e.dma_start(out=W1[:, :, g * 512:(g + 1) * 512], in_=w1v[:, :, g * 512:(g + 1) * 512]),
                  writes=[f"W1g{g}"], slot=f"w1_{g % 4}")
            P.add("pool", lambda e, g=g: e.dma_start(out=W2[:, 4 * g:4 * g + 4, :], in_=w2v[:, 4 * g:4 * g + 4, :]),
                  writes=[f"W2g{g}"], slot=f"w2_{g % 4}")

        trb7 = banks[7][:, :].bitcast(BF16)

        def prep_load(bb):
            for i in range(2):
                t = 2 * bb + i
                s = (bb % 2) * 2 + i
                P.add("sp", lambda e, t=t, s=s: e.dma_start(out=hn[s][:], in_=hbuf[t * 128:(t + 1) * 128, :]),
                      reads=[f"hd{t}"], writes=[f"hn{s}"], slot=f"hn{s}")
                P.add("act", lambda e, s=s: e.activation(out=junk[:], in_=hn[s][:], func=AF.Square, accum_out=ss[s][:]),
                      reads=[f"hn{s}"], writes=[f"ss{s}"])
                P.add("act", lambda e, s=s: e.activation(out=rs[s][:], in_=ss[s][:], func=AF.Ln, scale=1.0 / D, bias=epsc[:]),
                      reads=[f"ss{s}", "eps"], writes=[f"sd{s}"])
                P.add("act", lambda e, s=s: e.activation(out=rs[s][:], in_=rs[s][:], func=AF.Exp, scale=-0.5), reads=[f"sd{s}"], writes=[f"sd{s}"])
                P.add("dve", lambda e, s=s: e.scalar_tensor_tensor(out=xn[s][:], in0=hn[s][:], scalar=rs[s][:], in1=gff[:], op0=ALU.mult, op1=ALU.mult),
                      reads=[f"hn{s}", f"sd{s}", "gff"], writes=[f"xn{s}"])

        def prep_tr(bb):
            xt = xnT[bb % 2]
            for hb in range(2):
                for i in range(2):
                    s = (bb % 2) * 2 + i
                    for kk in range(4):
                        k = hb * 4 + kk
                        off = kk * 256 + i * 128
                        P.add("pe", lambda e, s=s, k=k, off=off: e.transpose(out=trb7[:, off:off + 128], in_=xn[s][:, k * 128:(k + 1) * 128], identity=ident[:]),
                              reads=[f"xn{s}", "ident"], writes=["bank7"])
                dst = xt[:, hb * 4:hb * 4 + 4, :].rearrange("p k t -> p (k t)")
                if hb == 0:
                    P.add("dve", lambda e, dst=dst: e.tensor_copy(out=dst, in_=trb7[:, :]), reads=["bank7"], writes=[f"xnT{bb % 2}a"])
                else:
                    P.add("act", lambda e, dst=dst: e.activation(out=dst, in_=trb7[:, :], func=AF.Copy), reads=["bank7"], writes=[f"xnT{bb % 2}b"])

        def ff1(bb, c):
            xt = xnT[bb % 2]
            q = c % 3
            pb = banks[4 + q][:, 0:256]
            for k in range(8):
                P.add("pe", lambda e, k=k, pb=pb, xt=xt, c=c: e.matmul(pb, lhsT=W1[:, k, c * 128:(c + 1) * 128], rhs=xt[:, k, :], start=(k == 0), stop=(k == 7)),
                      reads=[f"W1g{c // 4}", f"xnT{bb % 2}a", f"xnT{bb % 2}b"], writes=[f"bank{4 + q}"])
            j = c % 3
            P.add("act", lambda e, pb=pb, j=j: e.activation(out=rr[j][:], in_=pb, func=AF.Relu), reads=[f"bank{4 + q}"], writes=[f"r{j}"])
            P.add("pool", lambda e, j=j: e.tensor_tensor(out=fT[j][:], in0=rr[j][:], in1=rr[j][:], op=ALU.mult), reads=[f"r{j}"], writes=[f"fT{j}"])

        def ff2(bb, c):
            j = c % 3
            for i in range(2):
                for n in range(2):
                    P.add("pe", lambda e, i=i, n=n, j=j, c=c: e.matmul(banks[i * 2 + n][:, :], lhsT=fT[j][:, i * 128:(i + 1) * 128], rhs=W2[:, c, n * 512:(n + 1) * 512],
                                                                 start=(c == 0), stop=(c == 31)),
                          reads=[f"fT{j}", f"W2g{c // 4}"], writes=[f"bank{i * 2 + n}"])

        def epilogue(bb):
            for i in range(2):
                t = 2 * bb + i
                s = (bb % 2) * 2 + i
                for n in range(2):
                    P.add("dve", lambda e, i=i, n=n, s=s: e.tensor_tensor(out=hn[s][:, n * 512:(n + 1) * 512], in0=banks[i * 2 + n][:, :], in1=hn[s][:, n * 512:(n + 1) * 512], op=ALU.add),
                          reads=[f"bank{i * 2 + n}", f"hn{s}"], writes=[f"hn{s}"])
                if do_final:
                    P.add("act", lambda e, s=s: e.activation(out=junk[:], in_=hn[s][:], func=AF.Square, accum_out=ss2[s][:]),
                          reads=[f"hn{s}"], writes=[f"ss2{s}"])
                    P.add("act", lambda e, s=s: e.activation(out=rs2[s][:], in_=ss2[s][:], func=AF.Ln, scale=1.0 / D, bias=epsc[:]),
                          reads=[f"ss2{s}", "eps"], writes=[f"sd2{s}"])
                    P.add("act", lambda e, s=s: e.activation(out=rs2[s][:], in_=rs2[s][:], func=AF.Exp, scale=-0.5), reads=[f"sd2{s}"], writes=[f"sd2{s}"])
                    P.add("dve", lambda e, s=s: e.scalar_tensor_tensor(out=hn[s][:], in0=hn[s][:], scalar=rs2[s][:], in1=gfin[:], op0=ALU.mult, op1=ALU.mult),
                          reads=[f"hn{s}", f"sd2{s}", "gfin"], writes=[f"hn{s}"])
                dst = out if to_out else hbuf
                P.add("sp", lambda e, t=t, s=s, dst=dst: e.dma_start(out=dst[t * 128:(t + 1) * 128, :], in_=hn[s][:]),
                      reads=[f"hn{s}"], writes=[f"hd{t}"], slot=f"st{s}")

        prep_load(0)
        prep_tr(0)
        for bb in range(NBB):
            if bb == NBB // 2 and l + 1 < W["n_layers"]:
                wvn = W["w_in"][l + 1].rearrange("(k p) n -> p k n", p=128)
                for g in range(4):
                    P.add("pool", lambda e, g=g: e.dma_start(out=W["w_in_sb"][:, 2 * g:2 * g + 2, :], in_=wvn[:, 2 * g:2 * g + 2, :]), writes=[f"w_in{g}"], slot=f"wl{g}")
            for c in range(32):
                ff1(bb, c)
                if c > 0:
                    ff2(bb, c - 1)
                if c == 4 and bb + 1 < NBB:
                    prep_load(bb + 1)
                if c == 20 and bb + 1 < NBB:
                    prep_tr(bb + 1)
            ff2(bb, 31)
            epilogue(bb)
        P.emit()


def emit_pass_a(nc, sync, l, T, NB, NT, banks, hsrc, hbuf, ropetab, pos, w_in_sb, W):
    with ExitStack() as st:
        P = Prog(nc, f"a{l}", sync)
        sb = lambda n, s, d: st.enter_context(nc.sbuf_tensor(f"A{l}_{n}", s, d))
        w_krr = sb("w_krr", [128, 8, 128], BF16)
        w_out_sb = sb("w_out", [128, 8, D], BF16)
        w_uq_sb = sb("w_uq", [128, 2, 416], BF16)
        w_uqr = sb("w_uqr", [128, 2, 416], BF16)
        w_ukv_sb = sb("w_ukv", [128, 512], BF16)
        pw_sb = sb("pw", [128, 2, 256], BF16)
        w_kn = sb("w_kn", [128, 256], BF16)
        w_v = sb("w_v", [128, 256], BF16)
        poolw_sb = sb("poolw", [128, 2, 128], BF16)
        wsT_sb = sb("wsT", [128, 4, 128], BF16)
        dg = sb("dg", [128, 8, 128], BF16)
        colp = sb("colp", [128, NCOL], F32)
        gmix = sb("gmix", [128, D], F32)
        ggm = sb("ggm", [128, 256], F32)
        bsT = sb("bsT", [128, 2, 128], F32)
        c_sb = sb("c_sb", [128, NCST], F32)
        ident = sb("ident", [128, 128], BF16)
        maskT = sb("maskT", [128, 128], BF16)
        ones = sb("ones", [128, 128], BF16)
        epsc = sb("epsc", [128, 1], F32)
        onec = sb("onec", [128, 1], F32)
        KT = sb("KT", [128, 4, T], BF16)
        Vt = sb("Vt", [128, NT, 4, 65], BF16)
        ycv = sb("ycv", [128, 2, 542], BF16)
        zp = sb("zp", [128, 2, 527], F32)
        hn = [sb(f"hn{i}", [128, D], F32) for i in range(2)]
        xn = [sb(f"xn{i}", [128, D], BF16) for i in range(4)]
        hr = [sb(f"hr{i}", [128, D], F32) for i in range(2)]
        xnT = sb("xnT", [128, 8, 512], BF16)
        mixedT = sb("mixedT", [128, 8, 512], BF16)
        QT = sb("QT", [128, 4, 512], BF16)
        PT = [sb(f"PT{i}", [128, 512], BF16) for i in range(3)]
        cos2 = sb("cos2", [128, 512], F32)
        sin2 = sb("sin2", [128, 512], F32)
        junk = sb("junk", [128, 256], BF16)
        Fm = [sb(f"F{i}", [128, 512], F32) for i in range(10)]
        Hm = [sb(f"H{i}", [128, 512], BF16) for i in range(6)]
        PA = sb("PA", [128, 527], F32)
        PB = sb("PB", [128, 527], F32)
        omla = sb("omla", [128, 4, 256], F32)
        vn = sb("vn", [128, 4, 256], BF16)
        mtok = [sb(f"mtok{i}", [128, 256], BF16) for i in range(2)]
        sm = sb("sm", [128, 64], F32)

        def F(i):
            return Fm[i], f"F{i}"

        def H(i):
            return Hm[i], f"H{i}"

        P.add("sp", lambda e: e.dma_start(out=c_sb[:], in_=W["cst"]), writes=["c"], slot="c")
        P.add("sp", lambda e: e.dma_start(out=colp[:], in_=W["colp"][l]), writes=["colp"], slot="colp")
        P.add("sp", lambda e: e.dma_start(out=gmix[:], in_=W["rowp"][l, 0:1024].partition_broadcast(128)), writes=["gmix"], slot="gmix")
        P.add("sp", lambda e: e.dma_start(out=ggm[:], in_=W["rowp"][l, 2048:2304].partition_broadcast(128)), writes=["ggm"], slot="ggm")
        P.add("sp", lambda e: e.dma_start(out=bsT[:], in_=W["bsT"][l]), writes=["bsT"], slot="bsT")
        P.add("dve", lambda e: e.tensor_copy(out=ident[:], in_=c_sb[:, 0:128]), reads=["c"], writes=["ident"])
        P.add("dve", lambda e: e.tensor_copy(out=maskT[:], in_=c_sb[:, 128:256]), reads=["c"], writes=["maskT"])
        P.add("dve", lambda e: e.memset(ones[:], 1.0), writes=["ones"])
        P.add("dve", lambda e: e.memset(epsc[:], EPS), writes=["eps"])
        P.add("dve", lambda e: e.memset(onec[:], 1.0), writes=["eps"])
        P.add("pool", lambda e: e.memset(KT[96:128, :, :], 0.0), writes=["KTpad"])
        P.add("pool", lambda e: e.memset(QT[96:128, :, :], 0.0), writes=["QTpad"])
        P.add("pool", lambda e: e.memset(w_uq_sb[:, :, 384:416], 0.0), writes=["w_uq"])
        P.add("pool", lambda e: e.memset(w_uq_sb[64:128, 1, :], 0.0), writes=["w_uq"])
        P.add("pool", lambda e: e.memset(w_uqr[:], 0.0), writes=["w_uqr"])
        P.add("pool", lambda e: e.memset(ycv[:], 0.0), writes=["ycvh0", "ycvh1", "ycv0", "ycv1"])
        P.add("pool", lambda e: e.memset(zp[:], 0.0), writes=["zph0", "zph1", "zp0", "zp1"])
        P.add("pool", lambda e: e.memset(Vt[:], 1.0), writes=["Vinit"])
        P.add("pool", lambda e: e.memset(w_krr[:], 0.0), writes=["w_krr"])
        wv = W["w_in"][l].rearrange("(k p) n -> p k n", p=128)
        for g in range(0):
            P.add("pool", lambda e, g=g: e.dma_start(out=w_in_sb[:, 2 * g:2 * g + 2, :], in_=wv[:, 2 * g:2 * g + 2, :]), writes=[f"w_in{g}"], slot=f"wl{g}")
        P.add("pool", lambda e: e.dma_start(out=w_uq_sb[:, 0, 0:384], in_=W["w_uq"][l, 0:128, :]), writes=["w_uq"], slot="wl0")
        P.add("pool", lambda e: e.dma_start(out=w_uq_sb[0:64, 1, 0:384], in_=W["w_uq"][l, 128:192, :]), writes=["w_uq"], slot="wl1")
        P.add("pool", lambda e: e.dma_start(out=w_ukv_sb[:], in_=W["w_ukv"][l]), writes=["w_ukv"], slot="wl2")
        P.add("pool", lambda e: e.dma_start(out=pw_sb[:], in_=W["conv_pw"][l].rearrange("(k p) n -> p k n", p=128)), writes=["pw"], slot="wl3")
        P.add("pool", lambda e: e.dma_start(out=poolw_sb[:], in_=W["poolw"][l]), writes=["poolw"], slot="wl0")
        P.add("pool", lambda e: e.dma_start(out=wsT_sb[:], in_=W["wsT"][l]), writes=["wsT"], slot="wl1")
        wov = W["w_out"][l].rearrange("(k p) n -> p k n", p=128)
        for g in range(2):
            P.add("pool", lambda e, g=g: e.dma_start(out=w_out_sb[:, 4 * g:4 * g + 4, :], in_=wov[:, 4 * g:4 * g + 4, :]), writes=[f"w_out{g}"], slot=f"wl{2 + g}")
        WIN = [f"w_in{g}" for g in range(4)]
        WOUT = ["w_out0", "w_out1"]
        ukv4 = w_ukv_sb[:, :].rearrange("p (h d) -> p h d", h=4)
        P.add("act", lambda e: e.activation(out=w_kn[:, :].rearrange("p (h d) -> p h d", h=4), in_=ukv4[:, :, 0:64], func=AF.Copy), reads=["w_ukv"], writes=["w_kn"])
        P.add("act", lambda e: e.activation(out=w_v[:, :].rearrange("p (h d) -> p h d", h=4), in_=ukv4[:, :, 64:128], func=AF.Copy), reads=["w_ukv"], writes=["w_v"])
        for h in range(4):
            P.add("dve", lambda e, h=h: e.tensor_tensor(out=wsT_sb[:, h, :], in0=wsT_sb[:, h, :], in1=maskT[:], op=ALU.mult),
                  reads=["wsT", "maskT"], writes=["wsT"])
        for kc in range(2):
            pr = slice(0, 128) if kc == 0 else slice(0, 64)
            src = w_uq_sb[pr, kc, 0:384].rearrange("p (h d) -> p h d", h=4)
            dst = w_uqr[pr, kc, 0:384].rearrange("p (h d) -> p h d", h=4)
            P.add("act", lambda e, src=src, dst=dst: e.mul(out=dst[:, :, 64:80], in_=src[:, :, 80:96], mul=-1.0), reads=["w_uq", "w_uqr"], writes=["w_uqr"])
            P.add("act", lambda e, src=src, dst=dst: e.activation(out=dst[:, :, 80:96], in_=src[:, :, 64:80], func=AF.Copy), reads=["w_uq", "w_uqr"], writes=["w_uqr"])
            P.add("act", lambda e, src=src, dst=dst: e.activation(out=dst[:, :, 0:64], in_=src[:, :, 0:64], func=AF.Copy), reads=["w_uq", "w_uqr"], writes=["w_uqr"])
        P.add("act", lambda e: e.mul(out=w_krr[:, :, 64:80], in_=w_in_sb[:, :, 848:864], mul=-1.0), reads=WIN + ["w_krr"], writes=["w_krr"])
        P.add("act", lambda e: e.activation(out=w_krr[:, :, 80:96], in_=w_in_sb[:, :, 832:848], func=AF.Copy), reads=WIN + ["w_krr"], writes=["w_krr"])
        gen3 = Rot([(banks[i], f"bank{i}") for i in range(3)])
        gen1 = [Rot([(banks[i], f"bank{i}")]) for i in range(3)]
        genh = [gen3]
        stb = Rot([(banks[i], f"bank{i}") for i in range(3, 6)])
        pvb = Rot([(banks[i], f"bank{i}") for i in range(6, 8)])
        ptr = Rot([(PT[i], f"PT{i}") for i in range(3)])
        SCALE = 96.0 ** -0.5

        def mm_group(pout, pres, parts, extra_reads=()):
            n = len(parts)
            for i, (lt, rh, rd) in enumerate(parts):
                P.add("pe", lambda e, lt=lt, rh=rh, i=i: e.matmul(pout, lhsT=lt, rhs=rh, start=(i == 0), stop=(i == n - 1)),
                      reads=list(rd) + list(extra_reads), writes=[pres])

        def rstd_bcast(sq_parts, scale, dst, dres):
            bk, br = genh[0].next()
            mm_group(bk[:, :], br, [(ones[0:sq.shape[0], :], sq, rd + ["ones"]) for sq, rd in sq_parts])
            P.add("act", lambda e: e.activation(out=dst, in_=bk[:, :], func=AF.Ln, scale=scale, bias=epsc[:]), reads=[br, "eps"], writes=[dres])
            P.add("act", lambda e: e.activation(out=dst, in_=dst, func=AF.Exp, scale=-0.5), reads=[dres], writes=[dres])

        def group_norm_fm(g, of, ofres, sqt, rtt):
            sqs = []
            for c in range(2):
                hq, hres = sqt[c]
                hres = hres if isinstance(hres, list) else [hres]
                P.add("pool", lambda e, c=c, hq=hq: e.tensor_tensor(out=hq, in0=of[c], in1=of[c], op=ALU.mult), reads=[ofres[c]], writes=hres)
                sqs.append((hq, hres))
            rt, rres = rtt
            rstd_bcast(sqs, 1.0 / 256, rt, rres)
            for c in range(2):
                P.add("dve", lambda e, c=c: e.scalar_tensor_tensor(out=mixedT[:, 2 * g + c, :], in0=of[c], scalar=colp[:, 75 + 2 * g + c:76 + 2 * g + c], in1=rt,
                                                                  op0=ALU.mult, op1=ALU.mult),
                      reads=[ofres[c], rres, "colp"], writes=[f"mixedT{2 * g + c}"])

        genpre = Rot([(banks[i], f"bank{i}") for i in (1, 2, 3, 4)])
        genpost = Rot([(banks[i], f"bank{i}") for i in (0, 5)])
        pre_ops, main_ops, post_ops, prenorm_ops = [], [], [], []
        for b in range(NB):
            tb = b * 512
            P.cap = []
            genh[0] = genpre
            P.add("sp", lambda e, tb=tb: e.dma_start(out=cos2[64:96, :], in_=ropetab[0, :, tb:tb + 512]), writes=["cos2"], slot="cos2")
            P.add("sp", lambda e, tb=tb: e.dma_start(out=sin2[64:96, :], in_=ropetab[1, :, tb:tb + 512]), writes=["sin2"], slot="sin2")
            prenorm_cap = P.cap
            P.cap = []
            for ip in range(2):
                for i in (2 * ip, 2 * ip + 1):
                    t = 4 * b + i
                    s = i % 2
                    P.add("sp", lambda e, t=t, s=s: e.dma_start(out=hn[s][:], in_=hsrc[t * 128:(t + 1) * 128, :]), reads=[f"hd{t}"], writes=[f"hn{s}"], slot=f"hn{s}")
                    P.add("act", lambda e, s=s, i=i: e.activation(out=xn[i][:], in_=hn[s][:], func=AF.Square, accum_out=sm[:, 40 + i:41 + i]), reads=[f"hn{s}"], writes=[f"ss{i}", f"xn{i}"])
                lo = 2 * ip
                P.add("act", lambda e, lo=lo: e.activation(out=sm[:, 44 + lo:46 + lo], in_=sm[:, 40 + lo:42 + lo], func=AF.Ln, scale=1.0 / D, bias=epsc[:]),
                      reads=[f"ss{lo}", f"ss{lo + 1}", "eps"], writes=[f"sdp{ip}"])
                P.add("act", lambda e, lo=lo: e.activation(out=sm[:, 44 + lo:46 + lo], in_=sm[:, 44 + lo:46 + lo], func=AF.Exp, scale=-0.5), reads=[f"sdp{ip}"], writes=[f"sdp{ip}"])
                for i in (2 * ip, 2 * ip + 1):
                    s = i % 2
                    P.add("dve", lambda e, s=s, i=i: e.scalar_tensor_tensor(out=xn[i][:], in0=hn[s][:], scalar=sm[:, 44 + i:45 + i], in1=gmix[:], op0=ALU.mult, op1=ALU.mult),
                          reads=[f"hn{s}", f"sdp{ip}", "gmix"], writes=[f"xn{i}"])
            prenorm_ops.append(P.cap)
            P.cap = prenorm_cap
            for i in range(4):
                for hb in range(2):
                    bk, br = genh[0].next()
                    bv = bk[:, :].bitcast(BF16)
                    for kk in range(4):
                        k = hb * 4 + kk
                        P.add("pe", lambda e, i=i, k=k, kk=kk, bv=bv: e.transpose(out=bv[:, kk * 128:(kk + 1) * 128], in_=xn[i][:, k * 128:(k + 1) * 128], identity=ident[:]),
                              reads=[f"xn{i}", "ident"], writes=[br])
                    eng = "dve" if hb == 0 else "act"
                    dst = xnT[:, hb * 4:hb * 4 + 4, i * 128:(i + 1) * 128]
                    src = bv[:, 0:512].rearrange("p (k t) -> p k t", k=4)
                    if eng == "dve":
                        P.add("dve", lambda e, dst=dst, src=src: e.tensor_copy(out=dst, in_=src), reads=[br], writes=[f"xnT{i}"])
                    else:
                        P.add("act", lambda e, dst=dst, src=src: e.activation(out=dst, in_=src, func=AF.Copy), reads=[br], writes=[f"xnT{i}"])
            XNT = [f"xnT{i}" for i in range(4)]

            def inproj(c0, c1, lw=None):
                bk, br = genh[0].next()
                m = c1 - c0
                parts = []
                for k in range(8):
                    lt = (w_in_sb[:, k, c0:c1] if lw is None else lw[:, k, :])
                    parts.append((lt, xnT[:, k, :], XNT + WIN + (["w_krr"] if lw is not None else [])))
                mm_group(bk[0:m, :], br, parts)
                return bk, br

            craw = [F(0), F(1), F(2)]
            csq = [H(0), H(1), H(2)]
            spans = [(512, 640), (640, 704), (704, 832)]
            for j, (c0, c1) in enumerate(spans):
                m = c1 - c0
                bk, br = inproj(c0, c1)
                ft, fr = craw[j]
                P.add("act", lambda e, bk=bk, ft=ft, m=m: e.activation(out=ft[0:m, :], in_=bk[0:m, :], func=AF.Copy), reads=[br], writes=[fr])
                ht, hres = csq[j]
                P.add("pool", lambda e, ft=ft, ht=ht, m=m: e.tensor_tensor(out=ht[0:m, :], in0=ft[0:m, :], in1=ft[0:m, :], op=ALU.mult), reads=[fr], writes=[hres])
            rq, rqres = F(3)
            rstd_bcast([(csq[0][0][:, :], [csq[0][1]]), (csq[1][0][0:64, :], [csq[1][1]])], 1.0 / 192, rq[:], rqres)
            rkv, rkvres = F(4)
            rstd_bcast([(csq[2][0][:, :], [csq[2][1]])], 1.0 / 128, rkv[:], rkvres)
            cqn = [H(3), H(4), H(5)]
            gcols = [68, 69, 70]
            for j in range(3):
                m = spans[j][1] - spans[j][0]
                ft, fr = craw[j]
                ht, hres = cqn[j]
                rt, rres = (rq, rqres) if j < 2 else (rkv, rkvres)
                P.add("dve", lambda e, ft=ft, ht=ht, rt=rt, m=m, gc=gcols[j]: e.scalar_tensor_tensor(out=ht[0:m, :], in0=ft[0:m, :], scalar=colp[0:m, gc:gc + 1], in1=rt[0:m, :],
                                                                                                   op0=ALU.mult, op1=ALU.mult),
                      reads=[fr, rres, "colp"], writes=[hres])
            for h in range(4):
                pa, par = genh[0].next()
                mm_group(pa[:, :], par, [(w_uq_sb[:, 0, h * 96:h * 96 + 128], cqn[0][0][:, :], [cqn[0][1], "w_uq"]),
                                         (w_uq_sb[0:64, 1, h * 96:h * 96 + 128], cqn[1][0][0:64, :], [cqn[1][1], "w_uq"])])
                pbk, pbr = genh[0].next()
                mm_group(pbk[:, :], pbr, [(w_uqr[:, 0, h * 96:h * 96 + 128], cqn[0][0][:, :], [cqn[0][1], "w_uqr"]),
                                          (w_uqr[0:64, 1, h * 96:h * 96 + 128], cqn[1][0][0:64, :], [cqn[1][1], "w_uqr"])])
                P.add("act", lambda e, pa=pa, h=h: e.activation(out=QT[0:64, h, :], in_=pa[0:64, :], func=AF.Copy), reads=[par], writes=[f"QT{h}"])
                t1, t1r = F(5)
                t2, t2r = F(6)
                P.add("dve", lambda e, pa=pa, t1=t1: e.tensor_tensor(out=t1[64:96, :], in0=pa[64:96, :], in1=cos2[64:96, :], op=ALU.mult), reads=[par, "cos2"], writes=[t1r])
                P.add("dve", lambda e, pbk=pbk, t2=t2: e.tensor_tensor(out=t2[64:96, :], in0=pbk[64:96, :], in1=sin2[64:96, :], op=ALU.mult), reads=[pbr, "sin2"], writes=[t2r])
                P.add("pool", lambda e, t1=t1, t2=t2, h=h: e.tensor_tensor(out=QT[64:96, h, :], in0=t1[64:96, :], in1=t2[64:96, :], op=ALU.add), reads=[t1r, t2r], writes=[f"QT{h}r"])
            for hp in range(2):
                bk, br = genh[0].next()
                lt = w_kn[:, hp * 128:(hp + 1) * 128]
                mm_group(bk[:, :], br, [(lt, cqn[2][0][:, :], [cqn[2][1], "w_kn"])])
                P.add("act", lambda e, bk=bk, hp=hp, tb=tb: e.activation(out=KT[0:64, 2 * hp, tb:tb + 512], in_=bk[0:64, :], func=AF.Copy), reads=[br], writes=[f"KT{b}_{2 * hp}"])
                P.add("dve", lambda e, bk=bk, hp=hp, tb=tb: e.tensor_copy(out=KT[0:64, 2 * hp + 1, tb:tb + 512], in_=bk[64:128, :]), reads=[br], writes=[f"KT{b}_{2 * hp + 1}"])
            ka, kar = inproj(768, 896)
            kb, kbr = inproj(0, 128, lw=w_krr)
            t1, t1r = F(5)
            t2, t2r = F(6)
            P.add("dve", lambda e, ka=ka, t1=t1: e.tensor_tensor(out=t1[64:96, :], in0=ka[64:96, :], in1=cos2[64:96, :], op=ALU.mult), reads=[kar, "cos2"], writes=[t1r])
            P.add("dve", lambda e, kb=kb, t2=t2: e.tensor_tensor(out=t2[64:96, :], in0=kb[64:96, :], in1=sin2[64:96, :], op=ALU.mult), reads=[kbr, "sin2"], writes=[t2r])
            for h in range(4):
                P.add("pool", lambda e, t1=t1, t2=t2, h=h, tb=tb: e.tensor_tensor(out=KT[64:96, h, tb:tb + 512], in0=t1[64:96, :], in1=t2[64:96, :], op=ALU.add),
                      reads=[t1r, t2r], writes=[f"KT{b}_{h}r"])
            for i2 in range(2):
                bk, br = genh[0].next()
                for ii in range(2):
                    i = 2 * i2 + ii
                    rh = w_v[:, :]
                    P.add("pe", lambda e, bk=bk, ii=ii, i=i, rh=rh: e.matmul(bk[:, ii * 256:(ii + 1) * 256], lhsT=cqn[2][0][:, i * 128:(i + 1) * 128], rhs=rh, start=True, stop=True),
                          reads=[cqn[2][1], "w_v"], writes=[br])
                for ii in range(2):
                    i = 2 * i2 + ii
                    dst = Vt[:, 4 * b + i, :, 0:64]
                    src = bk[:, ii * 256:(ii + 1) * 256].rearrange("p (h d) -> p h d", h=4)
                    P.add("act", lambda e, dst=dst, src=src: e.activation(out=dst, in_=src, func=AF.Copy), reads=[br, "Vinit"], writes=[f"V{4 * b + i}"])

            pre_ops.append(P.cap)
            P.cap = None
            nk = 4 * b + 4
            units = [(h, kt) for h in range(4) for kt in range(nk)]
            accs = {}
            pts = {}
            LA = 2

            def emit_s(h, kt):
                j0 = max(0, kt - 4 * b)
                q0 = j0 * 128
                sbk, sbr = stb.next()
                P.add("pe", lambda e, sbk=sbk, h=h, kt=kt, q0=q0: e.matmul(sbk[:, q0:512], lhsT=KT[:, h, kt * 128:(kt + 1) * 128], rhs=QT[:, h, q0:512], start=True, stop=True),
                      reads=[f"KT{kt // 4}_{h}", f"KT{kt // 4}_{h}r", f"QT{h}", f"QT{h}r", "KTpad", "QTpad"], writes=[sbr])
                pt, ptres = ptr.next()
                P.add("act", lambda e, sbk=sbk, pt=pt, q0=q0: e.activation(out=pt[:, q0:512], in_=sbk[:, q0:512], func=AF.Exp, scale=SCALE), reads=[sbr], writes=[ptres])
                if kt >= 4 * b:
                    P.add("pool", lambda e, pt=pt, q0=q0: e.tensor_tensor(out=pt[:, q0:q0 + 128], in0=pt[:, q0:q0 + 128], in1=maskT[:], op=ALU.mult),
                          reads=[ptres, "maskT"], writes=[ptres])
                pts[(h, kt)] = (pt, ptres, j0)

            def emit_pv(h, kt):
                if kt == 0:
                    accs[h] = pvb.next()
                acc, accr = accs[h]
                accv = acc[:, 0:260].rearrange("p (j d) -> p j d", j=4)
                pt, ptres, j0 = pts.pop((h, kt))
                for j in range(j0, 4):
                    first = (kt == 0 and j == j0)
                    P.add("pe", lambda e, pt=pt, j=j, kt=kt, h=h, first=first, accv=accv: e.matmul(accv[:, j, :], lhsT=pt[:, j * 128:(j + 1) * 128], rhs=Vt[:, kt, h, :],
                                                                                                start=first, stop=(kt == nk - 1 and j == 3), skip_group_check=True),
                          reads=[ptres, f"V{kt}", "Vinit"], writes=[accr])
                if kt == nk - 1:
                    P.add("dve", lambda e, accv=accv, h=h: e.reciprocal(out=sm[:, 8 + 4 * h:12 + 4 * h], in_=accv[:, :, 64]), reads=[accr], writes=[f"rec{h}"])
                    for j in range(4):
                        P.add("dve", lambda e, accv=accv, h=h, j=j: e.tensor_scalar(out=omla[:, j, h * 64:(h + 1) * 64], in0=accv[:, j, 0:64], scalar1=sm[:, 8 + 4 * h + j:9 + 4 * h + j],
                                                                                    scalar2=None, op0=ALU.mult),
                              reads=[accr, f"rec{h}"], writes=[f"omla{j}"])

            att_units = []
            for idx in range(len(units) + LA):
                P.cap = []
                if idx < len(units):
                    emit_s(*units[idx])
                if idx - LA >= 0:
                    emit_pv(*units[idx - LA])
                att_units.append(P.cap)
                P.cap = None
            streams = []
            def emit_mla_out():
                bk, br = genh[0].next()
                bv = bk[:, :].bitcast(BF16)
                for j in range(4):
                    P.add("act", lambda e, j=j: e.activation(out=junk[:, 0:256], in_=omla[:, j, :], func=AF.Square, accum_out=sm[:, 24 + j:25 + j]), reads=[f"omla{j}"], writes=[f"oss{j}"])
                P.add("act", lambda e: e.activation(out=sm[:, 28:32], in_=sm[:, 24:28], func=AF.Ln, scale=1.0 / 256, bias=epsc[:]), reads=[f"oss{j}" for j in range(4)] + ["eps"], writes=["osd"])
                P.add("act", lambda e: e.activation(out=sm[:, 28:32], in_=sm[:, 28:32], func=AF.Exp, scale=-0.5), reads=["osd"], writes=["osd"])
                for j in range(4):
                    mt = mtok[j % 2]
                    P.add("dve", lambda e, j=j, mt=mt: e.scalar_tensor_tensor(out=mt[:], in0=omla[:, j, :], scalar=sm[:, 28 + j:29 + j], in1=ggm[:], op0=ALU.mult, op1=ALU.mult),
                          reads=[f"omla{j}", "osd", "ggm"], writes=[f"mtok{j % 2}"])
                    for c in range(2):
                        P.add("pe", lambda e, mt=mt, c=c, j=j, bv=bv: e.transpose(out=bv[:, c * 512 + j * 128:c * 512 + (j + 1) * 128], in_=mt[:, c * 128:(c + 1) * 128], identity=ident[:]),
                              reads=[f"mtok{j % 2}", "ident"], writes=[br])
                P.add("act", lambda e, bv=bv: e.activation(out=mixedT[:, 2:4, :].rearrange("p c t -> p (c t)"), in_=bv[:, :], func=AF.Copy), reads=[br], writes=["mixedT2", "mixedT3"])

            P.cap = []
            genh[0] = gen1[0]
            for c in range(2):
                gb, gbr = inproj(256 + c * 128, 256 + (c + 1) * 128)
                sg, sgr = F(0)
                P.add("act", lambda e, gb=gb, sg=sg: e.activation(out=sg[:], in_=gb[:, :], func=AF.Exp, scale=-1.0), reads=[gbr], writes=[sgr])
                P.add("act", lambda e, sg=sg: e.activation(out=sg[:], in_=sg[:], func=AF.Ln, bias=onec[:]), reads=[sgr, "eps"], writes=[sgr])
                P.add("act", lambda e, sg=sg: e.activation(out=sg[:], in_=sg[:], func=AF.Exp, scale=-1.0), reads=[sgr], writes=[sgr])
                ab, abr = inproj(c * 128, (c + 1) * 128)
                P.add("dve", lambda e, ab=ab, sg=sg, c=c: e.tensor_tensor(out=ycv[:, c, 30:542], in0=ab[:, :], in1=sg[:], op=ALU.mult), reads=[abr, sgr], writes=[f"ycv{c}"])
            ycf = [F(1), F(2)]
            ycb = [H(0), H(1)]
            ysq = [H(2), H(3)]
            for c in range(2):
                bk, br = genh[0].next()
                for k in range(31):
                    ds = (c * 31 + k) % 8
                    P.add("dve", lambda e, c=c, k=k, ds=ds: e.tensor_scalar(out=dg[:, ds, :], in0=ident[:], scalar1=colp[:, c * 31 + k:c * 31 + k + 1], scalar2=None, op0=ALU.mult),
                          reads=["ident", "colp"], writes=[f"dg{ds}"])
                    P.add("pe", lambda e, bk=bk, c=c, k=k, ds=ds: e.matmul(bk[:, :], lhsT=dg[:, ds, :], rhs=ycv[:, c, k:k + 512], start=(k == 0), stop=(k == 30)),
                          reads=[f"dg{ds}", f"ycv{c}", f"ycvh{c}"], writes=[br])
                ft, fr = ycf[c]
                P.add("dve", lambda e, bk=bk, ft=ft, c=c: e.tensor_scalar(out=ft[:], in0=bk[:, :], scalar1=colp[:, 62 + c:63 + c], scalar2=None, op0=ALU.add), reads=[br, "colp"], writes=[fr])
                P.add("pool", lambda e, c=c: e.tensor_copy(out=ycv[:, c, 0:30], in_=ycv[:, c, 512:542]), reads=[f"ycv{c}"], writes=[f"ycvh{c}"])
                hb_, hbr = ycb[c]
                P.add("pool", lambda e, ft=ft, hb_=hb_: e.tensor_copy(out=hb_[:], in_=ft[:]), reads=[fr], writes=[hbr])
                hq, hqr = ysq[c]
                P.add("pool", lambda e, ft=ft, hq=hq: e.tensor_tensor(out=hq[:], in0=ft[:], in1=ft[:], op=ALU.mult), reads=[fr], writes=[hqr])
            mt_, mtr = F(3)
            m2, m2r = F(4)
            vr, vrr = F(5)
            mb, mbr = genh[0].next()
            mm_group(mb[:, :], mbr, [(ones[:, :], ycb[c][0][:, :], [ycb[c][1], "ones"]) for c in range(2)])
            P.add("dve", lambda e, mb=mb, mt_=mt_: e.tensor_scalar(out=mt_[:], in0=mb[:, :], scalar1=1.0 / 256, scalar2=None, op0=ALU.mult), reads=[mbr], writes=[mtr])
            qb, qbr = genh[0].next()
            mm_group(qb[:, :], qbr, [(ones[:, :], ysq[c][0][:, :], [ysq[c][1], "ones"]) for c in range(2)])
            P.add("pool", lambda e, mt_=mt_, m2=m2: e.tensor_tensor(out=m2[:], in0=mt_[:], in1=mt_[:], op=ALU.mult), reads=[mtr], writes=[m2r])
            P.add("dve", lambda e, qb=qb, m2=m2, vr=vr: e.scalar_tensor_tensor(out=vr[:], in0=qb[:, :], scalar=1.0 / 256, in1=m2[:], op0=ALU.mult, op1=ALU.subtract),
                  reads=[qbr, m2r], writes=[vrr])
            P.add("act", lambda e, vr=vr: e.activation(out=vr[:], in_=vr[:], func=AF.Ln, bias=epsc[:]), reads=[vrr, "eps"], writes=[vrr])
            P.add("act", lambda e, vr=vr: e.activation(out=vr[:], in_=vr[:], func=AF.Exp, scale=-0.5), reads=[vrr], writes=[vrr])
            sact = [H(0), H(1)]
            for c in range(2):
                ft, fr = ycf[c]
                P.add("dve", lambda e, ft=ft, mt_=mt_: e.tensor_tensor(out=ft[:], in0=ft[:], in1=mt_[:], op=ALU.subtract), reads=[fr, mtr], writes=[fr])
                P.add("pool", lambda e, ft=ft, vr=vr: e.tensor_tensor(out=ft[:], in0=ft[:], in1=vr[:], op=ALU.mult), reads=[fr, vrr], writes=[fr])
                ht, hres = sact[c]
                sg, sgr = F(0)
                P.add("dve", lambda e, ft=ft, c=c: e.tensor_scalar(out=ft[:], in0=ft[:], scalar1=colp[:, 64 + c:65 + c], scalar2=colp[:, 66 + c:67 + c], op0=ALU.mult, op1=ALU.add),
                      reads=[fr, "colp"], writes=[fr])
                P.add("act", lambda e, ft=ft, sg=sg: e.activation(out=sg[:], in_=ft[:], func=AF.Exp, scale=-1.0), reads=[fr], writes=[sgr])
                P.add("act", lambda e, sg=sg: e.activation(out=sg[:], in_=sg[:], func=AF.Ln, bias=onec[:]), reads=[sgr, "eps"], writes=[sgr])
                P.add("act", lambda e, sg=sg: e.activation(out=sg[:], in_=sg[:], func=AF.Exp, scale=-1.0), reads=[sgr], writes=[sgr])
                P.add("pool", lambda e, ft=ft, sg=sg, ht=ht: e.tensor_tensor(out=ht[:], in0=ft[:], in1=sg[:], op=ALU.mult), reads=[fr, sgr], writes=[hres])
            of = [F(3), F(4)]
            for co in range(2):
                bk, br = genh[0].next()
                mm_group(bk[:, :], br, [(pw_sb[:, ci, co * 128:(co + 1) * 128], sact[ci][0][:, :], [sact[ci][1], "pw"]) for ci in range(2)])
                ft, fr = of[co]
                P.add("dve", lambda e, bk=bk, ft=ft: e.tensor_copy(out=ft[:], in_=bk[:, :]), reads=[br], writes=[fr])
            group_norm_fm(0, [of[0][0][:], of[1][0][:]], [of[0][1], of[1][1]], [(Hm[2][:], "H2"), (Hm[3][:], "H3")], (Fm[5][:], "F5"))

            streams.append(P.cap)
            P.cap = []
            genh[0] = gen1[1]
            yp = [H(4), H(5)]
            for c in range(2):
                pbk, pbr = inproj(864 + c * 128, 864 + (c + 1) * 128)
                P.add("dve", lambda e, pbk=pbk, c=c: e.tensor_copy(out=zp[:, c, 15:527], in_=pbk[:, :]), reads=[pbr], writes=[f"zp{c}"])
                zc = zp[:, c, :]
                P.add("pool", lambda e, zc=zc: e.tensor_tensor(out=PA[:, 1:527], in0=zc[:, 1:527], in1=zc[:, 0:526], op=ALU.add), reads=[f"zp{c}", f"zph{c}"], writes=["PA"])
                P.add("pool", lambda e: e.tensor_tensor(out=PB[:, 3:527], in0=PA[:, 3:527], in1=PA[:, 1:525], op=ALU.add), reads=["PA"], writes=["PB"])
                if c == 0:
                    lo, hi = PA, PB
                    lor, hir = "PA", "PB"
                else:
                    P.add("pool", lambda e: e.tensor_tensor(out=PA[:, 7:527], in0=PB[:, 7:527], in1=PB[:, 3:523], op=ALU.add), reads=["PB", "PA"], writes=["PA"])
                    P.add("pool", lambda e: e.tensor_tensor(out=PB[:, 15:527], in0=PA[:, 15:527], in1=PA[:, 7:519], op=ALU.add), reads=["PA", "PB"], writes=["PB"])
                    lo, hi = PA, PB
                    lor, hir = "PA", "PB"
                ht, hres = yp[c]
                for (pr, srcT, srcr) in ((slice(0, 64), lo, lor), (slice(64, 128), hi, hir)):
                    P.add("dve", lambda e, pr=pr, srcT=srcT, c=c, ht=ht, zc=zc: e.scalar_tensor_tensor(out=ht[pr, :], in0=srcT[pr, 15:527], scalar=c_sb[pr, 288 + c:289 + c], in1=zc[pr, 15:527],
                                                                                                     op0=ALU.mult, op1=ALU.subtract),
                          reads=[srcr, f"zp{c}", "c"], writes=[hres])
                    if b == 0:
                        tt, ttr = F(6)
                        P.add("dve", lambda e, pr=pr, srcT=srcT, c=c, tt=tt: e.tensor_tensor(out=tt[pr, 0:16], in0=srcT[pr, 15:31], in1=c_sb[pr, 256 + c * 16:272 + c * 16], op=ALU.mult),
                              reads=[srcr, "c"], writes=[ttr])
                        P.add("dve", lambda e, pr=pr, c=c, tt=tt, ht=ht, zc=zc: e.tensor_tensor(out=ht[pr, 0:16], in0=tt[pr, 0:16], in1=zc[pr, 15:31], op=ALU.subtract),
                              reads=[ttr, f"zp{c}", hres], writes=[hres])
                P.add("pool", lambda e, c=c: e.tensor_copy(out=zp[:, c, 0:15], in_=zp[:, c, 512:527]), reads=[f"zp{c}"], writes=[f"zph{c}"])
            of = [(PA[:, 0:512], "PA"), (PB[:, 0:512], "PB")]
            for c in range(2):
                bk, br = genh[0].next()
                mm_group(bk[:, :], br, [(poolw_sb[:, c, :], yp[c][0][:, :], [yp[c][1], "poolw"])])
                ft, fr = of[c]
                P.add("dve", lambda e, bk=bk, ft=ft, c=c: e.tensor_scalar(out=ft, in0=bk[:, :], scalar1=colp[:, 71 + c:72 + c], scalar2=None, op0=ALU.mult), reads=[br, "colp"], writes=[fr])
            group_norm_fm(2, [of[0][0], of[1][0]], [of[0][1], of[1][1]], [(Hm[4][:], "H4"), (Hm[5][:], "H5")], (Fm[6][:], "F6"))

            streams.append(P.cap)
            P.cap = []
            genh[0] = gen1[2]
            for i2 in range(2):
                bk, br = genh[0].next()
                for ii in range(2):
                    i = 2 * i2 + ii
                    for k in range(8):
                        P.add("pe", lambda e, bk=bk, ii=ii, i=i, k=k: e.matmul(bk[:, ii * 256:(ii + 1) * 256], lhsT=xnT[:, k, i * 128:(i + 1) * 128], rhs=w_in_sb[:, k, 1376:1632],
                                                                            start=(k == 0), stop=(k == 7)),
                              reads=XNT + WIN, writes=[br])
                for ii in range(2):
                    i = 2 * i2 + ii
                    src = bk[:, ii * 256:(ii + 1) * 256]
                    P.add("act", lambda e, src=src, i=i: e.activation(out=junk[:, 0:256], in_=src, func=AF.Square, accum_out=sm[:, 32 + i:33 + i]), reads=[br], writes=[f"vss{i}"])
                lo = 2 * i2
                P.add("act", lambda e, lo=lo: e.activation(out=sm[:, 36 + lo:38 + lo], in_=sm[:, 32 + lo:34 + lo], func=AF.Ln, scale=1.0 / 256, bias=epsc[:]), reads=[f"vss{lo}", f"vss{lo + 1}", "eps"], writes=[f"vsdp{i2}"])
                P.add("act", lambda e, lo=lo: e.activation(out=sm[:, 36 + lo:38 + lo], in_=sm[:, 36 + lo:38 + lo], func=AF.Exp, scale=-0.5), reads=[f"vsdp{i2}"], writes=[f"vsdp{i2}"])
                for ii in range(2):
                    i = 2 * i2 + ii
                    src = bk[:, ii * 256:(ii + 1) * 256]
                    P.add("dve", lambda e, src=src, i=i: e.tensor_scalar(out=vn[:, i, :], in0=src, scalar1=sm[:, 36 + i:37 + i], scalar2=None, op0=ALU.mult), reads=[br, f"vsdp{i2}"], writes=[f"vn{i}"])
            of = [F(7), F(8)]
            gate = of
            for c in range(2):
                gb, gbr = genh[0].next()
                for i in range(4):
                    for hh in range(2):
                        h = 2 * c + hh
                        P.add("pe", lambda e, gb=gb, i=i, hh=hh, h=h: e.matmul(gb[hh * 64:(hh + 1) * 64, i * 128:(i + 1) * 128], lhsT=vn[:, i, h * 64:(h + 1) * 64], rhs=wsT_sb[:, h, :],
                                                                            start=True, stop=True),
                              reads=[f"vn{i}", "wsT"], writes=[gbr])
                gt, gtr = gate[c]
                bsb = bsT[:, c, :].unsqueeze(1).broadcast_to([128, 4, 128])
                P.add("dve", lambda e, gb=gb, gt=gt, c=c, bsb=bsb: e.scalar_tensor_tensor(out=gt[:].rearrange("p (i t) -> p i t", i=4), in0=gb[:, :].rearrange("p (i t) -> p i t", i=4),
                                                                                        scalar=colp[:, 73 + c:74 + c], in1=bsb, op0=ALU.mult, op1=ALU.add),
                      reads=[gbr, "colp", "bsT"], writes=[gtr])
                ub, ubr = inproj(1120 + c * 128, 1120 + (c + 1) * 128)
                ft, fr = of[c]
                P.add("dve", lambda e, ub=ub, gt=gt, ft=ft: e.tensor_tensor(out=ft[:], in0=ub[:, :], in1=gt[:], op=ALU.mult), reads=[ubr, gtr], writes=[fr])
            group_norm_fm(3, [of[0][0][:], of[1][0][:]], [of[0][1], of[1][1]],
                          [(vn[:, 0:2, :].rearrange("p a b -> p (a b)"), ["vn0", "vn1"]), (vn[:, 2:4, :].rearrange("p a b -> p (a b)"), ["vn2", "vn3"])], (Fm[9][:], "F9"))

            streams.append(P.cap)
            P.cap = None
            genh[0] = gen3
            main_ops.append((att_units, streams))
            P.cap = []
            genh[0] = genpost
            emit_mla_out()
            MX = [f"mixedT{k}" for k in range(8)]
            for i in range(4):
                t = 4 * b + i
                s = i % 2
                P.add("sp", lambda e, t=t, s=s: e.dma_start(out=hr[s][:], in_=hsrc[t * 128:(t + 1) * 128, :]), reads=[f"hd{t}"], writes=[f"hr{s}"], slot=f"hr{s}")
                for n in range(2):
                    bk, br = genh[0].next()
                    mm_group(bk[:, :], br, [(mixedT[:, k, i * 128:(i + 1) * 128], w_out_sb[:, k, n * 512:(n + 1) * 512], MX + WOUT) for k in range(8)])
                    P.add("dve", lambda e, bk=bk, s=s, n=n: e.tensor_tensor(out=hr[s][:, n * 512:(n + 1) * 512], in0=bk[:, :], in1=hr[s][:, n * 512:(n + 1) * 512], op=ALU.add),
                          reads=[br, f"hr{s}"], writes=[f"hr{s}"])
                P.add("sp", lambda e, t=t, s=s: e.dma_start(out=hbuf[t * 128:(t + 1) * 128, :], in_=hr[s][:]), reads=[f"hr{s}"], writes=[f"hd{t}"], slot=f"st{s}")
            post_ops.append(P.cap)
            P.cap = None
        def merge_main(att_units, streams, extra):
            strs = list(streams) + [extra]
            order = [0, 1, 0, 2, 3]
            totY = sum(len(x) for x in strs)
            per = max(3, -(-totY // max(1, len(att_units))))
            posn = [0] * len(strs)
            st = {"rr": 0}
            out = []

            def emit_y(n):
                done = 0
                while done < n and any(posn[i] < len(strs[i]) for i in range(len(strs))):
                    i = order[st["rr"] % len(order)]
                    st["rr"] += 1
                    if posn[i] < len(strs[i]):
                        out.append(strs[i][posn[i]])
                        posn[i] += 1
                        done += 1

            for u in att_units:
                out.extend(u)
                emit_y(per)
            emit_y(10 ** 9)
            return out

        P.replay(prenorm_ops[0])
        P.replay(pre_ops[0])
        for b in range(NB):
            att_units, streams = main_ops[b]
            P.replay(merge_main(att_units, streams, prenorm_ops[b + 1] if b + 1 < NB else []))
            A_ = post_ops[b]
            B_ = pre_ops[b + 1] if b + 1 < NB else []
            ia = ib = 0
            ra = max(1, len(A_))
            rb = max(1, len(B_))
            while ia < len(A_) or ib < len(B_):
                if ib >= len(B_) or (ia < len(A_) and ia * rb <= ib * ra):
                    P.replay([A_[ia]])
                    ia += 1
                else:
                    P.replay([B_[ib]])
                    ib += 1
        P.emit()


POOL_WINDOWS = (2, 4, 8, 16)


def make_consts():
    c = np.zeros((128, NCST), np.float32)
    c[:, 0:128] = np.eye(128, dtype=np.float32)
    k = np.arange(128)[:, None]
    q = np.arange(128)[None, :]
    c[:, 128:256] = (q >= k).astype(np.float32)
    for ch in range(2):
        for p in range(128):
            w = POOL_WINDOWS[2 * ch + p // 64]
            c[p, 288 + ch] = 1.0 / w
            for t in range(16):
                c[p, 256 + ch * 16 + t] = 1.0 / min(t + 1, w)
    inv_freq = (10000.0 ** (-np.arange(0, 32, 2, dtype=np.float32) / 32)).astype(np.float32)
    for p in range(128):
        c[p, 290] = inv_freq[p % 16]
    return c


def col128(v):
    v = np.asarray(v, np.float32)
    return np.ascontiguousarray(v.reshape(-1, 128).T)


def pack_layer_inputs(inp, layers):
    L = len(layers)
    colp = np.zeros((L, 128, NCOL), np.float32)
    rowp = np.zeros((L, NROW), np.float32)
    poolw = np.zeros((L, 128, 2, 128), np.float32)
    wsT = np.zeros((L, 128, 4, 128), np.float32)
    bsT = np.zeros((L, 128, 2, 128), np.float32)
    for li, l in enumerate(layers):
        dw = inp["conv_dw_w"][l]
        for c in range(2):
            colp[li, :, c * 31:(c + 1) * 31] = dw[:, c * 128:(c + 1) * 128].T
        colp[li, :, 62:64] = col128(inp["conv_dw_b"][l])
        colp[li, :, 64:66] = col128(inp["conv_ln_g"][l])
        colp[li, :, 66:68] = col128(inp["conv_ln_b"][l])
        qg = np.zeros(256, np.float32)
        qg[0:192] = inp["mla_q_norm_g"][l]
        colp[li, :, 68:70] = col128(qg)
        colp[li, :, 70:71] = col128(inp["mla_kv_norm_g"][l])
        colp[li, :, 71:73] = col128(inp["pool_scale"][l])
        colp[li, :, 73:75] = col128(inp["gmlp_norm_g"][l])
        colp[li, :, 75:83] = col128(inp["group_norm_g"][l].reshape(-1))
        rowp[li, 0:1024] = inp["mix_norm_g"][l]
        rowp[li, 1024:2048] = inp["ffn_norm_g"][l]
        rowp[li, 2048:2304] = inp["group_norm_g"][l, 1]
        pw_ = inp["pool_w"][l]
        for c in range(2):
            for hh in range(2):
                poolw[li, hh * 64:(hh + 1) * 64, c, hh * 64:(hh + 1) * 64] = pw_[2 * c + hh]
        wsT[li] = np.transpose(inp["gmlp_ws"][l], (2, 0, 1))
        bs = inp["gmlp_bs"][l]
        for c in range(2):
            for hh in range(2):
                bsT[li, hh * 64:(hh + 1) * 64, c, :] = bs[2 * c + hh][None, :]
    sl = list(layers)
    d = dict(
        w_in=np.ascontiguousarray(inp["w_in"][sl]), w_out=np.ascontiguousarray(inp["w_out"][sl]),
        w_uq=np.ascontiguousarray(inp["mla_w_uq"][sl]), w_ukv=np.ascontiguousarray(inp["mla_w_ukv"][sl]),
        conv_pw=np.ascontiguousarray(inp["conv_pw_w"][sl]), poolw=poolw, wsT=wsT, bsT=bsT, colp=colp, rowp=rowp,
        w_ff1=np.ascontiguousarray(inp["w_ff1"][sl]), w_ff2=np.ascontiguousarray(inp["w_ff2"][sl]),
        fng=np.ascontiguousarray(np.asarray(inp["final_norm_g"], np.float32).reshape(1, D)),
        cst=make_consts(),
    )
    return d


_PROGS = {}


def get_prog(T, n_layers, final):
    key = (T, n_layers, final)
    if key not in _PROGS:
        _PROGS[key] = build_program(T, n_layers, final)
    return _PROGS[key]


def run_layers(inp, hs, positions, layers, final, T):
    shared = pack_layer_inputs(inp, layers)
    nc = get_prog(T, len(layers), final)
    in_maps = []
    for ci in range(len(hs)):
        m = dict(shared)
        m["x"] = np.ascontiguousarray(hs[ci], dtype=np.float32)
        m["pos"] = np.ascontiguousarray(positions[ci].reshape(1, T).astype(np.int32))
        in_maps.append(m)
    res = run_bass_kernel_spmd(nc, in_maps, core_ids=list(range(len(hs))))
    return [np.asarray(r["out"]) for r in res.results]


FUSED = True


def kernel(**inputs):
    inp = {k: np.asarray(v) for k, v in inputs.items()}
    x = inp["x"].astype(np.float32)
    B, T, _ = x.shape
    depth = inp["w_in"].shape[0]
    positions = inp["positions"]
    hs = [x[b] for b in range(B)]
    if FUSED:
        outs = run_layers(inp, hs, positions, list(range(depth)), True, T)
    else:
        for l in range(depth):
            hs = run_layers(inp, hs, positions, [l], l == depth - 1, T)
        outs = hs
    return np.stack(outs, axis=0).astype(np.float32)
```

```python
from contextlib import ExitStack
import math
import numpy as np
import concourse.bass as bass
import concourse.mybir as mybir
from concourse.bass_utils import run_bass_kernel_spmd

F32 = mybir.dt.float32
BF16 = mybir.dt.bfloat16
I32 = mybir.dt.int32
ALU = mybir.AluOpType
AF = mybir.ActivationFunctionType

D = 1024
DIN = 1632
DFF = 4096
EPS = 1e-6
NCOL = 83
NROW = 2304
NCST = 291
ENGS = ("pe", "act", "dve", "pool", "sp")


class Res:
    __slots__ = ("name", "w", "rs")

    def __init__(self, name):
        self.name = name
        self.w = None
        self.rs = []


class Slot:
    __slots__ = ("name", "n", "last", "sem")

    def __init__(self, name):
        self.name = name
        self.n = 0
        self.last = None
        self.sem = None


class Op:
    __slots__ = ("eng", "fn", "deps", "sig", "val", "slot", "idx")


class Sync:
    def __init__(self, nc, stack):
        self.nc = nc
        self.stack = stack
        self.esem = {e: stack.enter_context(nc.semaphore(f"s_{e}")) for e in ENGS}
        self.ecount = {e: 0 for e in ENGS}
        self.slots = {}


class Prog:
    def __init__(self, nc, name, sync):
        self.nc = nc
        self.name = name
        self.sync = sync
        self.ops = {e: [] for e in ENGS}
        self.slots = sync.slots
        for s in self.slots.values():
            s.last = None
        self.used = []
        self.res = {}
        self.cap = None

    def R(self, name):
        r = self.res.get(name)
        if r is None:
            r = self.res[name] = Res(name)
        return r

    def slot(self, name):
        s = self.slots.get(name)
        if s is None:
            s = self.slots[name] = Slot(name)
        return s

    def add(self, eng, fn, reads=(), writes=(), slot=None):
        if self.cap is not None:
            self.cap.append((eng, fn, list(reads), list(writes), slot))
            return None
        reads = [self.R(r) if isinstance(r, str) else r for r in reads]
        writes = [self.R(r) if isinstance(r, str) else r for r in writes]
        if isinstance(slot, str):
            slot = self.slot(slot)
        op = Op()
        op.eng = eng
        op.fn = fn
        op.sig = slot is not None
        op.val = None
        op.slot = slot
        deps = []
        xr = [r for r in reads if r.name.startswith("bank")]
        xw = [r for r in writes if r.name.startswith("bank")]
        reads = [r for r in reads if not r.name.startswith("bank")]
        writes = [r for r in writes if not r.name.startswith("bank")]
        for r, kind in [(r, "r") for r in xr] + [(r, "w") for r in xw]:
            if r.w is not None:
                pk = r.rs[0] if r.rs else "w"
                if not (r.w.eng == eng and pk == "r" and kind == "r"):
                    deps.append(r.w)
            r.w = op
            r.rs = [kind]
        for r in reads:
            if r.w is not None:
                deps.append(r.w)
        for r in writes:
            if r.w is not None:
                deps.append(r.w)
            deps.extend(r.rs)
        if slot is not None and slot.last is not None:
            deps.append(slot.last)
        for r in reads:
            r.rs.append(op)
        for r in writes:
            r.w = op
            r.rs = []
        if slot is not None:
            slot.last = op
            slot.n += 1
            op.val = 16 * slot.n
            if slot not in self.used:
                self.used.append(slot)
        dd = []
        seen = set()
        latest = {}
        for d in deps:
            if d is op or id(d) in seen:
                continue
            seen.add(id(d))
            if d.eng == "pe" and eng == "pe" and d.slot is None and slot is None:
                continue
            if d.slot is None:
                cur = latest.get(d.eng)
                if cur is None or d.idx > cur.idx:
                    latest[d.eng] = d
            else:
                dd.append(d)
        dd.extend(latest.values())
        for d in dd:
            d.sig = True
        op.deps = dd
        op.idx = len(self.ops[eng])
        self.ops[eng].append(op)
        return op

    def replay(self, items):
        for it in items:
            self.add(*it)

    def emit(self):
        nc = self.nc
        sync = self.sync
        for e in ENGS:
            c = sync.ecount[e]
            for op in self.ops[e]:
                if op.slot is None and op.sig:
                    c += 1
                    op.val = c
            sync.ecount[e] = c
        with ExitStack() as st:
            esem = sync.esem
            fin = list(self.used)
            for s in fin:
                if s.sem is None:
                    s.sem = sync.stack.enter_context(nc.semaphore(f"d_{s.name}"))
            block = st.enter_context(nc.Block())

            def run(eng_name, e):
                known = {}
                for op in self.ops[eng_name]:
                    for d in op.deps:
                        sem = d.slot.sem if d.slot is not None else esem[d.eng]
                        k = id(sem)
                        if known.get(k, 0) >= d.val:
                            continue
                        known[k] = d.val
                        e.wait_ge(sem, d.val)
                    ins = op.fn(e)
                    if op.slot is not None:
                        ins.then_inc(op.slot.sem, 16)
                    elif op.sig:
                        ins.then_inc(esem[eng_name], 1)
                if eng_name == "sp":
                    for s in fin:
                        e.wait_ge(s.sem, 16 * s.n)

            block.tensor(lambda e: run("pe", e))
            block.scalar(lambda e: run("act", e))
            block.vector(lambda e: run("dve", e))
            block.gpsimd(lambda e: run("pool", e))
            block.sync(lambda e: run("sp", e))


class Rot:
    def __init__(self, items):
        self.items = items
        self.i = 0

    def next(self):
        it = self.items[self.i % len(self.items)]
        self.i += 1
        return it


def build_program(T, n_layers, final):
    NB = T // 512
    NT = T // 128
    nc = bass.Bass("TRN2", target_bir_lowering=False)

    def din(name, shape, dt=F32):
        return nc.dram_tensor(name, list(shape), dt, kind="ExternalInput").ap()

    L = n_layers
    x = din("x", [T, D])
    pos = din("pos", [1, T], I32)
    w_in = din("w_in", [L, D, DIN])
    w_out = din("w_out", [L, D, D])
    w_uq = din("w_uq", [L, 192, 384])
    w_ukv = din("w_ukv", [L, 128, 512])
    conv_pw = din("conv_pw", [L, 256, 256])
    poolw = din("poolw", [L, 128, 2, 128])
    wsT = din("wsT", [L, 128, 4, 128])
    bsT = din("bsT", [L, 128, 2, 128])
    colp = din("colp", [L, 128, NCOL])
    rowp = din("rowp", [L, NROW])
    w_ff1 = din("w_ff1", [L, D, DFF])
    w_ff2 = din("w_ff2", [L, DFF, D])
    fng = din("fng", [1, D])
    cst = din("cst", [128, NCST])
    out = nc.dram_tensor("out", [T, D], F32, kind="ExternalOutput").ap()
    hbuf = nc.dram_tensor("hbuf", [T, D], F32).ap()
    ropetab = nc.dram_tensor("ropetab", [2, 32, T], F32).ap()

    with ExitStack() as top:
        banks = [top.enter_context(nc.psum_tensor(f"bank{i}", [128, 512], F32)) for i in range(8)]
        sync = Sync(nc, top)
        w_in_sb = top.enter_context(nc.sbuf_tensor("w_in_sb", [128, 8, DIN], BF16))

        with ExitStack() as st:
            P = Prog(nc, "pr", sync)
            sb = lambda n, s, d: st.enter_context(nc.sbuf_tensor(n, s, d))
            posi = sb("posi", [32, T], I32)
            posf = sb("posf", [32, T], F32)
            ang = sb("ang", [32, T], F32)
            u = sb("u", [32, T], F32)
            ki = sb("ki", [32, T], I32)
            kf = sb("kf", [32, T], F32)
            ng = sb("ng", [32, T], F32)
            tab = [sb("tab0", [32, T], F32), sb("tab1", [32, T], F32)]
            c_sb = sb("c_sb", [128, NCST], F32)
            nbias = sb("nbias", [32, 1], F32)
            P.add("sp", lambda e: e.dma_start(out=posi[:], in_=pos[0, :].partition_broadcast(32)), writes=["posi"], slot="posi")
            P.add("sp", lambda e: e.dma_start(out=c_sb[:], in_=cst), writes=["c"], slot="c")
            P.add("dve", lambda e: e.tensor_copy(out=posf[:], in_=posi[:]), reads=["posi"], writes=["posf"])
            P.add("dve", lambda e: e.memset(nbias[:], -math.pi * (1 - 1e-6)), writes=["nbias"])
            P.add("dve", lambda e: e.tensor_scalar(out=ang[:], in0=posf[:], scalar1=c_sb[0:32, 290:291], scalar2=None, op0=ALU.mult),
                  reads=["posf", "c"], writes=["ang"])
            for i, shift in enumerate((0.75, 0.5)):
                P.add("dve", lambda e, shift=shift: e.tensor_scalar(out=u[:], in0=ang[:], scalar1=1.0 / (2 * math.pi), scalar2=shift, op0=ALU.mult, op1=ALU.add),
                      reads=["ang"], writes=["u"])
                P.add("dve", lambda e: e.tensor_copy(out=ki[:], in_=u[:]), reads=["u"], writes=["ki"])
                P.add("dve", lambda e: e.tensor_copy(out=kf[:], in_=ki[:]), reads=["ki"], writes=["kf"])
                P.add("dve", lambda e: e.tensor_tensor(out=u[:], in0=u[:], in1=kf[:], op=ALU.subtract), reads=["u", "kf"], writes=["u"])
                P.add("dve", lambda e: e.tensor_scalar(out=ng[:], in0=u[:], scalar1=0.0, scalar2=None, op0=ALU.is_lt), reads=["u"], writes=["ng"])
                P.add("dve", lambda e: e.tensor_tensor(out=u[:], in0=u[:], in1=ng[:], op=ALU.add), reads=["u", "ng"], writes=["u"])
                P.add("act", lambda e, i=i: e.activation(out=tab[i][:], in_=u[:], func=AF.Sin, scale=2 * math.pi * (1 - 1e-6), bias=nbias[:]),
                      reads=["u", "nbias"], writes=[f"tab{i}"])
                P.add("sp", lambda e, i=i: e.dma_start(out=ropetab[i], in_=tab[i][:]), reads=[f"tab{i}"], slot=f"tabo{i}")
            P.emit()

        for l in range(n_layers):
            hsrc = x if l == 0 else hbuf
            if True:
              emit_pass_a(nc, sync, l, T, NB, NT, banks, hsrc, hbuf, ropetab, pos, w_in_sb,
                        dict(w_in=w_in, w_out=w_out, w_uq=w_uq, w_ukv=w_ukv, conv_pw=conv_pw, poolw=poolw,
                             wsT=wsT, bsT=bsT, colp=colp, rowp=rowp, cst=cst))
            is_last = (l == n_layers - 1)
            if True:
              emit_pass_b(nc, sync, l, T, NT, banks, hbuf, out, dict(w_ff1=w_ff1, w_ff2=w_ff2, rowp=rowp, fng=fng, cst=cst, w_in=w_in, w_in_sb=w_in_sb, n_layers=n_layers),
                        do_final=(is_last and final), to_out=is_last)
    return nc


def emit_pass_b(nc, sync, l, T, NT, banks, hbuf, out, W, do_final, to_out):
    NBB = T // 256
    with ExitStack() as st:
        P = Prog(nc, f"b{l}", sync)
        sb = lambda n, s, d: st.enter_context(nc.sbuf_tensor(f"B{l}_{n}", s, d))
        W1 = sb("W1", [128, 8, DFF], BF16)
        W2 = sb("W2", [128, 32, D], BF16)
        gff = sb("gff", [128, D], F32)
        gfin = sb("gfin", [128, D], F32)
        c_sb = sb("c_sb", [128, 128], F32)
        ident = sb("ident", [128, 128], BF16)
        hn = [sb(f"hn{i}", [128, D], F32) for i in range(4)]
        xn = [sb(f"xn{i}", [128, D], BF16) for i in range(4)]
        xnT = [sb(f"xnT{i}", [128, 8, 256], BF16) for i in range(2)]
        rr = [sb(f"r{i}", [128, 256], F32) for i in range(3)]
        fT = [sb(f"fT{i}", [128, 256], BF16) for i in range(3)]
        junk = sb("junk", [128, D], BF16)
        ss = [sb(f"ss{i}", [128, 1], F32) for i in range(4)]
        rs = [sb(f"rs{i}", [128, 1], F32) for i in range(4)]
        ss2 = [sb(f"ss2{i}", [128, 1], F32) for i in range(4)]
        rs2 = [sb(f"rs2{i}", [128, 1], F32) for i in range(4)]
        epsc = sb("epsc", [128, 1], F32)

        w1v = W["w_ff1"][l].rearrange("(k p) n -> p k n", p=128)
        w2v = W["w_ff2"][l].rearrange("(c p) n -> p c n", p=128)
        P.add("sp", lambda e: e.dma_start(out=c_sb[:], in_=W["cst"][:, 0:128]), writes=["c"], slot="c")
        P.add("sp", lambda e: e.dma_start(out=gff[:], in_=W["rowp"][l, 1024:2048].partition_broadcast(128)), writes=["gff"], slot="gff")
        if do_final:
            P.add("sp", lambda e: e.dma_start(out=gfin[:], in_=W["fng"][0, :].partition_broadcast(128)), writes=["gfin"], slot="gfin")
        P.add("dve", lambda e: e.tensor_copy(out=ident[:], in_=c_sb[:]), reads=["c"], writes=["ident"])
        P.add("dve", lambda e: e.memset(epsc[:], EPS), writes=["eps"])
        for g in range(8):
            P.add("pool", lambda e, g=g: e.dma_start(out=W1[:, :, g * 512:(g + 1) * 512], in_=w1v[:, :, g * 512:(g + 1) * 512]),
                  writes=[f"W1g{g}"], slot=f"w1_{g % 4}")
            P.add("pool", lambda e, g=g: e.dma_start(out=W2[:, 4 * g:4 * g + 4, :], in_=w2v[:, 4 * g:4 * g + 4, :]),
                  writes=[f"W2g{g}"], slot=f"w2_{g % 4}")

        trb7 = banks[7][:, :].bitcast(BF16)

        def prep_load(bb):
            for i in range(2):
                t = 2 * bb + i
                s = (bb % 2) * 2 + i
                P.add("sp", lambda e, t=t, s=s: e.dma_start(out=hn[s][:], in_=hbuf[t * 128:(t + 1) * 128, :]),
                      reads=[f"hd{t}"], writes=[f"hn{s}"], slot=f"hn{s}")
                P.add("act", lambda e, s=s: e.activation(out=junk[:], in_=hn[s][:], func=AF.Square, accum_out=ss[s][:]),
                      reads=[f"hn{s}"], writes=[f"ss{s}"])
                P.add("act", lambda e, s=s: e.activation(out=rs[s][:], in_=ss[s][:], func=AF.Ln, scale=1.0 / D, bias=epsc[:]),
                      reads=[f"ss{s}", "eps"], writes=[f"sd{s}"])
                P.add("act", lambda e, s=s: e.activation(out=rs[s][:], in_=rs[s][:], func=AF.Exp, scale=-0.5), reads=[f"sd{s}"], writes=[f"sd{s}"])
                P.add("dve", lambda e, s=s: e.scalar_tensor_tensor(out=xn[s][:], in0=hn[s][:], scalar=rs[s][:], in1=gff[:], op0=ALU.mult, op1=ALU.mult),
                      reads=[f"hn{s}", f"sd{s}", "gff"], writes=[f"xn{s}"])

        def prep_tr(bb):
            xt = xnT[bb % 2]
            for hb in range(2):
                for i in range(2):
                    s = (bb % 2) * 2 + i
                    for kk in range(4):
                        k = hb * 4 + kk
                        off = kk * 256 + i * 128
                        P.add("pe", lambda e, s=s, k=k, off=off: e.transpose(out=trb7[:, off:off + 128], in_=xn[s][:, k * 128:(k + 1) * 128], identity=ident[:]),
                              reads=[f"xn{s}", "ident"], writes=["bank7"])
                dst = xt[:, hb * 4:hb * 4 + 4, :].rearrange("p k t -> p (k t)")
                if hb == 0:
                    P.add("dve", lambda e, dst=dst: e.tensor_copy(out=dst, in_=trb7[:, :]), reads=["bank7"], writes=[f"xnT{bb % 2}a"])
                else:
                    P.add("act", lambda e, dst=dst: e.activation(out=dst, in_=trb7[:, :], func=AF.Copy), reads=["bank7"], writes=[f"xnT{bb % 2}b"])

        def ff1(bb, c):
            xt = xnT[bb % 2]
            q = c % 3
            pb = banks[4 + q][:, 0:256]
            for k in range(8):
                P.add("pe", lambda e, k=k, pb=pb, xt=xt, c=c: e.matmul(pb, lhsT=W1[:, k, c * 128:(c + 1) * 128], rhs=xt[:, k, :], start=(k == 0), stop=(k == 7)),
                      reads=[f"W1g{c // 4}", f"xnT{bb % 2}a", f"xnT{bb % 2}b"], writes=[f"bank{4 + q}"])
            j = c % 3
            P.add("act", lambda e, pb=pb, j=j: e.activation(out=rr[j][:], in_=pb, func=AF.Relu), reads=[f"bank{4 + q}"], writes=[f"r{j}"])
            P.add("pool", lambda e, j=j: e.tensor_tensor(out=fT[j][:], in0=rr[j][:], in1=rr[j][:], op=ALU.mult), reads=[f"r{j}"], writes=[f"fT{j}"])

        def ff2(bb, c):
            j = c % 3
            for i in range(2):
                for n in range(2):
                    P.add("pe", lambda e, i=i, n=n, j=j, c=c: e.matmul(banks[i * 2 + n][:, :], lhsT=fT[j][:, i * 128:(i + 1) * 128], rhs=W2[:, c, n * 512:(n + 1) * 512],
                                                                 start=(c == 0), stop=(c == 31)),
                          reads=[f"fT{j}", f"W2g{c // 4}"], writes=[f"bank{i * 2 + n}"])

        def epilogue(bb):
            for i in range(2):
                t = 2 * bb + i
                s = (bb % 2) * 2 + i
                for n in range(2):
                    P.add("dve", lambda e, i=i, n=n, s=s: e.tensor_tensor(out=hn[s][:, n * 512:(n + 1) * 512], in0=banks[i * 2 + n][:, :], in1=hn[s][:, n * 512:(n + 1) * 512], op=ALU.add),
                          reads=[f"bank{i * 2 + n}", f"hn{s}"], writes=[f"hn{s}"])
                if do_final:
                    P.add("act", lambda e, s=s: e.activation(out=junk[:], in_=hn[s][:], func=AF.Square, accum_out=ss2[s][:]),
                          reads=[f"hn{s}"], writes=[f"ss2{s}"])
                    P.add("act", lambda e, s=s: e.activation(out=rs2[s][:], in_=ss2[s][:], func=AF.Ln, scale=1.0 / D, bias=epsc[:]),
                          reads=[f"ss2{s}", "eps"], writes=[f"sd2{s}"])
                    P.add("act", lambda e, s=s: e.activation(out=rs2[s][:], in_=rs2[s][:], func=AF.Exp, scale=-0.5), reads=[f"sd2{s}"], writes=[f"sd2{s}"])
                    P.add("dve", lambda e, s=s: e.scalar_tensor_tensor(out=hn[s][:], in0=hn[s][:], scalar=rs2[s][:], in1=gfin[:], op0=ALU.mult, op1=ALU.mult),
                          reads=[f"hn{s}", f"sd2{s}", "gfin"], writes=[f"hn{s}"])
                dst = out if to_out else hbuf
                P.add("sp", lambda e, t=t, s=s, dst=dst: e.dma_start(out=dst[t * 128:(t + 1) * 128, :], in_=hn[s][:]),
                      reads=[f"hn{s}"], writes=[f"hd{t}"], slot=f"st{s}")

        prep_load(0)
        prep_tr(0)
        for bb in range(NBB):
            if bb == NBB // 2 and l + 1 < W["n_layers"]:
                wvn = W["w_in"][l + 1].rearrange("(k p) n -> p k n", p=128)
                for g in range(4):
                    P.add("pool", lambda e, g=g: e.dma_start(out=W["w_in_sb"][:, 2 * g:2 * g + 2, :], in_=wvn[:, 2 * g:2 * g + 2, :]), writes=[f"w_in{g}"], slot=f"wl{g}")
            for c in range(32):
                ff1(bb, c)
                if c > 0:
                    ff2(bb, c - 1)
                if c == 4 and bb + 1 < NBB:
                    prep_load(bb + 1)
                if c == 20 and bb + 1 < NBB:
                    prep_tr(bb + 1)
            ff2(bb, 31)
            epilogue(bb)
        P.emit()


def emit_pass_a(nc, sync, l, T, NB, NT, banks, hsrc, hbuf, ropetab, pos, w_in_sb, W):
    with ExitStack() as st:
        P = Prog(nc, f"a{l}", sync)
        sb = lambda n, s, d: st.enter_context(nc.sbuf_tensor(f"A{l}_{n}", s, d))
        w_krr = sb("w_krr", [128, 8, 128], BF16)
        w_out_sb = sb("w_out", [128, 8, D], BF16)
        w_uq_sb = sb("w_uq", [128, 2, 416], BF16)
        w_uqr = sb("w_uqr", [128, 2, 416], BF16)
        w_ukv_sb = sb("w_ukv", [128, 512], BF16)
        pw_sb = sb("pw", [128, 2, 256], BF16)
        w_kn = sb("w_kn", [128, 256], BF16)
        w_v = sb("w_v", [128, 256], BF16)
        poolw_sb = sb("poolw", [128, 2, 128], BF16)
        wsT_sb = sb("wsT", [128, 4, 128], BF16)
        dg = sb("dg", [128, 8, 128], BF16)
        colp = sb("colp", [128, NCOL], F32)
        gmix = sb("gmix", [128, D], F32)
        ggm = sb("ggm", [128, 256], F32)
        bsT = sb("bsT", [128, 2, 128], F32)
        c_sb = sb("c_sb", [128, NCST], F32)
        ident = sb("ident", [128, 128], BF16)
        maskT = sb("maskT", [128, 128], BF16)
        ones = sb("ones", [128, 128], BF16)
        epsc = sb("epsc", [128, 1], F32)
        onec = sb("onec", [128, 1], F32)
        KT = sb("KT", [128, 4, T], BF16)
        Vt = sb("Vt", [128, NT, 4, 65], BF16)
        ycv = sb("ycv", [128, 2, 542], BF16)
        zp = sb("zp", [128, 2, 527], F32)
        hn = [sb(f"hn{i}", [128, D], F32) for i in range(2)]
        xn = [sb(f"xn{i}", [128, D], BF16) for i in range(4)]
        hr = [sb(f"hr{i}", [128, D], F32) for i in range(2)]
        xnT = sb("xnT", [128, 8, 512], BF16)
        mixedT = sb("mixedT", [128, 8, 512], BF16)
        QT = sb("QT", [128, 4, 512], BF16)
        PT = [sb(f"PT{i}", [128, 512], BF16) for i in range(3)]
        cos2 = sb("cos2", [128, 512], F32)
        sin2 = sb("sin2", [128, 512], F32)
        junk = sb("junk", [128, 256], BF16)
        Fm = [sb(f"F{i}", [128, 512], F32) for i in range(10)]
        Hm = [sb(f"H{i}", [128, 512], BF16) for i in range(6)]
        PA = sb("PA", [128, 527], F32)
        PB = sb("PB", [128, 527], F32)
        omla = sb("omla", [128, 4, 256], F32)
        vn = sb("vn", [128, 4, 256], BF16)
        mtok = [sb(f"mtok{i}", [128, 256], BF16) for i in range(2)]
        sm = sb("sm", [128, 64], F32)

        def F(i):
            return Fm[i], f"F{i}"

        def H(i):
            return Hm[i], f"H{i}"

        P.add("sp", lambda e: e.dma_start(out=c_sb[:], in_=W["cst"]), writes=["c"], slot="c")
        P.add("sp", lambda e: e.dma_start(out=colp[:], in_=W["colp"][l]), writes=["colp"], slot="colp")
        P.add("sp", lambda e: e.dma_start(out=gmix[:], in_=W["rowp"][l, 0:1024].partition_broadcast(128)), writes=["gmix"], slot="gmix")
        P.add("sp", lambda e: e.dma_start(out=ggm[:], in_=W["rowp"][l, 2048:2304].partition_broadcast(128)), writes=["ggm"], slot="ggm")
        P.add("sp", lambda e: e.dma_start(out=bsT[:], in_=W["bsT"][l]), writes=["bsT"], slot="bsT")
        P.add("dve", lambda e: e.tensor_copy(out=ident[:], in_=c_sb[:, 0:128]), reads=["c"], writes=["ident"])
        P.add("dve", lambda e: e.tensor_copy(out=maskT[:], in_=c_sb[:, 128:256]), reads=["c"], writes=["maskT"])
        P.add("dve", lambda e: e.memset(ones[:], 1.0), writes=["ones"])
        P.add("dve", lambda e: e.memset(epsc[:], EPS), writes=["eps"])
        P.add("dve", lambda e: e.memset(onec[:], 1.0), writes=["eps"])
        P.add("pool", lambda e: e.memset(KT[96:128, :, :], 0.0), writes=["KTpad"])
        P.add("pool", lambda e: e.memset(QT[96:128, :, :], 0.0), writes=["QTpad"])
        P.add("pool", lambda e: e.memset(w_uq_sb[:, :, 384:416], 0.0), writes=["w_uq"])
        P.add("pool", lambda e: e.memset(w_uq_sb[64:128, 1, :], 0.0), writes=["w_uq"])
        P.add("pool", lambda e: e.memset(w_uqr[:], 0.0), writes=["w_uqr"])
        P.add("pool", lambda e: e.memset(ycv[:], 0.0), writes=["ycvh0", "ycvh1", "ycv0", "ycv1"])
        P.add("pool", lambda e: e.memset(zp[:], 0.0), writes=["zph0", "zph1", "zp0", "zp1"])
        P.add("pool", lambda e: e.memset(Vt[:], 1.0), writes=["Vinit"])
        P.add("pool", lambda e: e.memset(w_krr[:], 0.0), writes=["w_krr"])
        wv = W["w_in"][l].rearrange("(k p) n -> p k n", p=128)
        for g in range(4 if l == 0 else 0):
            P.add("pool", lambda e, g=g: e.dma_start(out=w_in_sb[:, 2 * g:2 * g + 2, :], in_=wv[:, 2 * g:2 * g + 2, :]), writes=[f"w_in{g}"], slot=f"wl{g}")
        P.add("pool", lambda e: e.dma_start(out=w_uq_sb[:, 0, 0:384], in_=W["w_uq"][l, 0:128, :]), writes=["w_uq"], slot="wl0")
        P.add("pool", lambda e: e.dma_start(out=w_uq_sb[0:64, 1, 0:384], in_=W["w_uq"][l, 128:192, :]), writes=["w_uq"], slot="wl1")
        P.add("pool", lambda e: e.dma_start(out=w_ukv_sb[:], in_=W["w_ukv"][l]), writes=["w_ukv"], slot="wl2")
        P.add("pool", lambda e: e.dma_start(out=pw_sb[:], in_=W["conv_pw"][l].rearrange("(k p) n -> p k n", p=128)), writes=["pw"], slot="wl3")
        P.add("pool", lambda e: e.dma_start(out=poolw_sb[:], in_=W["poolw"][l]), writes=["poolw"], slot="wl0")
        P.add("pool", lambda e: e.dma_start(out=wsT_sb[:], in_=W["wsT"][l]), writes=["wsT"], slot="wl1")
        wov = W["w_out"][l].rearrange("(k p) n -> p k n", p=128)
        for g in range(2):
            P.add("pool", lambda e, g=g: e.dma_start(out=w_out_sb[:, 4 * g:4 * g + 4, :], in_=wov[:, 4 * g:4 * g + 4, :]), writes=[f"w_out{g}"], slot=f"wl{2 + g}")
        WIN = [f"w_in{g}" for g in range(4)]
        WOUT = ["w_out0", "w_out1"]
        ukv4 = w_ukv_sb[:, :].rearrange("p (h d) -> p h d", h=4)
        P.add("act", lambda e: e.activation(out=w_kn[:, :].rearrange("p (h d) -> p h d", h=4), in_=ukv4[:, :, 0:64], func=AF.Copy), reads=["w_ukv"], writes=["w_kn"])
        P.add("act", lambda e: e.activation(out=w_v[:, :].rearrange("p (h d) -> p h d", h=4), in_=ukv4[:, :, 64:128], func=AF.Copy), reads=["w_ukv"], writes=["w_v"])
        for h in range(4):
            P.add("dve", lambda e, h=h: e.tensor_tensor(out=wsT_sb[:, h, :], in0=wsT_sb[:, h, :], in1=maskT[:], op=ALU.mult),
                  reads=["wsT", "maskT"], writes=["wsT"])
        for kc in range(2):
            pr = slice(0, 128) if kc == 0 else slice(0, 64)
            src = w_uq_sb[pr, kc, 0:384].rearrange("p (h d) -> p h d", h=4)
            dst = w_uqr[pr, kc, 0:384].rearrange("p (h d) -> p h d", h=4)
            P.add("act", lambda e, src=src, dst=dst: e.mul(out=dst[:, :, 64:80], in_=src[:, :, 80:96], mul=-1.0), reads=["w_uq", "w_uqr"], writes=["w_uqr"])
            P.add("act", lambda e, src=src, dst=dst: e.activation(out=dst[:, :, 80:96], in_=src[:, :, 64:80], func=AF.Copy), reads=["w_uq", "w_uqr"], writes=["w_uqr"])
            P.add("act", lambda e, src=src, dst=dst: e.activation(out=dst[:, :, 0:64], in_=src[:, :, 0:64], func=AF.Copy), reads=["w_uq", "w_uqr"], writes=["w_uqr"])
        P.add("act", lambda e: e.mul(out=w_krr[:, :, 64:80], in_=w_in_sb[:, :, 848:864], mul=-1.0), reads=WIN + ["w_krr"], writes=["w_krr"])
        P.add("act", lambda e: e.activation(out=w_krr[:, :, 80:96], in_=w_in_sb[:, :, 832:848], func=AF.Copy), reads=WIN + ["w_krr"], writes=["w_krr"])
        gen3 = Rot([(banks[i], f"bank{i}") for i in range(3)])
        gen1 = [Rot([(banks[i], f"bank{i}")]) for i in range(3)]
        genh = [gen3]
        stb = Rot([(banks[i], f"bank{i}") for i in range(3, 6)])
        pvb = Rot([(banks[i], f"bank{i}") for i in range(6, 8)])
        ptr = Rot([(PT[i], f"PT{i}") for i in range(3)])
        SCALE = 96.0 ** -0.5

        def mm_group(pout, pres, parts, extra_reads=()):
            n = len(parts)
            for i, (lt, rh, rd) in enumerate(parts):
                P.add("pe", lambda e, lt=lt, rh=rh, i=i: e.matmul(pout, lhsT=lt, rhs=rh, start=(i == 0), stop=(i == n - 1)),
                      reads=list(rd) + list(extra_reads), writes=[pres])

        def rstd_bcast(sq_parts, scale, dst, dres):
            bk, br = genh[0].next()
            mm_group(bk[:, :], br, [(ones[0:sq.shape[0], :], sq, rd + ["ones"]) for sq, rd in sq_parts])
            P.add("act", lambda e: e.activation(out=dst, in_=bk[:, :], func=AF.Ln, scale=scale, bias=epsc[:]), reads=[br, "eps"], writes=[dres])
            P.add("act", lambda e: e.activation(out=dst, in_=dst, func=AF.Exp, scale=-0.5), reads=[dres], writes=[dres])

        def group_norm_fm(g, of, ofres, sqt, rtt):
            sqs = []
            for c in range(2):
                hq, hres = sqt[c]
                hres = hres if isinstance(hres, list) else [hres]
                P.add("pool", lambda e, c=c, hq=hq: e.tensor_tensor(out=hq, in0=of[c], in1=of[c], op=ALU.mult), reads=[ofres[c]], writes=hres)
                sqs.append((hq, hres))
            rt, rres = rtt
            rstd_bcast(sqs, 1.0 / 256, rt, rres)
            for c in range(2):
                P.add("dve", lambda e, c=c: e.scalar_tensor_tensor(out=mixedT[:, 2 * g + c, :], in0=of[c], scalar=colp[:, 75 + 2 * g + c:76 + 2 * g + c], in1=rt,
                                                                  op0=ALU.mult, op1=ALU.mult),
                      reads=[ofres[c], rres, "colp"], writes=[f"mixedT{2 * g + c}"])

        genpre = Rot([(banks[i], f"bank{i}") for i in (1, 2, 3, 4)])
        genpost = Rot([(banks[i], f"bank{i}") for i in (0, 5)])
        pre_ops, main_ops, post_ops, prenorm_ops = [], [], [], []
        for b in range(NB):
            tb = b * 512
            P.cap = []
            genh[0] = genpre
            P.add("sp", lambda e, tb=tb: e.dma_start(out=cos2[64:96, :], in_=ropetab[0, :, tb:tb + 512]), writes=["cos2"], slot="cos2")
            P.add("sp", lambda e, tb=tb: e.dma_start(out=sin2[64:96, :], in_=ropetab[1, :, tb:tb + 512]), writes=["sin2"], slot="sin2")
            prenorm_cap = P.cap
            P.cap = []
            for ip in range(2):
                for i in (2 * ip, 2 * ip + 1):
                    t = 4 * b + i
                    s = i % 2
                    P.add("sp", lambda e, t=t, s=s: e.dma_start(out=hn[s][:], in_=hsrc[t * 128:(t + 1) * 128, :]), reads=[f"hd{t}"], writes=[f"hn{s}"], slot=f"hn{s}")
                    P.add("act", lambda e, s=s, i=i: e.activation(out=xn[i][:], in_=hn[s][:], func=AF.Square, accum_out=sm[:, 40 + i:41 + i]), reads=[f"hn{s}"], writes=[f"ss{i}", f"xn{i}"])
                lo = 2 * ip
                P.add("act", lambda e, lo=lo: e.activation(out=sm[:, 44 + lo:46 + lo], in_=sm[:, 40 + lo:42 + lo], func=AF.Ln, scale=1.0 / D, bias=epsc[:]),
                      reads=[f"ss{lo}", f"ss{lo + 1}", "eps"], writes=[f"sdp{ip}"])
                P.add("act", lambda e, lo=lo: e.activation(out=sm[:, 44 + lo:46 + lo], in_=sm[:, 44 + lo:46 + lo], func=AF.Exp, scale=-0.5), reads=[f"sdp{ip}"], writes=[f"sdp{ip}"])
                for i in (2 * ip, 2 * ip + 1):
                    s = i % 2
                    P.add("dve", lambda e, s=s, i=i: e.scalar_tensor_tensor(out=xn[i][:], in0=hn[s][:], scalar=sm[:, 44 + i:45 + i], in1=gmix[:], op0=ALU.mult, op1=ALU.mult),
                          reads=[f"hn{s}", f"sdp{ip}", "gmix"], writes=[f"xn{i}"])
            prenorm_ops.append(P.cap)
            P.cap = prenorm_cap
            for i in range(4):
                for hb in range(2):
                    bk, br = genh[0].next()
                    bv = bk[:, :].bitcast(BF16)
                    for kk in range(4):
                        k = hb * 4 + kk
                        P.add("pe", lambda e, i=i, k=k, kk=kk, bv=bv: e.transpose(out=bv[:, kk * 128:(kk + 1) * 128], in_=xn[i][:, k * 128:(k + 1) * 128], identity=ident[:]),
                              reads=[f"xn{i}", "ident"], writes=[br])
                    eng = "dve" if hb == 0 else "act"
                    dst = xnT[:, hb * 4:hb * 4 + 4, i * 128:(i + 1) * 128]
                    src = bv[:, 0:512].rearrange("p (k t) -> p k t", k=4)
                    if eng == "dve":
                        P.add("dve", lambda e, dst=dst, src=src: e.tensor_copy(out=dst, in_=src), reads=[br], writes=[f"xnT{i}"])
                    else:
                        P.add("act", lambda e, dst=dst, src=src: e.activation(out=dst, in_=src, func=AF.Copy), reads=[br], writes=[f"xnT{i}"])
            XNT = [f"xnT{i}" for i in range(4)]

            def inproj(c0, c1, lw=None):
                bk, br = genh[0].next()
                m = c1 - c0
                parts = []
                for k in range(8):
                    lt = (w_in_sb[:, k, c0:c1] if lw is None else lw[:, k, :])
                    parts.append((lt, xnT[:, k, :], XNT + WIN + (["w_krr"] if lw is not None else [])))
                mm_group(bk[0:m, :], br, parts)
                return bk, br

            craw = [F(0), F(1), F(2)]
            csq = [H(0), H(1), H(2)]
            spans = [(512, 640), (640, 704), (704, 832)]
            for j, (c0, c1) in enumerate(spans):
                m = c1 - c0
                bk, br = inproj(c0, c1)
                ft, fr = craw[j]
                P.add("act", lambda e, bk=bk, ft=ft, m=m: e.activation(out=ft[0:m, :], in_=bk[0:m, :], func=AF.Copy), reads=[br], writes=[fr])
                ht, hres = csq[j]
                P.add("pool", lambda e, ft=ft, ht=ht, m=m: e.tensor_tensor(out=ht[0:m, :], in0=ft[0:m, :], in1=ft[0:m, :], op=ALU.mult), reads=[fr], writes=[hres])
            rq, rqres = F(3)
            rstd_bcast([(csq[0][0][:, :], [csq[0][1]]), (csq[1][0][0:64, :], [csq[1][1]])], 1.0 / 192, rq[:], rqres)
            rkv, rkvres = F(4)
            rstd_bcast([(csq[2][0][:, :], [csq[2][1]])], 1.0 / 128, rkv[:], rkvres)
            cqn = [H(3), H(4), H(5)]
            gcols = [68, 69, 70]
            for j in range(3):
                m = spans[j][1] - spans[j][0]
                ft, fr = craw[j]
                ht, hres = cqn[j]
                rt, rres = (rq, rqres) if j < 2 else (rkv, rkvres)
                P.add("dve", lambda e, ft=ft, ht=ht, rt=rt, m=m, gc=gcols[j]: e.scalar_tensor_tensor(out=ht[0:m, :], in0=ft[0:m, :], scalar=colp[0:m, gc:gc + 1], in1=rt[0:m, :],
                                                                                                   op0=ALU.mult, op1=ALU.mult),
                      reads=[fr, rres, "colp"], writes=[hres])
            for h in range(4):
                pa, par = genh[0].next()
                mm_group(pa[:, :], par, [(w_uq_sb[:, 0, h * 96:h * 96 + 128], cqn[0][0][:, :], [cqn[0][1], "w_uq"]),
                                         (w_uq_sb[0:64, 1, h * 96:h * 96 + 128], cqn[1][0][0:64, :], [cqn[1][1], "w_uq"])])
                pbk, pbr = genh[0].next()
                mm_group(pbk[:, :], pbr, [(w_uqr[:, 0, h * 96:h * 96 + 128], cqn[0][0][:, :], [cqn[0][1], "w_uqr"]),
                                          (w_uqr[0:64, 1, h * 96:h * 96 + 128], cqn[1][0][0:64, :], [cqn[1][1], "w_uqr"])])
                P.add("act", lambda e, pa=pa, h=h: e.activation(out=QT[0:64, h, :], in_=pa[0:64, :], func=AF.Copy), reads=[par], writes=[f"QT{h}"])
                t1, t1r = F(5)
                t2, t2r = F(6)
                P.add("dve", lambda e, pa=pa, t1=t1: e.tensor_tensor(out=t1[64:96, :], in0=pa[64:96, :], in1=cos2[64:96, :], op=ALU.mult), reads=[par, "cos2"], writes=[t1r])
                P.add("dve", lambda e, pbk=pbk, t2=t2: e.tensor_tensor(out=t2[64:96, :], in0=pbk[64:96, :], in1=sin2[64:96, :], op=ALU.mult), reads=[pbr, "sin2"], writes=[t2r])
                P.add("pool", lambda e, t1=t1, t2=t2, h=h: e.tensor_tensor(out=QT[64:96, h, :], in0=t1[64:96, :], in1=t2[64:96, :], op=ALU.add), reads=[t1r, t2r], writes=[f"QT{h}r"])
            for hp in range(2):
                bk, br = genh[0].next()
                lt = w_kn[:, hp * 128:(hp + 1) * 128]
                mm_group(bk[:, :], br, [(lt, cqn[2][0][:, :], [cqn[2][1], "w_kn"])])
                P.add("act", lambda e, bk=bk, hp=hp, tb=tb: e.activation(out=KT[0:64, 2 * hp, tb:tb + 512], in_=bk[0:64, :], func=AF.Copy), reads=[br], writes=[f"KT{b}_{2 * hp}"])
                P.add("dve", lambda e, bk=bk, hp=hp, tb=tb: e.tensor_copy(out=KT[0:64, 2 * hp + 1, tb:tb + 512], in_=bk[64:128, :]), reads=[br], writes=[f"KT{b}_{2 * hp + 1}"])
            ka, kar = inproj(768, 896)
            kb, kbr = inproj(0, 128, lw=w_krr)
            t1, t1r = F(5)
            t2, t2r = F(6)
            P.add("dve", lambda e, ka=ka, t1=t1: e.tensor_tensor(out=t1[64:96, :], in0=ka[64:96, :], in1=cos2[64:96, :], op=ALU.mult), reads=[kar, "cos2"], writes=[t1r])
            P.add("dve", lambda e, kb=kb, t2=t2: e.tensor_tensor(out=t2[64:96, :], in0=kb[64:96, :], in1=sin2[64:96, :], op=ALU.mult), reads=[kbr, "sin2"], writes=[t2r])
            for h in range(4):
                P.add("pool", lambda e, t1=t1, t2=t2, h=h, tb=tb: e.tensor_tensor(out=KT[64:96, h, tb:tb + 512], in0=t1[64:96, :], in1=t2[64:96, :], op=ALU.add),
                      reads=[t1r, t2r], writes=[f"KT{b}_{h}r"])
            for i2 in range(2):
                bk, br = genh[0].next()
                for ii in range(2):
                    i = 2 * i2 + ii
                    rh = w_v[:, :]
                    P.add("pe", lambda e, bk=bk, ii=ii, i=i, rh=rh: e.matmul(bk[:, ii * 256:(ii + 1) * 256], lhsT=cqn[2][0][:, i * 128:(i + 1) * 128], rhs=rh, start=True, stop=True),
                          reads=[cqn[2][1], "w_v"], writes=[br])
                for ii in range(2):
                    i = 2 * i2 + ii
                    dst = Vt[:, 4 * b + i, :, 0:64]
                    src = bk[:, ii * 256:(ii + 1) * 256].rearrange("p (h d) -> p h d", h=4)
                    P.add("act", lambda e, dst=dst, src=src: e.activation(out=dst, in_=src, func=AF.Copy), reads=[br, "Vinit"], writes=[f"V{4 * b + i}"])

            pre_ops.append(P.cap)
            P.cap = None
            nk = 4 * b + 4
            units = [(h, kt) for h in range(4) for kt in range(nk)]
            accs = {}
            pts = {}
            LA = 2

            def emit_s(h, kt):
                j0 = max(0, kt - 4 * b)
                q0 = j0 * 128
                sbk, sbr = stb.next()
                P.add("pe", lambda e, sbk=sbk, h=h, kt=kt, q0=q0: e.matmul(sbk[:, q0:512], lhsT=KT[:, h, kt * 128:(kt + 1) * 128], rhs=QT[:, h, q0:512], start=True, stop=True),
                      reads=[f"KT{kt // 4}_{h}", f"KT{kt // 4}_{h}r", f"QT{h}", f"QT{h}r", "KTpad", "QTpad"], writes=[sbr])
                pt, ptres = ptr.next()
                P.add("act", lambda e, sbk=sbk, pt=pt, q0=q0: e.activation(out=pt[:, q0:512], in_=sbk[:, q0:512], func=AF.Exp, scale=SCALE), reads=[sbr], writes=[ptres])
                if kt >= 4 * b:
                    P.add("pool", lambda e, pt=pt, q0=q0: e.tensor_tensor(out=pt[:, q0:q0 + 128], in0=pt[:, q0:q0 + 128], in1=maskT[:], op=ALU.mult),
                          reads=[ptres, "maskT"], writes=[ptres])
                pts[(h, kt)] = (pt, ptres, j0)

            def emit_pv(h, kt):
                if kt == 0:
                    accs[h] = pvb.next()
                acc, accr = accs[h]
                accv = acc[:, 0:260].rearrange("p (j d) -> p j d", j=4)
                pt, ptres, j0 = pts.pop((h, kt))
                for j in range(j0, 4):
                    first = (kt == 0 and j == j0)
                    P.add("pe", lambda e, pt=pt, j=j, kt=kt, h=h, first=first, accv=accv: e.matmul(accv[:, j, :], lhsT=pt[:, j * 128:(j + 1) * 128], rhs=Vt[:, kt, h, :],
                                                                                                start=first, stop=(kt == nk - 1 and j == 3), skip_group_check=True),
                          reads=[ptres, f"V{kt}", "Vinit"], writes=[accr])
                if kt == nk - 1:
                    P.add("dve", lambda e, accv=accv, h=h: e.reciprocal(out=sm[:, 8 + 4 * h:12 + 4 * h], in_=accv[:, :, 64]), reads=[accr], writes=[f"rec{h}"])
                    for j in range(4):
                        P.add("dve", lambda e, accv=accv, h=h, j=j: e.tensor_scalar(out=omla[:, j, h * 64:(h + 1) * 64], in0=accv[:, j, 0:64], scalar1=sm[:, 8 + 4 * h + j:9 + 4 * h + j],
                                                                                    scalar2=None, op0=ALU.mult),
                              reads=[accr, f"rec{h}"], writes=[f"omla{j}"])

            att_units = []
            for idx in range(len(units) + LA):
                P.cap = []
                if idx < len(units):
                    emit_s(*units[idx])
                if idx - LA >= 0:
                    emit_pv(*units[idx - LA])
                att_units.append(P.cap)
                P.cap = None
            streams = []
            def emit_mla_out():
                bk, br = genh[0].next()
                bv = bk[:, :].bitcast(BF16)
                for j in range(4):
                    P.add("act", lambda e, j=j: e.activation(out=junk[:, 0:256], in_=omla[:, j, :], func=AF.Square, accum_out=sm[:, 24 + j:25 + j]), reads=[f"omla{j}"], writes=[f"oss{j}"])
                P.add("act", lambda e: e.activation(out=sm[:, 28:32], in_=sm[:, 24:28], func=AF.Ln, scale=1.0 / 256, bias=epsc[:]), reads=[f"oss{j}" for j in range(4)] + ["eps"], writes=["osd"])
                P.add("act", lambda e: e.activation(out=sm[:, 28:32], in_=sm[:, 28:32], func=AF.Exp, scale=-0.5), reads=["osd"], writes=["osd"])
                for j in range(4):
                    mt = mtok[j % 2]
                    P.add("dve", lambda e, j=j, mt=mt: e.scalar_tensor_tensor(out=mt[:], in0=omla[:, j, :], scalar=sm[:, 28 + j:29 + j], in1=ggm[:], op0=ALU.mult, op1=ALU.mult),
                          reads=[f"omla{j}", "osd", "ggm"], writes=[f"mtok{j % 2}"])
                    for c in range(2):
                        P.add("pe", lambda e, mt=mt, c=c, j=j, bv=bv: e.transpose(out=bv[:, c * 512 + j * 128:c * 512 + (j + 1) * 128], in_=mt[:, c * 128:(c + 1) * 128], identity=ident[:]),
                              reads=[f"mtok{j % 2}", "ident"], writes=[br])
                P.add("act", lambda e, bv=bv: e.activation(out=mixedT[:, 2:4, :].rearrange("p c t -> p (c t)"), in_=bv[:, :], func=AF.Copy), reads=[br], writes=["mixedT2", "mixedT3"])

            P.cap = []
            genh[0] = gen1[0]
            for c in range(2):
                gb, gbr = inproj(256 + c * 128, 256 + (c + 1) * 128)
                sg, sgr = F(0)
                P.add("act", lambda e, gb=gb, sg=sg: e.activation(out=sg[:], in_=gb[:, :], func=AF.Exp, scale=-1.0), reads=[gbr], writes=[sgr])
                P.add("act", lambda e, sg=sg: e.activation(out=sg[:], in_=sg[:], func=AF.Ln, bias=onec[:]), reads=[sgr, "eps"], writes=[sgr])
                P.add("act", lambda e, sg=sg: e.activation(out=sg[:], in_=sg[:], func=AF.Exp, scale=-1.0), reads=[sgr], writes=[sgr])
                ab, abr = inproj(c * 128, (c + 1) * 128)
                P.add("dve", lambda e, ab=ab, sg=sg, c=c: e.tensor_tensor(out=ycv[:, c, 30:542], in0=ab[:, :], in1=sg[:], op=ALU.mult), reads=[abr, sgr], writes=[f"ycv{c}"])
            ycf = [F(1), F(2)]
            ycb = [H(0), H(1)]
            ysq = [H(2), H(3)]
            for c in range(2):
                bk, br = genh[0].next()
                for k in range(31):
                    ds = (c * 31 + k) % 8
                    P.add("dve", lambda e, c=c, k=k, ds=ds: e.tensor_scalar(out=dg[:, ds, :], in0=ident[:], scalar1=colp[:, c * 31 + k:c * 31 + k + 1], scalar2=None, op0=ALU.mult),
                          reads=["ident", "colp"], writes=[f"dg{ds}"])
                    P.add("pe", lambda e, bk=bk, c=c, k=k, ds=ds: e.matmul(bk[:, :], lhsT=dg[:, ds, :], rhs=ycv[:, c, k:k + 512], start=(k == 0), stop=(k == 30)),
                          reads=[f"dg{ds}", f"ycv{c}", f"ycvh{c}"], writes=[br])
                ft, fr = ycf[c]
                P.add("dve", lambda e, bk=bk, ft=ft, c=c: e.tensor_scalar(out=ft[:], in0=bk[:, :], scalar1=colp[:, 62 + c:63 + c], scalar2=None, op0=ALU.add), reads=[br, "colp"], writes=[fr])
                P.add("pool", lambda e, c=c: e.tensor_copy(out=ycv[:, c, 0:30], in_=ycv[:, c, 512:542]), reads=[f"ycv{c}"], writes=[f"ycvh{c}"])
                hb_, hbr = ycb[c]
                P.add("pool", lambda e, ft=ft, hb_=hb_: e.tensor_copy(out=hb_[:], in_=ft[:]), reads=[fr], writes=[hbr])
                hq, hqr = ysq[c]
                P.add("pool", lambda e, ft=ft, hq=hq: e.tensor_tensor(out=hq[:], in0=ft[:], in1=ft[:], op=ALU.mult), reads=[fr], writes=[hqr])
            mt_, mtr = F(3)
            m2, m2r = F(4)
            vr, vrr = F(5)
            mb, mbr = genh[0].next()
            mm_group(mb[:, :], mbr, [(ones[:, :], ycb[c][0][:, :], [ycb[c][1], "ones"]) for c in range(2)])
            P.add("dve", lambda e, mb=mb, mt_=mt_: e.tensor_scalar(out=mt_[:], in0=mb[:, :], scalar1=1.0 / 256, scalar2=None, op0=ALU.mult), reads=[mbr], writes=[mtr])
            qb, qbr = genh[0].next()
            mm_group(qb[:, :], qbr, [(ones[:, :], ysq[c][0][:, :], [ysq[c][1], "ones"]) for c in range(2)])
            P.add("pool", lambda e, mt_=mt_, m2=m2: e.tensor_tensor(out=m2[:], in0=mt_[:], in1=mt_[:], op=ALU.mult), reads=[mtr], writes=[m2r])
            P.add("dve", lambda e, qb=qb, m2=m2, vr=vr: e.scalar_tensor_tensor(out=vr[:], in0=qb[:, :], scalar=1.0 / 256, in1=m2[:], op0=ALU.mult, op1=ALU.subtract),
                  reads=[qbr, m2r], writes=[vrr])
            P.add("act", lambda e, vr=vr: e.activation(out=vr[:], in_=vr[:], func=AF.Ln, bias=epsc[:]), reads=[vrr, "eps"], writes=[vrr])
            P.add("act", lambda e, vr=vr: e.activation(out=vr[:], in_=vr[:], func=AF.Exp, scale=-0.5), reads=[vrr], writes=[vrr])
            sact = [H(0), H(1)]
            for c in range(2):
                ft, fr = ycf[c]
                P.add("dve", lambda e, ft=ft, mt_=mt_: e.tensor_tensor(out=ft[:], in0=ft[:], in1=mt_[:], op=ALU.subtract), reads=[fr, mtr], writes=[fr])
                P.add("pool", lambda e, ft=ft, vr=vr: e.tensor_tensor(out=ft[:], in0=ft[:], in1=vr[:], op=ALU.mult), reads=[fr, vrr], writes=[fr])
                ht, hres = sact[c]
                sg, sgr = F(0)
                P.add("dve", lambda e, ft=ft, c=c: e.tensor_scalar(out=ft[:], in0=ft[:], scalar1=colp[:, 64 + c:65 + c], scalar2=colp[:, 66 + c:67 + c], op0=ALU.mult, op1=ALU.add),
                      reads=[fr, "colp"], writes=[fr])
                P.add("act", lambda e, ft=ft, sg=sg: e.activation(out=sg[:], in_=ft[:], func=AF.Exp, scale=-1.0), reads=[fr], writes=[sgr])
                P.add("act", lambda e, sg=sg: e.activation(out=sg[:], in_=sg[:], func=AF.Ln, bias=onec[:]), reads=[sgr, "eps"], writes=[sgr])
                P.add("act", lambda e, sg=sg: e.activation(out=sg[:], in_=sg[:], func=AF.Exp, scale=-1.0), reads=[sgr], writes=[sgr])
                P.add("pool", lambda e, ft=ft, sg=sg, ht=ht: e.tensor_tensor(out=ht[:], in0=ft[:], in1=sg[:], op=ALU.mult), reads=[fr, sgr], writes=[hres])
            of = [F(3), F(4)]
            for co in range(2):
                bk, br = genh[0].next()
                mm_group(bk[:, :], br, [(pw_sb[:, ci, co * 128:(co + 1) * 128], sact[ci][0][:, :], [sact[ci][1], "pw"]) for ci in range(2)])
                ft, fr = of[co]
                P.add("dve", lambda e, bk=bk, ft=ft: e.tensor_copy(out=ft[:], in_=bk[:, :]), reads=[br], writes=[fr])
            group_norm_fm(0, [of[0][0][:], of[1][0][:]], [of[0][1], of[1][1]], [(Hm[2][:], "H2"), (Hm[3][:], "H3")], (Fm[5][:], "F5"))

            streams.append(P.cap)
            P.cap = []
            genh[0] = gen1[1]
            yp = [H(4), H(5)]
            for c in range(2):
                pbk, pbr = inproj(864 + c * 128, 864 + (c + 1) * 128)
                P.add("dve", lambda e, pbk=pbk, c=c: e.tensor_copy(out=zp[:, c, 15:527], in_=pbk[:, :]), reads=[pbr], writes=[f"zp{c}"])
                zc = zp[:, c, :]
                P.add("pool", lambda e, zc=zc: e.tensor_tensor(out=PA[:, 1:527], in0=zc[:, 1:527], in1=zc[:, 0:526], op=ALU.add), reads=[f"zp{c}", f"zph{c}"], writes=["PA"])
                P.add("pool", lambda e: e.tensor_tensor(out=PB[:, 3:527], in0=PA[:, 3:527], in1=PA[:, 1:525], op=ALU.add), reads=["PA"], writes=["PB"])
                if c == 0:
                    lo, hi = PA, PB
                    lor, hir = "PA", "PB"
                else:
                    P.add("pool", lambda e: e.tensor_tensor(out=PA[:, 7:527], in0=PB[:, 7:527], in1=PB[:, 3:523], op=ALU.add), reads=["PB", "PA"], writes=["PA"])
                    P.add("pool", lambda e: e.tensor_tensor(out=PB[:, 15:527], in0=PA[:, 15:527], in1=PA[:, 7:519], op=ALU.add), reads=["PA", "PB"], writes=["PB"])
                    lo, hi = PA, PB
                    lor, hir = "PA", "PB"
                ht, hres = yp[c]
                for (pr, srcT, srcr) in ((slice(0, 64), lo, lor), (slice(64, 128), hi, hir)):
                    P.add("dve", lambda e, pr=pr, srcT=srcT, c=c, ht=ht, zc=zc: e.scalar_tensor_tensor(out=ht[pr, :], in0=srcT[pr, 15:527], scalar=c_sb[pr, 288 + c:289 + c], in1=zc[pr, 15:527],
                                                                                                     op0=ALU.mult, op1=ALU.subtract),
                          reads=[srcr, f"zp{c}", "c"], writes=[hres])
                    if b == 0:
                        tt, ttr = F(6)
                        P.add("dve", lambda e, pr=pr, srcT=srcT, c=c, tt=tt: e.tensor_tensor(out=tt[pr, 0:16], in0=srcT[pr, 15:31], in1=c_sb[pr, 256 + c * 16:272 + c * 16], op=ALU.mult),
                              reads=[srcr, "c"], writes=[ttr])
                        P.add("dve", lambda e, pr=pr, c=c, tt=tt, ht=ht, zc=zc: e.tensor_tensor(out=ht[pr, 0:16], in0=tt[pr, 0:16], in1=zc[pr, 15:31], op=ALU.subtract),
                              reads=[ttr, f"zp{c}", hres], writes=[hres])
                P.add("pool", lambda e, c=c: e.tensor_copy(out=zp[:, c, 0:15], in_=zp[:, c, 512:527]), reads=[f"zp{c}"], writes=[f"zph{c}"])
            of = [(PA[:, 0:512], "PA"), (PB[:, 0:512], "PB")]
            for c in range(2):
                bk, br = genh[0].next()
                mm_group(bk[:, :], br, [(poolw_sb[:, c, :], yp[c][0][:, :], [yp[c][1], "poolw"])])
                ft, fr = of[c]
                P.add("dve", lambda e, bk=bk, ft=ft, c=c: e.tensor_scalar(out=ft, in0=bk[:, :], scalar1=colp[:, 71 + c:72 + c], scalar2=None, op0=ALU.mult), reads=[br, "colp"], writes=[fr])
            group_norm_fm(2, [of[0][0], of[1][0]], [of[0][1], of[1][1]], [(Hm[4][:], "H4"), (Hm[5][:], "H5")], (Fm[6][:], "F6"))

            streams.append(P.cap)
            P.cap = []
            genh[0] = gen1[2]
            for i2 in range(2):
                bk, br = genh[0].next()
                for ii in range(2):
                    i = 2 * i2 + ii
                    for k in range(8):
                        P.add("pe", lambda e, bk=bk, ii=ii, i=i, k=k: e.matmul(bk[:, ii * 256:(ii + 1) * 256], lhsT=xnT[:, k, i * 128:(i + 1) * 128], rhs=w_in_sb[:, k, 1376:1632],
                                                                            start=(k == 0), stop=(k == 7)),
                              reads=XNT + WIN, writes=[br])
                for ii in range(2):
                    i = 2 * i2 + ii
                    src = bk[:, ii * 256:(ii + 1) * 256]
                    P.add("act", lambda e, src=src, i=i: e.activation(out=junk[:, 0:256], in_=src, func=AF.Square, accum_out=sm[:, 32 + i:33 + i]), reads=[br], writes=[f"vss{i}"])
                lo = 2 * i2
                P.add("act", lambda e, lo=lo: e.activation(out=sm[:, 36 + lo:38 + lo], in_=sm[:, 32 + lo:34 + lo], func=AF.Ln, scale=1.0 / 256, bias=epsc[:]), reads=[f"vss{lo}", f"vss{lo + 1}", "eps"], writes=[f"vsdp{i2}"])
                P.add("act", lambda e, lo=lo: e.activation(out=sm[:, 36 + lo:38 + lo], in_=sm[:, 36 + lo:38 + lo], func=AF.Exp, scale=-0.5), reads=[f"vsdp{i2}"], writes=[f"vsdp{i2}"])
                for ii in range(2):
                    i = 2 * i2 + ii
                    src = bk[:, ii * 256:(ii + 1) * 256]
                    P.add("dve", lambda e, src=src, i=i: e.tensor_scalar(out=vn[:, i, :], in0=src, scalar1=sm[:, 36 + i:37 + i], scalar2=None, op0=ALU.mult), reads=[br, f"vsdp{i2}"], writes=[f"vn{i}"])
            of = [F(7), F(8)]
            gate = of
            for c in range(2):
                gb, gbr = genh[0].next()
                for i in range(4):
                    for hh in range(2):
                        h = 2 * c + hh
                        P.add("pe", lambda e, gb=gb, i=i, hh=hh, h=h: e.matmul(gb[hh * 64:(hh + 1) * 64, i * 128:(i + 1) * 128], lhsT=vn[:, i, h * 64:(h + 1) * 64], rhs=wsT_sb[:, h, :],
                                                                            start=True, stop=True),
                              reads=[f"vn{i}", "wsT"], writes=[gbr])
                gt, gtr = gate[c]
                bsb = bsT[:, c, :].unsqueeze(1).broadcast_to([128, 4, 128])
                P.add("dve", lambda e, gb=gb, gt=gt, c=c, bsb=bsb: e.scalar_tensor_tensor(out=gt[:].rearrange("p (i t) -> p i t", i=4), in0=gb[:, :].rearrange("p (i t) -> p i t", i=4),
                                                                                        scalar=colp[:, 73 + c:74 + c], in1=bsb, op0=ALU.mult, op1=ALU.add),
                      reads=[gbr, "colp", "bsT"], writes=[gtr])
                ub, ubr = inproj(1120 + c * 128, 1120 + (c + 1) * 128)
                ft, fr = of[c]
                P.add("dve", lambda e, ub=ub, gt=gt, ft=ft: e.tensor_tensor(out=ft[:], in0=ub[:, :], in1=gt[:], op=ALU.mult), reads=[ubr, gtr], writes=[fr])
            group_norm_fm(3, [of[0][0][:], of[1][0][:]], [of[0][1], of[1][1]],
                          [(vn[:, 0:2, :].rearrange("p a b -> p (a b)"), ["vn0", "vn1"]), (vn[:, 2:4, :].rearrange("p a b -> p (a b)"), ["vn2", "vn3"])], (Fm[9][:], "F9"))

            streams.append(P.cap)
            P.cap = None
            genh[0] = gen3
            main_ops.append((att_units, streams))
            P.cap = []
            genh[0] = genpost
            emit_mla_out()
            MX = [f"mixedT{k}" for k in range(8)]
            for i in range(4):
                t = 4 * b + i
                s = i % 2
                P.add("sp", lambda e, t=t, s=s: e.dma_start(out=hr[s][:], in_=hsrc[t * 128:(t + 1) * 128, :]), reads=[f"hd{t}"], writes=[f"hr{s}"], slot=f"hr{s}")
                for n in range(2):
                    bk, br = genh[0].next()
                    mm_group(bk[:, :], br, [(mixedT[:, k, i * 128:(i + 1) * 128], w_out_sb[:, k, n * 512:(n + 1) * 512], MX + WOUT) for k in range(8)])
                    P.add("dve", lambda e, bk=bk, s=s, n=n: e.tensor_tensor(out=hr[s][:, n * 512:(n + 1) * 512], in0=bk[:, :], in1=hr[s][:, n * 512:(n + 1) * 512], op=ALU.add),
                          reads=[br, f"hr{s}"], writes=[f"hr{s}"])
                P.add("sp", lambda e, t=t, s=s: e.dma_start(out=hbuf[t * 128:(t + 1) * 128, :], in_=hr[s][:]), reads=[f"hr{s}"], writes=[f"hd{t}"], slot=f"st{s}")
            post_ops.append(P.cap)
            P.cap = None
        def merge_main(att_units, streams, extra):
            strs = list(streams) + [extra]
            order = [0, 1, 0, 2, 3]
            totY = sum(len(x) for x in strs)
            per = max(3, -(-totY // max(1, len(att_units))))
            posn = [0] * len(strs)
            st = {"rr": 0}
            out = []

            def emit_y(n):
                done = 0
                while done < n and any(posn[i] < len(strs[i]) for i in range(len(strs))):
                    i = order[st["rr"] % len(order)]
                    st["rr"] += 1
                    if posn[i] < len(strs[i]):
                        out.append(strs[i][posn[i]])
                        posn[i] += 1
                        done += 1

            for u in att_units:
                out.extend(u)
                emit_y(per)
            emit_y(10 ** 9)
            return out

        P.replay(prenorm_ops[0])
        P.replay(pre_ops[0])
        for b in range(NB):
            att_units, streams = main_ops[b]
            P.replay(merge_main(att_units, streams, prenorm_ops[b + 1] if b + 1 < NB else []))
            A_ = post_ops[b]
            B_ = pre_ops[b + 1] if b + 1 < NB else []
            ia = ib = 0
            ra = max(1, len(A_))
            rb = max(1, len(B_))
            while ia < len(A_) or ib < len(B_):
                if ib >= len(B_) or (ia < len(A_) and ia * rb <= ib * ra):
                    P.replay([A_[ia]])
                    ia += 1
                else:
                    P.replay([B_[ib]])
                    ib += 1
        P.emit()


POOL_WINDOWS = (2, 4, 8, 16)


def make_consts():
    c = np.zeros((128, NCST), np.float32)
    c[:, 0:128] = np.eye(128, dtype=np.float32)
    k = np.arange(128)[:, None]
    q = np.arange(128)[None, :]
    c[:, 128:256] = (q >= k).astype(np.float32)
    for ch in range(2):
        for p in range(128):
            w = POOL_WINDOWS[2 * ch + p // 64]
            c[p, 288 + ch] = 1.0 / w
            for t in range(16):
                c[p, 256 + ch * 16 + t] = 1.0 / min(t + 1, w)
    inv_freq = (10000.0 ** (-np.arange(0, 32, 2, dtype=np.float32) / 32)).astype(np.float32)
    for p in range(32):
        c[p, 290] = inv_freq[p % 16]
    return c


def col128(v):
    v = np.asarray(v, np.float32)
    return np.ascontiguousarray(v.reshape(-1, 128).T)


def pack_layer_inputs(inp, layers):
    L = len(layers)
    colp = np.zeros((L, 128, NCOL), np.float32)
    rowp = np.zeros((L, NROW), np.float32)
    poolw = np.zeros((L, 128, 2, 128), np.float32)
    wsT = np.zeros((L, 128, 4, 128), np.float32)
    bsT = np.zeros((L, 128, 2, 128), np.float32)
    for li, l in enumerate(layers):
        dw = inp["conv_dw_w"][l]
        for c in range(2):
            colp[li, :, c * 31:(c + 1) * 31] = dw[:, c * 128:(c + 1) * 128].T
        colp[li, :, 62:64] = col128(inp["conv_dw_b"][l])
        colp[li, :, 64:66] = col128(inp["conv_ln_g"][l])
        colp[li, :, 66:68] = col128(inp["conv_ln_b"][l])
        qg = np.zeros(256, np.float32)
        qg[0:192] = inp["mla_q_norm_g"][l]
        colp[li, :, 68:70] = col128(qg)
        colp[li, :, 70:71] = col128(inp["mla_kv_norm_g"][l])
        colp[li, :, 71:73] = col128(inp["pool_scale"][l])
        colp[li, :, 73:75] = col128(inp["gmlp_norm_g"][l])
        colp[li, :, 75:83] = col128(inp["group_norm_g"][l].reshape(-1))
        rowp[li, 0:1024] = inp["mix_norm_g"][l]
        rowp[li, 1024:2048] = inp["ffn_norm_g"][l]
        rowp[li, 2048:2304] = inp["group_norm_g"][l, 1]
        pw_ = inp["pool_w"][l]
        for c in range(2):
            for hh in range(2):
                poolw[li, hh * 64:(hh + 1) * 64, c, hh * 64:(hh + 1) * 64] = pw_[2 * c + hh]
        wsT[li] = np.transpose(inp["gmlp_ws"][l], (2, 0, 1))
        bs = inp["gmlp_bs"][l]
        for c in range(2):
            for hh in range(2):
                bsT[li, hh * 64:(hh + 1) * 64, c, :] = bs[2 * c + hh][None, :]
    sl = list(layers)
    d = dict(
        w_in=np.ascontiguousarray(inp["w_in"][sl]), w_out=np.ascontiguousarray(inp["w_out"][sl]),
        w_uq=np.ascontiguousarray(inp["mla_w_uq"][sl]), w_ukv=np.ascontiguousarray(inp["mla_w_ukv"][sl]),
        conv_pw=np.ascontiguousarray(inp["conv_pw_w"][sl]), poolw=poolw, wsT=wsT, bsT=bsT, colp=colp, rowp=rowp,
        w_ff1=np.ascontiguousarray(inp["w_ff1"][sl]), w_ff2=np.ascontiguousarray(inp["w_ff2"][sl]),
        fng=np.ascontiguousarray(np.asarray(inp["final_norm_g"], np.float32).reshape(1, D)),
        cst=make_consts(),
    )
    return d


_PROGS = {}


def get_prog(T, n_layers, final):
    key = (T, n_layers, final)
    if key not in _PROGS:
        _PROGS[key] = build_program(T, n_layers, final)
    return _PROGS[key]


def run_layers(inp, hs, positions, layers, final, T):
    shared = pack_layer_inputs(inp, layers)
    nc = get_prog(T, len(layers), final)
    in_maps = []
    for ci in range(len(hs)):
        m = dict(shared)
        m["x"] = np.ascontiguousarray(hs[ci], dtype=np.float32)
        m["pos"] = np.ascontiguousarray(positions[ci].reshape(1, T).astype(np.int32))
        in_maps.append(m)
    res = run_bass_kernel_spmd(nc, in_maps, core_ids=list(range(len(hs))))
    return [np.asarray(r["out"]) for r in res.results]


FUSED = True


def kernel(**inputs):
    inp = {k: np.asarray(v) for k, v in inputs.items()}
    x = inp["x"].astype(np.float32)
    B, T, _ = x.shape
    depth = inp["w_in"].shape[0]
    positions = inp["positions"]
    hs = [x[b] for b in range(B)]
    if FUSED:
        outs = run_layers(inp, hs, positions, list(range(depth)), True, T)
    else:
        for l in range(depth):
            hs = run_layers(inp, hs, positions, [l], l == depth - 1, T)
        outs = hs
    return np.stack(outs, axis=0).astype(np.float32)
```

```python
from contextlib import ExitStack
import math
import numpy as np
import concourse.bass as bass
import concourse.mybir as mybir
from concourse.bass_utils import run_bass_kernel_spmd

F32 = mybir.dt.float32
BF16 = mybir.dt.bfloat16
I32 = mybir.dt.int32
ALU = mybir.AluOpType
AF = mybir.ActivationFunctionType

D = 1024
DIN = 1632
DFF = 4096
EPS = 1e-6
NCOL = 83
NROW = 2304
NCST = 291
ENGS = ("pe", "act", "dve", "pool", "sp")


class Res:
    __slots__ = ("name", "w", "rs")

    def __init__(self, name):
        self.name = name
        self.w = None
        self.rs = []


class Slot:
    __slots__ = ("name", "n", "last", "sem")

    def __init__(self, name):
        self.name = name
        self.n = 0
        self.last = None
        self.sem = None


class Op:
    __slots__ = ("eng", "fn", "deps", "sig", "val", "slot", "idx", "waits", "snap")


class Sync:
    def __init__(self, nc, stack):
        self.nc = nc
        self.stack = stack
        self.esem = {e: stack.enter_context(nc.semaphore(f"s_{e}")) for e in ENGS}
        self.ecount = {e: 0 for e in ENGS}
        self.slots = {}


class Prog:
    def __init__(self, nc, name, sync):
        self.nc = nc
        self.name = name
        self.sync = sync
        self.ops = {e: [] for e in ENGS}
        self.slots = sync.slots
        for s in self.slots.values():
            s.last = None
        self.used = []
        self.res = {}
        self.cap = None
        self.all = []

    def R(self, name):
        r = self.res.get(name)
        if r is None:
            r = self.res[name] = Res(name)
        return r

    def slot(self, name):
        s = self.slots.get(name)
        if s is None:
            s = self.slots[name] = Slot(name)
        return s

    def add(self, eng, fn, reads=(), writes=(), slot=None):
        if self.cap is not None:
            self.cap.append((eng, fn, list(reads), list(writes), slot))
            return None
        reads = [self.R(r) if isinstance(r, str) else r for r in reads]
        writes = [self.R(r) if isinstance(r, str) else r for r in writes]
        if isinstance(slot, str):
            slot = self.slot(slot)
        op = Op()
        op.eng = eng
        op.fn = fn
        op.sig = slot is not None
        op.val = None
        op.slot = slot
        deps = []
        xr = [r for r in reads if r.name.startswith("bank")]
        xw = [r for r in writes if r.name.startswith("bank")]
        reads = [r for r in reads if not r.name.startswith("bank")]
        writes = [r for r in writes if not r.name.startswith("bank")]
        for r, kind in [(r, "r") for r in xr] + [(r, "w") for r in xw]:
            if r.w is not None:
                pk = r.rs[0] if r.rs else "w"
                if not (r.w.eng == eng and pk == "r" and kind == "r"):
                    deps.append(r.w)
            r.w = op
            r.rs = [kind]
        for r in reads:
            if r.w is not None:
                deps.append(r.w)
        for r in writes:
            if r.w is not None:
                deps.append(r.w)
            deps.extend(r.rs)
        if slot is not None and slot.last is not None:
            deps.append(slot.last)
        for r in reads:
            r.rs.append(op)
        for r in writes:
            r.w = op
            r.rs = []
        if slot is not None:
            slot.last = op
            slot.n += 1
            op.val = 16 * slot.n
            if slot not in self.used:
                self.used.append(slot)
        dd = []
        seen = set()
        latest = {}
        for d in deps:
            if d is op or id(d) in seen:
                continue
            seen.add(id(d))
            if d.eng == "pe" and eng == "pe" and d.slot is None and slot is None:
                continue
            if d.slot is None:
                cur = latest.get(d.eng)
                if cur is None or d.idx > cur.idx:
                    latest[d.eng] = d
            else:
                dd.append(d)
        dd.extend(latest.values())
        for d in dd:
            d.sig = True
        op.deps = dd
        op.idx = len(self.ops[eng])
        self.ops[eng].append(op)
        self.all.append(op)
        return op

    def replay(self, items):
        for it in items:
            self.add(*it)

    def emit(self):
        nc = self.nc
        sync = self.sync
        for e in ENGS:
            c = sync.ecount[e]
            for op in self.ops[e]:
                if op.slot is None and op.sig:
                    c += 1
                    op.val = c
            sync.ecount[e] = c
        with ExitStack() as st:
            esem = sync.esem
            fin = list(self.used)
            for s in fin:
                if s.sem is None:
                    s.sem = sync.stack.enter_context(nc.semaphore(f"d_{s.name}"))
            known = {e: {} for e in ENGS}
            for op in self.all:
                k = known[op.eng]
                waits = []
                for d in op.deps:
                    sem = d.slot.sem if d.slot is not None else esem[d.eng]
                    key = id(sem)
                    if k.get(key, 0) >= d.val:
                        continue
                    waits.append((sem, d.val))
                    k[key] = d.val
                    if d.snap:
                        for kk, vv in d.snap.items():
                            if k.get(kk, 0) < vv:
                                k[kk] = vv
                op.waits = waits
                op.snap = dict(k) if op.sig else None
            block = st.enter_context(nc.Block())

            def run(eng_name, e):
                for op in self.ops[eng_name]:
                    for sem, val in op.waits:
                        e.wait_ge(sem, val)
                    ins = op.fn(e)
                    if op.slot is not None:
                        ins.then_inc(op.slot.sem, 16)
                    elif op.sig:
                        ins.then_inc(esem[eng_name], 1)
                if eng_name == "sp":
                    for s in fin:
                        e.wait_ge(s.sem, 16 * s.n)

            block.tensor(lambda e: run("pe", e))
            block.scalar(lambda e: run("act", e))
            block.vector(lambda e: run("dve", e))
            block.gpsimd(lambda e: run("pool", e))
            block.sync(lambda e: run("sp", e))


class Rot:
    def __init__(self, items):
        self.items = items
        self.i = 0

    def next(self):
        it = self.items[self.i % len(self.items)]
        self.i += 1
        return it


def build_program(T, n_layers, final):
    NB = T // 512
    NT = T // 128
    nc = bass.Bass("TRN2", target_bir_lowering=False)

    def din(name, shape, dt=F32):
        return nc.dram_tensor(name, list(shape), dt, kind="ExternalInput").ap()

    L = n_layers
    x = din("x", [T, D])
    pos = din("pos", [1, T], I32)
    w_in = din("w_in", [L, D, DIN])
    w_out = din("w_out", [L, D, D])
    w_uq = din("w_uq", [L, 192, 384])
    w_ukv = din("w_ukv", [L, 128, 512])
    conv_pw = din("conv_pw", [L, 256, 256])
    poolw = din("poolw", [L, 128, 2, 128])
    wsT = din("wsT", [L, 128, 4, 128])
    bsT = din("bsT", [L, 128, 2, 128])
    colp = din("colp", [L, 128, NCOL])
    rowp = din("rowp", [L, NROW])
    w_ff1 = din("w_ff1", [L, D, DFF])
    w_ff2 = din("w_ff2", [L, DFF, D])
    fng = din("fng", [1, D])
    cst = din("cst", [128, NCST])
    out = nc.dram_tensor("out", [T, D], F32, kind="ExternalOutput").ap()
    hbuf = nc.dram_tensor("hbuf", [T, D], F32).ap()
    ropetab = nc.dram_tensor("ropetab", [2, 32, T], F32).ap()

    with ExitStack() as top:
        banks = [top.enter_context(nc.psum_tensor(f"bank{i}", [128, 512], F32)) for i in range(8)]
        sync = Sync(nc, top)
        w_in_sb = top.enter_context(nc.sbuf_tensor("w_in_sb", [128, 8, DIN], BF16))

        with ExitStack() as st:
            P = Prog(nc, "pr", sync)
            sb = lambda n, s, d: st.enter_context(nc.sbuf_tensor(n, s, d))
            posi = sb("posi", [32, T], I32)
            posf = sb("posf", [32, T], F32)
            ang = sb("ang", [32, T], F32)
            u = sb("u", [32, T], F32)
            ki = sb("ki", [32, T], I32)
            kf = sb("kf", [32, T], F32)
            ng = sb("ng", [32, T], F32)
            tab = [sb("tab0", [32, T], F32), sb("tab1", [32, T], F32)]
            c_sb = sb("c_sb", [128, NCST], F32)
            nbias = sb("nbias", [32, 1], F32)
            P.add("sp", lambda e: e.dma_start(out=posi[:], in_=pos[0, :].partition_broadcast(32)), writes=["posi"], slot="posi")
            P.add("sp", lambda e: e.dma_start(out=c_sb[:], in_=cst), writes=["c"], slot="c")
            P.add("dve", lambda e: e.tensor_copy(out=posf[:], in_=posi[:]), reads=["posi"], writes=["posf"])
            P.add("dve", lambda e: e.memset(nbias[:], -math.pi * (1 - 1e-6)), writes=["nbias"])
            P.add("dve", lambda e: e.tensor_scalar(out=ang[:], in0=posf[:], scalar1=c_sb[0:32, 290:291], scalar2=None, op0=ALU.mult),
                  reads=["posf", "c"], writes=["ang"])
            for i, shift in enumerate((0.75, 0.5)):
                P.add("dve", lambda e, shift=shift: e.tensor_scalar(out=u[:], in0=ang[:], scalar1=1.0 / (2 * math.pi), scalar2=shift, op0=ALU.mult, op1=ALU.add),
                      reads=["ang"], writes=["u"])
                P.add("dve", lambda e: e.tensor_copy(out=ki[:], in_=u[:]), reads=["u"], writes=["ki"])
                P.add("dve", lambda e: e.tensor_copy(out=kf[:], in_=ki[:]), reads=["ki"], writes=["kf"])
                P.add("dve", lambda e: e.tensor_tensor(out=u[:], in0=u[:], in1=kf[:], op=ALU.subtract), reads=["u", "kf"], writes=["u"])
                P.add("dve", lambda e: e.tensor_scalar(out=ng[:], in0=u[:], scalar1=0.0, scalar2=None, op0=ALU.is_lt), reads=["u"], writes=["ng"])
                P.add("dve", lambda e: e.tensor_tensor(out=u[:], in0=u[:], in1=ng[:], op=ALU.add), reads=["u", "ng"], writes=["u"])
                P.add("act", lambda e, i=i: e.activation(out=tab[i][:], in_=u[:], func=AF.Sin, scale=2 * math.pi * (1 - 1e-6), bias=nbias[:]),
                      reads=["u", "nbias"], writes=[f"tab{i}"])
                P.add("sp", lambda e, i=i: e.dma_start(out=ropetab[i], in_=tab[i][:]), reads=[f"tab{i}"], slot=f"tabo{i}")
            P.emit()

        for l in range(n_layers):
            hsrc = x if l == 0 else hbuf
            if True:
              emit_pass_a(nc, sync, l, T, NB, NT, banks, hsrc, hbuf, ropetab, pos, w_in_sb,
                        dict(w_in=w_in, w_out=w_out, w_uq=w_uq, w_ukv=w_ukv, conv_pw=conv_pw, poolw=poolw,
                             wsT=wsT, bsT=bsT, colp=colp, rowp=rowp, cst=cst))
            is_last = (l == n_layers - 1)
            if True:
              emit_pass_b(nc, sync, l, T, NT, banks, hbuf, out, dict(w_ff1=w_ff1, w_ff2=w_ff2, rowp=rowp, fng=fng, cst=cst, w_in=w_in, w_in_sb=w_in_sb, n_layers=n_layers),
                        do_final=(is_last and final), to_out=is_last)
    return nc


def emit_pass_b(nc, sync, l, T, NT, banks, hbuf, out, W, do_final, to_out):
    NBB = T // 256
    with ExitStack() as st:
        P = Prog(nc, f"b{l}", sync)
        sb = lambda n, s, d: st.enter_context(nc.sbuf_tensor(f"B{l}_{n}", s, d))
        W1 = sb("W1", [128, 8, DFF], BF16)
        W2 = sb("W2", [128, 32, D], BF16)
        gff = sb("gff", [128, D], F32)
        gfin = sb("gfin", [128, D], F32)
        c_sb = sb("c_sb", [128, 128], F32)
        ident = sb("ident", [128, 128], BF16)
        hn = [sb(f"hn{i}", [128, D], F32) for i in range(4)]
        xn = [sb(f"xn{i}", [128, D], BF16) for i in range(4)]
        xnT = [sb(f"xnT{i}", [128, 8, 256], BF16) for i in range(2)]
        rr = [sb(f"r{i}", [128, 256], F32) for i in range(3)]
        fT = [sb(f"fT{i}", [128, 256], BF16) for i in range(3)]
        junk = sb("junk", [128, D], BF16)
        ss = [sb(f"ss{i}", [128, 1], F32) for i in range(4)]
        rs = [sb(f"rs{i}", [128, 1], F32) for i in range(4)]
        ss2 = [sb(f"ss2{i}", [128, 1], F32) for i in range(4)]
        rs2 = [sb(f"rs2{i}", [128, 1], F32) for i in range(4)]
        epsc = sb("epsc", [128, 1], F32)

        w1v = W["w_ff1"][l].rearrange("(k p) n -> p k n", p=128)
        w2v = W["w_ff2"][l].rearrange("(c p) n -> p c n", p=128)
        P.add("sp", lambda e: e.dma_start(out=c_sb[:], in_=W["cst"][:, 0:128]), writes=["c"], slot="c")
        P.add("sp", lambda e: e.dma_start(out=gff[:], in_=W["rowp"][l, 1024:2048].partition_broadcast(128)), writes=["gff"], slot="gff")
        if do_final:
            P.add("sp", lambda e: e.dma_start(out=gfin[:], in_=W["fng"][0, :].partition_broadcast(128)), writes=["gfin"], slot="gfin")
        P.add("dve", lambda e: e.tensor_copy(out=ident[:], in_=c_sb[:]), reads=["c"], writes=["ident"])
        P.add("dve", lambda e: e.memset(epsc[:], EPS), writes=["eps"])
        for g in range(8):
            P.add("pool", lambda e, g=g: e.dma_start(out=W1[:, :, g * 512:(g + 1) * 512], in_=w1v[:, :, g * 512:(g + 1) * 512]),
                  writes=[f"W1g{g}"], slot=f"w1_{g % 4}")
            P.add("pool", lambda e, g=g: e.dma_start(out=W2[:, 4 * g:4 * g + 4, :], in_=w2v[:, 4 * g:4 * g + 4, :]),
                  writes=[f"W2g{g}"], slot=f"w2_{g % 4}")

        trb7 = banks[7][:, :].bitcast(BF16)

        def prep_load(bb):
            for i in range(2):
                t = 2 * bb + i
                s = (bb % 2) * 2 + i
                P.add("sp", lambda e, t=t, s=s: e.dma_start(out=hn[s][:], in_=hbuf[t * 128:(t + 1) * 128, :]),
                      reads=[f"hd{t}"], writes=[f"hn{s}"], slot=f"hn{s}")
                P.add("act", lambda e, s=s: e.activation(out=junk[:], in_=hn[s][:], func=AF.Square, accum_out=ss[s][:]),
                      reads=[f"hn{s}"], writes=[f"ss{s}"])
                P.add("act", lambda e, s=s: e.activation(out=rs[s][:], in_=ss[s][:], func=AF.Ln, scale=1.0 / D, bias=epsc[:]),
                      reads=[f"ss{s}", "eps"], writes=[f"sd{s}"])
                P.add("act", lambda e, s=s: e.activation(out=rs[s][:], in_=rs[s][:], func=AF.Exp, scale=-0.5), reads=[f"sd{s}"], writes=[f"sd{s}"])
                P.add("dve", lambda e, s=s: e.scalar_tensor_tensor(out=xn[s][:], in0=hn[s][:], scalar=rs[s][:], in1=gff[:], op0=ALU.mult, op1=ALU.mult),
                      reads=[f"hn{s}", f"sd{s}", "gff"], writes=[f"xn{s}"])

        def prep_tr(bb):
            xt = xnT[bb % 2]
            for hb in range(2):
                for i in range(2):
                    s = (bb % 2) * 2 + i
                    for kk in range(4):
                        k = hb * 4 + kk
                        off = kk * 256 + i * 128
                        P.add("pe", lambda e, s=s, k=k, off=off: e.transpose(out=trb7[:, off:off + 128], in_=xn[s][:, k * 128:(k + 1) * 128], identity=ident[:]),
                              reads=[f"xn{s}", "ident"], writes=["bank7"])
                dst = xt[:, hb * 4:hb * 4 + 4, :].rearrange("p k t -> p (k t)")
                if hb == 0:
                    P.add("dve", lambda e, dst=dst: e.tensor_copy(out=dst, in_=trb7[:, :]), reads=["bank7"], writes=[f"xnT{bb % 2}a"])
                else:
                    P.add("act", lambda e, dst=dst: e.activation(out=dst, in_=trb7[:, :], func=AF.Copy), reads=["bank7"], writes=[f"xnT{bb % 2}b"])

        def ff1(bb, c):
            xt = xnT[bb % 2]
            q = c % 3
            pb = banks[4 + q][:, 0:256]
            for k in range(8):
                P.add("pe", lambda e, k=k, pb=pb, xt=xt, c=c: e.matmul(pb, lhsT=W1[:, k, c * 128:(c + 1) * 128], rhs=xt[:, k, :], start=(k == 0), stop=(k == 7)),
                      reads=[f"W1g{c // 4}", f"xnT{bb % 2}a", f"xnT{bb % 2}b"], writes=[f"bank{4 + q}"])
            j = c % 3
            P.add("act", lambda e, pb=pb, j=j: e.activation(out=rr[j][:], in_=pb, func=AF.Relu), reads=[f"bank{4 + q}"], writes=[f"r{j}"])
            P.add("pool", lambda e, j=j: e.tensor_tensor(out=fT[j][:], in0=rr[j][:], in1=rr[j][:], op=ALU.mult), reads=[f"r{j}"], writes=[f"fT{j}"])

        def ff2(bb, c):
            j = c % 3
            for i in range(2):
                for n in range(2):
                    P.add("pe", lambda e, i=i, n=n, j=j, c=c: e.matmul(banks[i * 2 + n][:, :], lhsT=fT[j][:, i * 128:(i + 1) * 128], rhs=W2[:, c, n * 512:(n + 1) * 512],
                                                                 start=(c == 0), stop=(c == 31)),
                          reads=[f"fT{j}", f"W2g{c // 4}"], writes=[f"bank{i * 2 + n}"])

        def epilogue(bb):
            for i in range(2):
                t = 2 * bb + i
                s = (bb % 2) * 2 + i
                for n in range(2):
                    P.add("dve", lambda e, i=i, n=n, s=s: e.tensor_tensor(out=hn[s][:, n * 512:(n + 1) * 512], in0=banks[i * 2 + n][:, :], in1=hn[s][:, n * 512:(n + 1) * 512], op=ALU.add),
                          reads=[f"bank{i * 2 + n}", f"hn{s}"], writes=[f"hn{s}"])
                if do_final:
                    P.add("act", lambda e, s=s: e.activation(out=junk[:], in_=hn[s][:], func=AF.Square, accum_out=ss2[s][:]),
                          reads=[f"hn{s}"], writes=[f"ss2{s}"])
                    P.add("act", lambda e, s=s: e.activation(out=rs2[s][:], in_=ss2[s][:], func=AF.Ln, scale=1.0 / D, bias=epsc[:]),
                          reads=[f"ss2{s}", "eps"], writes=[f"sd2{s}"])
                    P.add("act", lambda e, s=s: e.activation(out=rs2[s][:], in_=rs2[s][:], func=AF.Exp, scale=-0.5), reads=[f"sd2{s}"], writes=[f"sd2{s}"])
                    P.add("dve", lambda e, s=s: e.scalar_tensor_tensor(out=hn[s][:], in0=hn[s][:], scalar=rs2[s][:], in1=gfin[:], op0=ALU.mult, op1=ALU.mult),
                          reads=[f"hn{s}", f"sd2{s}", "gfin"], writes=[f"hn{s}"])
                dst = out if to_out else hbuf
                P.add("sp", lambda e, t=t, s=s, dst=dst: e.dma_start(out=dst[t * 128:(t + 1) * 128, :], in_=hn[s][:]),
                      reads=[f"hn{s}"], writes=[f"hd{t}"], slot=f"st{s}")

        prep_load(0)
        prep_tr(0)
        for bb in range(NBB):
            if bb == NBB // 2 and l + 1 < W["n_layers"]:
                wvn = W["w_in"][l + 1].rearrange("(k p) n -> p k n", p=128)
                for g in range(4):
                    P.add("pool", lambda e, g=g: e.dma_start(out=W["w_in_sb"][:, 2 * g:2 * g + 2, :], in_=wvn[:, 2 * g:2 * g + 2, :]), writes=[f"w_in{g}"], slot=f"wl{g}")
            for c in range(32):
                ff1(bb, c)
                if c > 0:
                    ff2(bb, c - 1)
                if c == 4 and bb + 1 < NBB:
                    prep_load(bb + 1)
                if c == 20 and bb + 1 < NBB:
                    prep_tr(bb + 1)
            ff2(bb, 31)
            epilogue(bb)
        P.emit()


def emit_pass_a(nc, sync, l, T, NB, NT, banks, hsrc, hbuf, ropetab, pos, w_in_sb, W):
    with ExitStack() as st:
        P = Prog(nc, f"a{l}", sync)
        sb = lambda n, s, d: st.enter_context(nc.sbuf_tensor(f"A{l}_{n}", s, d))
        w_krr = sb("w_krr", [128, 8, 128], BF16)
        w_out_sb = sb("w_out", [128, 8, D], BF16)
        w_uq_sb = sb("w_uq", [128, 2, 416], BF16)
        w_uqr = sb("w_uqr", [128, 2, 416], BF16)
        w_ukv_sb = sb("w_ukv", [128, 512], BF16)
        pw_sb = sb("pw", [128, 2, 256], BF16)
        w_kn = sb("w_kn", [128, 256], BF16)
        w_v = sb("w_v", [128, 256], BF16)
        poolw_sb = sb("poolw", [128, 2, 128], BF16)
        wsT_sb = sb("wsT", [128, 4, 128], BF16)
        dg = sb("dg", [128, 8, 128], BF16)
        colp = sb("colp", [128, NCOL], F32)
        gmix = sb("gmix", [128, D], F32)
        ggm = sb("ggm", [128, 256], F32)
        bsT = sb("bsT", [128, 2, 128], F32)
        c_sb = sb("c_sb", [128, NCST], F32)
        ident = sb("ident", [128, 128], BF16)
        maskT = sb("maskT", [128, 128], BF16)
        ones = sb("ones", [128, 128], BF16)
        epsc = sb("epsc", [128, 1], F32)
        onec = sb("onec", [128, 1], F32)
        KT = sb("KT", [128, 4, T], BF16)
        Vt = sb("Vt", [128, NT, 4, 65], BF16)
        ycv = sb("ycv", [128, 2, 542], BF16)
        zp = sb("zp", [128, 2, 527], F32)
        hn = [sb(f"hn{i}", [128, D], F32) for i in range(2)]
        xn = [sb(f"xn{i}", [128, D], BF16) for i in range(4)]
        hr = [sb(f"hr{i}", [128, D], F32) for i in range(2)]
        xnT = sb("xnT", [128, 8, 512], BF16)
        mixedT = sb("mixedT", [128, 8, 512], BF16)
        QT = sb("QT", [128, 4, 512], BF16)
        PT = [sb(f"PT{i}", [128, 512], BF16) for i in range(3)]
        cos2 = sb("cos2", [128, 512], F32)
        sin2 = sb("sin2", [128, 512], F32)
        junk = sb("junk", [128, 256], BF16)
        Fm = [sb(f"F{i}", [128, 512], F32) for i in range(10)]
        Hm = [sb(f"H{i}", [128, 512], BF16) for i in range(6)]
        PA = sb("PA", [128, 527], F32)
        PB = sb("PB", [128, 527], F32)
        omla = sb("omla", [128, 4, 256], F32)
        vn = sb("vn", [128, 4, 256], BF16)
        mtok = [sb(f"mtok{i}", [128, 256], BF16) for i in range(2)]
        sm = sb("sm", [128, 64], F32)

        def F(i):
            return Fm[i], f"F{i}"

        def H(i):
            return Hm[i], f"H{i}"

        P.add("sp", lambda e: e.dma_start(out=c_sb[:], in_=W["cst"]), writes=["c"], slot="c")
        P.add("sp", lambda e: e.dma_start(out=colp[:], in_=W["colp"][l]), writes=["colp"], slot="colp")
        P.add("sp", lambda e: e.dma_start(out=gmix[:], in_=W["rowp"][l, 0:1024].partition_broadcast(128)), writes=["gmix"], slot="gmix")
        P.add("sp", lambda e: e.dma_start(out=ggm[:], in_=W["rowp"][l, 2048:2304].partition_broadcast(128)), writes=["ggm"], slot="ggm")
        P.add("sp", lambda e: e.dma_start(out=bsT[:], in_=W["bsT"][l]), writes=["bsT"], slot="bsT")
        P.add("dve", lambda e: e.tensor_copy(out=ident[:], in_=c_sb[:, 0:128]), reads=["c"], writes=["ident"])
        P.add("dve", lambda e: e.tensor_copy(out=maskT[:], in_=c_sb[:, 128:256]), reads=["c"], writes=["maskT"])
        P.add("dve", lambda e: e.memset(ones[:], 1.0), writes=["ones"])
        P.add("dve", lambda e: e.memset(epsc[:], EPS), writes=["eps"])
        P.add("dve", lambda e: e.memset(onec[:], 1.0), writes=["eps"])
        P.add("pool", lambda e: e.memset(KT[96:128, :, :], 0.0), writes=["KTpad"])
        P.add("pool", lambda e: e.memset(QT[96:128, :, :], 0.0), writes=["QTpad"])
        P.add("pool", lambda e: e.memset(w_uq_sb[:, :, 384:416], 0.0), writes=["w_uq"])
        P.add("pool", lambda e: e.memset(w_uq_sb[64:128, 1, :], 0.0), writes=["w_uq"])
        P.add("pool", lambda e: e.memset(w_uqr[:], 0.0), writes=["w_uqr"])
        P.add("pool", lambda e: e.memset(ycv[:], 0.0), writes=["ycvh0", "ycvh1", "ycv0", "ycv1"])
        P.add("pool", lambda e: e.memset(zp[:], 0.0), writes=["zph0", "zph1", "zp0", "zp1"])
        P.add("pool", lambda e: e.memset(Vt[:], 1.0), writes=["Vinit"])
        P.add("pool", lambda e: e.memset(w_krr[:], 0.0), writes=["w_krr"])
        wv = W["w_in"][l].rearrange("(k p) n -> p k n", p=128)
        for g in range(4 if l == 0 else 0):
            P.add("pool", lambda e, g=g: e.dma_start(out=w_in_sb[:, 2 * g:2 * g + 2, :], in_=wv[:, 2 * g:2 * g + 2, :]), writes=[f"w_in{g}"], slot=f"wl{g}")
        P.add("pool", lambda e: e.dma_start(out=w_uq_sb[:, 0, 0:384], in_=W["w_uq"][l, 0:128, :]), writes=["w_uq"], slot="wl0")
        P.add("pool", lambda e: e.dma_start(out=w_uq_sb[0:64, 1, 0:384], in_=W["w_uq"][l, 128:192, :]), writes=["w_uq"], slot="wl1")
        P.add("pool", lambda e: e.dma_start(out=w_ukv_sb[:], in_=W["w_ukv"][l]), writes=["w_ukv"], slot="wl2")
        P.add("pool", lambda e: e.dma_start(out=pw_sb[:], in_=W["conv_pw"][l].rearrange("(k p) n -> p k n", p=128)), writes=["pw"], slot="wl3")
        P.add("pool", lambda e: e.dma_start(out=poolw_sb[:], in_=W["poolw"][l]), writes=["poolw"], slot="wl0")
        P.add("pool", lambda e: e.dma_start(out=wsT_sb[:], in_=W["wsT"][l]), writes=["wsT"], slot="wl1")
        wov = W["w_out"][l].rearrange("(k p) n -> p k n", p=128)
        for g in range(2):
            P.add("pool", lambda e, g=g: e.dma_start(out=w_out_sb[:, 4 * g:4 * g + 4, :], in_=wov[:, 4 * g:4 * g + 4, :]), writes=[f"w_out{g}"], slot=f"wl{2 + g}")
        WIN = [f"w_in{g}" for g in range(4)]
        WOUT = ["w_out0", "w_out1"]
        ukv4 = w_ukv_sb[:, :].rearrange("p (h d) -> p h d", h=4)
        P.add("act", lambda e: e.activation(out=w_kn[:, :].rearrange("p (h d) -> p h d", h=4), in_=ukv4[:, :, 0:64], func=AF.Copy), reads=["w_ukv"], writes=["w_kn"])
        P.add("act", lambda e: e.activation(out=w_v[:, :].rearrange("p (h d) -> p h d", h=4), in_=ukv4[:, :, 64:128], func=AF.Copy), reads=["w_ukv"], writes=["w_v"])
        for h in range(4):
            P.add("dve", lambda e, h=h: e.tensor_tensor(out=wsT_sb[:, h, :], in0=wsT_sb[:, h, :], in1=maskT[:], op=ALU.mult),
                  reads=["wsT", "maskT"], writes=["wsT"])
        for kc in range(2):
            pr = slice(0, 128) if kc == 0 else slice(0, 64)
            src = w_uq_sb[pr, kc, 0:384].rearrange("p (h d) -> p h d", h=4)
            dst = w_uqr[pr, kc, 0:384].rearrange("p (h d) -> p h d", h=4)
            P.add("act", lambda e, src=src, dst=dst: e.mul(out=dst[:, :, 64:80], in_=src[:, :, 80:96], mul=-1.0), reads=["w_uq", "w_uqr"], writes=["w_uqr"])
            P.add("act", lambda e, src=src, dst=dst: e.activation(out=dst[:, :, 80:96], in_=src[:, :, 64:80], func=AF.Copy), reads=["w_uq", "w_uqr"], writes=["w_uqr"])
            P.add("act", lambda e, src=src, dst=dst: e.activation(out=dst[:, :, 0:64], in_=src[:, :, 0:64], func=AF.Copy), reads=["w_uq", "w_uqr"], writes=["w_uqr"])
        P.add("act", lambda e: e.mul(out=w_krr[:, :, 64:80], in_=w_in_sb[:, :, 848:864], mul=-1.0), reads=WIN + ["w_krr"], writes=["w_krr"])
        P.add("act", lambda e: e.activation(out=w_krr[:, :, 80:96], in_=w_in_sb[:, :, 832:848], func=AF.Copy), reads=WIN + ["w_krr"], writes=["w_krr"])
        gen3 = Rot([(banks[i], f"bank{i}") for i in range(3)])
        gen1 = [Rot([(banks[i], f"bank{i}")]) for i in range(3)]
        genh = [gen3]
        stb = Rot([(banks[i], f"bank{i}") for i in range(3, 6)])
        pvb = Rot([(banks[i], f"bank{i}") for i in range(6, 8)])
        ptr = Rot([(PT[i], f"PT{i}") for i in range(3)])
        SCALE = 96.0 ** -0.5

        def mm_group(pout, pres, parts, extra_reads=()):
            n = len(parts)
            for i, (lt, rh, rd) in enumerate(parts):
                P.add("pe", lambda e, lt=lt, rh=rh, i=i: e.matmul(pout, lhsT=lt, rhs=rh, start=(i == 0), stop=(i == n - 1)),
                      reads=list(rd) + list(extra_reads), writes=[pres])

        def rstd_bcast(sq_parts, scale, dst, dres):
            bk, br = genh[0].next()
            mm_group(bk[:, :], br, [(ones[0:sq.shape[0], :], sq, rd + ["ones"]) for sq, rd in sq_parts])
            P.add("act", lambda e: e.activation(out=dst, in_=bk[:, :], func=AF.Ln, scale=scale, bias=epsc[:]), reads=[br, "eps"], writes=[dres])
            P.add("act", lambda e: e.activation(out=dst, in_=dst, func=AF.Exp, scale=-0.5), reads=[dres], writes=[dres])

        def group_norm_fm(g, of, ofres, sqt, rtt):
            sqs = []
            for c in range(2):
                hq, hres = sqt[c]
                hres = hres if isinstance(hres, list) else [hres]
                P.add("pool", lambda e, c=c, hq=hq: e.tensor_tensor(out=hq, in0=of[c], in1=of[c], op=ALU.mult), reads=[ofres[c]], writes=hres)
                sqs.append((hq, hres))
            rt, rres = rtt
            rstd_bcast(sqs, 1.0 / 256, rt, rres)
            for c in range(2):
                P.add("dve", lambda e, c=c: e.scalar_tensor_tensor(out=mixedT[:, 2 * g + c, :], in0=of[c], scalar=colp[:, 75 + 2 * g + c:76 + 2 * g + c], in1=rt,
                                                                  op0=ALU.mult, op1=ALU.mult),
                      reads=[ofres[c], rres, "colp"], writes=[f"mixedT{2 * g + c}"])

        genpre = Rot([(banks[i], f"bank{i}") for i in (1, 2, 3, 4)])
        genpost = Rot([(banks[i], f"bank{i}") for i in (0, 5)])
        pre_ops, main_ops, post_ops, prenorm_ops = [], [], [], []
        for b in range(NB):
            tb = b * 512
            P.cap = []
            genh[0] = genpre
            P.add("sp", lambda e, tb=tb: e.dma_start(out=cos2[64:96, :], in_=ropetab[0, :, tb:tb + 512]), writes=["cos2"], slot="cos2")
            P.add("sp", lambda e, tb=tb: e.dma_start(out=sin2[64:96, :], in_=ropetab[1, :, tb:tb + 512]), writes=["sin2"], slot="sin2")
            prenorm_cap = P.cap
            P.cap = []
            for ip in range(2):
                for i in (2 * ip, 2 * ip + 1):
                    t = 4 * b + i
                    s = i % 2
                    P.add("sp", lambda e, t=t, s=s: e.dma_start(out=hn[s][:], in_=hsrc[t * 128:(t + 1) * 128, :]), reads=[f"hd{t}"], writes=[f"hn{s}"], slot=f"hn{s}")
                    P.add("act", lambda e, s=s, i=i: e.activation(out=xn[i][:], in_=hn[s][:], func=AF.Square, accum_out=sm[:, 40 + i:41 + i]), reads=[f"hn{s}"], writes=[f"ss{i}", f"xn{i}"])
                lo = 2 * ip
                P.add("act", lambda e, lo=lo: e.activation(out=sm[:, 44 + lo:46 + lo], in_=sm[:, 40 + lo:42 + lo], func=AF.Ln, scale=1.0 / D, bias=epsc[:]),
                      reads=[f"ss{lo}", f"ss{lo + 1}", "eps"], writes=[f"sdp{ip}"])
                P.add("act", lambda e, lo=lo: e.activation(out=sm[:, 44 + lo:46 + lo], in_=sm[:, 44 + lo:46 + lo], func=AF.Exp, scale=-0.5), reads=[f"sdp{ip}"], writes=[f"sdp{ip}"])
                for i in (2 * ip, 2 * ip + 1):
                    s = i % 2
                    P.add("dve", lambda e, s=s, i=i: e.scalar_tensor_tensor(out=xn[i][:], in0=hn[s][:], scalar=sm[:, 44 + i:45 + i], in1=gmix[:], op0=ALU.mult, op1=ALU.mult),
                          reads=[f"hn{s}", f"sdp{ip}", "gmix"], writes=[f"xn{i}"])
            prenorm_ops.append(P.cap)
            P.cap = prenorm_cap
            for i in range(4):
                for hb in range(2):
                    bk, br = genh[0].next()
                    bv = bk[:, :].bitcast(BF16)
                    for kk in range(4):
                        k = hb * 4 + kk
                        P.add("pe", lambda e, i=i, k=k, kk=kk, bv=bv: e.transpose(out=bv[:, kk * 128:(kk + 1) * 128], in_=xn[i][:, k * 128:(k + 1) * 128], identity=ident[:]),
                              reads=[f"xn{i}", "ident"], writes=[br])
                    eng = "dve" if hb == 0 else "act"
                    dst = xnT[:, hb * 4:hb * 4 + 4, i * 128:(i + 1) * 128]
                    src = bv[:, 0:512].rearrange("p (k t) -> p k t", k=4)
                    if eng == "dve":
                        P.add("dve", lambda e, dst=dst, src=src: e.tensor_copy(out=dst, in_=src), reads=[br], writes=[f"xnT{i}"])
                    else:
                        P.add("act", lambda e, dst=dst, src=src: e.activation(out=dst, in_=src, func=AF.Copy), reads=[br], writes=[f"xnT{i}"])
            XNT = [f"xnT{i}" for i in range(4)]

            def inproj(c0, c1, lw=None):
                bk, br = genh[0].next()
                m = c1 - c0
                parts = []
                for k in range(8):
                    lt = (w_in_sb[:, k, c0:c1] if lw is None else lw[:, k, :])
                    parts.append((lt, xnT[:, k, :], XNT + WIN + (["w_krr"] if lw is not None else [])))
                mm_group(bk[0:m, :], br, parts)
                return bk, br

            craw = [F(0), F(1), F(2)]
            csq = [H(0), H(1), H(2)]
            spans = [(512, 640), (640, 704), (704, 832)]
            for j, (c0, c1) in enumerate(spans):
                m = c1 - c0
                bk, br = inproj(c0, c1)
                ft, fr = craw[j]
                P.add("act", lambda e, bk=bk, ft=ft, m=m: e.activation(out=ft[0:m, :], in_=bk[0:m, :], func=AF.Copy), reads=[br], writes=[fr])
                ht, hres = csq[j]
                P.add("pool", lambda e, ft=ft, ht=ht, m=m: e.tensor_tensor(out=ht[0:m, :], in0=ft[0:m, :], in1=ft[0:m, :], op=ALU.mult), reads=[fr], writes=[hres])
            rq, rqres = F(3)
            rstd_bcast([(csq[0][0][:, :], [csq[0][1]]), (csq[1][0][0:64, :], [csq[1][1]])], 1.0 / 192, rq[:], rqres)
            rkv, rkvres = F(4)
            rstd_bcast([(csq[2][0][:, :], [csq[2][1]])], 1.0 / 128, rkv[:], rkvres)
            cqn = [H(3), H(4), H(5)]
            gcols = [68, 69, 70]
            for j in range(3):
                m = spans[j][1] - spans[j][0]
                ft, fr = craw[j]
                ht, hres = cqn[j]
                rt, rres = (rq, rqres) if j < 2 else (rkv, rkvres)
                P.add("dve", lambda e, ft=ft, ht=ht, rt=rt, m=m, gc=gcols[j]: e.scalar_tensor_tensor(out=ht[0:m, :], in0=ft[0:m, :], scalar=colp[0:m, gc:gc + 1], in1=rt[0:m, :],
                                                                                                   op0=ALU.mult, op1=ALU.mult),
                      reads=[fr, rres, "colp"], writes=[hres])
            for h in range(4):
                pa, par = genh[0].next()
                mm_group(pa[:, :], par, [(w_uq_sb[:, 0, h * 96:h * 96 + 128], cqn[0][0][:, :], [cqn[0][1], "w_uq"]),
                                         (w_uq_sb[0:64, 1, h * 96:h * 96 + 128], cqn[1][0][0:64, :], [cqn[1][1], "w_uq"])])
                pbk, pbr = genh[0].next()
                mm_group(pbk[:, :], pbr, [(w_uqr[:, 0, h * 96:h * 96 + 128], cqn[0][0][:, :], [cqn[0][1], "w_uqr"]),
                                          (w_uqr[0:64, 1, h * 96:h * 96 + 128], cqn[1][0][0:64, :], [cqn[1][1], "w_uqr"])])
                P.add("act", lambda e, pa=pa, h=h: e.activation(out=QT[0:64, h, :], in_=pa[0:64, :], func=AF.Copy), reads=[par], writes=[f"QT{h}"])
                t1, t1r = F(5)
                t2, t2r = F(6)
                P.add("dve", lambda e, pa=pa, t1=t1: e.tensor_tensor(out=t1[64:96, :], in0=pa[64:96, :], in1=cos2[64:96, :], op=ALU.mult), reads=[par, "cos2"], writes=[t1r])
                P.add("dve", lambda e, pbk=pbk, t2=t2: e.tensor_tensor(out=t2[64:96, :], in0=pbk[64:96, :], in1=sin2[64:96, :], op=ALU.mult), reads=[pbr, "sin2"], writes=[t2r])
                P.add("pool", lambda e, t1=t1, t2=t2, h=h: e.tensor_tensor(out=QT[64:96, h, :], in0=t1[64:96, :], in1=t2[64:96, :], op=ALU.add), reads=[t1r, t2r], writes=[f"QT{h}r"])
            for hp in range(2):
                bk, br = genh[0].next()
                lt = w_kn[:, hp * 128:(hp + 1) * 128]
                mm_group(bk[:, :], br, [(lt, cqn[2][0][:, :], [cqn[2][1], "w_kn"])])
                P.add("act", lambda e, bk=bk, hp=hp, tb=tb: e.activation(out=KT[0:64, 2 * hp, tb:tb + 512], in_=bk[0:64, :], func=AF.Copy), reads=[br], writes=[f"KT{b}_{2 * hp}"])
                P.add("dve", lambda e, bk=bk, hp=hp, tb=tb: e.tensor_copy(out=KT[0:64, 2 * hp + 1, tb:tb + 512], in_=bk[64:128, :]), reads=[br], writes=[f"KT{b}_{2 * hp + 1}"])
            ka, kar = inproj(768, 896)
            kb, kbr = inproj(0, 128, lw=w_krr)
            t1, t1r = F(5)
            t2, t2r = F(6)
            P.add("dve", lambda e, ka=ka, t1=t1: e.tensor_tensor(out=t1[64:96, :], in0=ka[64:96, :], in1=cos2[64:96, :], op=ALU.mult), reads=[kar, "cos2"], writes=[t1r])
            P.add("dve", lambda e, kb=kb, t2=t2: e.tensor_tensor(out=t2[64:96, :], in0=kb[64:96, :], in1=sin2[64:96, :], op=ALU.mult), reads=[kbr, "sin2"], writes=[t2r])
            for h in range(4):
                P.add("pool", lambda e, t1=t1, t2=t2, h=h, tb=tb: e.tensor_tensor(out=KT[64:96, h, tb:tb + 512], in0=t1[64:96, :], in1=t2[64:96, :], op=ALU.add),
                      reads=[t1r, t2r], writes=[f"KT{b}_{h}r"])
            for i2 in range(2):
                bk, br = genh[0].next()
                for ii in range(2):
                    i = 2 * i2 + ii
                    rh = w_v[:, :]
                    P.add("pe", lambda e, bk=bk, ii=ii, i=i, rh=rh: e.matmul(bk[:, ii * 256:(ii + 1) * 256], lhsT=cqn[2][0][:, i * 128:(i + 1) * 128], rhs=rh, start=True, stop=True),
                          reads=[cqn[2][1], "w_v"], writes=[br])
                for ii in range(2):
                    i = 2 * i2 + ii
                    dst = Vt[:, 4 * b + i, :, 0:64]
                    src = bk[:, ii * 256:(ii + 1) * 256].rearrange("p (h d) -> p h d", h=4)
                    P.add("act", lambda e, dst=dst, src=src: e.activation(out=dst, in_=src, func=AF.Copy), reads=[br, "Vinit"], writes=[f"V{4 * b + i}"])

            pre_ops.append(P.cap)
            P.cap = None
            nk = 4 * b + 4
            units = [(h, kt) for h in range(4) for kt in range(nk)]
            accs = {}
            pts = {}
            LA = 2

            def emit_s(h, kt):
                j0 = max(0, kt - 4 * b)
                q0 = j0 * 128
                sbk, sbr = stb.next()
                P.add("pe", lambda e, sbk=sbk, h=h, kt=kt, q0=q0: e.matmul(sbk[:, q0:512], lhsT=KT[:, h, kt * 128:(kt + 1) * 128], rhs=QT[:, h, q0:512], start=True, stop=True),
                      reads=[f"KT{kt // 4}_{h}", f"KT{kt // 4}_{h}r", f"QT{h}", f"QT{h}r", "KTpad", "QTpad"], writes=[sbr])
                pt, ptres = ptr.next()
                P.add("act", lambda e, sbk=sbk, pt=pt, q0=q0: e.activation(out=pt[:, q0:512], in_=sbk[:, q0:512], func=AF.Exp, scale=SCALE), reads=[sbr], writes=[ptres])
                if kt >= 4 * b:
                    P.add("pool", lambda e, pt=pt, q0=q0: e.tensor_tensor(out=pt[:, q0:q0 + 128], in0=pt[:, q0:q0 + 128], in1=maskT[:], op=ALU.mult),
                          reads=[ptres, "maskT"], writes=[ptres])
                pts[(h, kt)] = (pt, ptres, j0)

            def emit_pv(h, kt):
                if kt == 0:
                    accs[h] = pvb.next()
                acc, accr = accs[h]
                accv = acc[:, 0:260].rearrange("p (j d) -> p j d", j=4)
                pt, ptres, j0 = pts.pop((h, kt))
                for j in range(j0, 4):
                    first = (kt == 0 and j == j0)
                    P.add("pe", lambda e, pt=pt, j=j, kt=kt, h=h, first=first, accv=accv: e.matmul(accv[:, j, :], lhsT=pt[:, j * 128:(j + 1) * 128], rhs=Vt[:, kt, h, :],
                                                                                                start=first, stop=(kt == nk - 1 and j == 3), skip_group_check=True),
                          reads=[ptres, f"V{kt}", "Vinit"], writes=[accr])
                if kt == nk - 1:
                    P.add("dve", lambda e, accv=accv, h=h: e.reciprocal(out=sm[:, 8 + 4 * h:12 + 4 * h], in_=accv[:, :, 64]), reads=[accr], writes=[f"rec{h}"])
                    for j in range(4):
                        P.add("dve", lambda e, accv=accv, h=h, j=j: e.tensor_scalar(out=omla[:, j, h * 64:(h + 1) * 64], in0=accv[:, j, 0:64], scalar1=sm[:, 8 + 4 * h + j:9 + 4 * h + j],
                                                                                    scalar2=None, op0=ALU.mult),
                              reads=[accr, f"rec{h}"], writes=[f"omla{j}"])

            att_units = []
            for idx in range(len(units) + LA):
                P.cap = []
                if idx < len(units):
                    emit_s(*units[idx])
                if idx - LA >= 0:
                    emit_pv(*units[idx - LA])
                att_units.append(P.cap)
                P.cap = None
            streams = []
            def emit_mla_out():
                bk, br = genh[0].next()
                bv = bk[:, :].bitcast(BF16)
                for j in range(4):
                    P.add("act", lambda e, j=j: e.activation(out=junk[:, 0:256], in_=omla[:, j, :], func=AF.Square, accum_out=sm[:, 24 + j:25 + j]), reads=[f"omla{j}"], writes=[f"oss{j}"])
                P.add("act", lambda e: e.activation(out=sm[:, 28:32], in_=sm[:, 24:28], func=AF.Ln, scale=1.0 / 256, bias=epsc[:]), reads=[f"oss{j}" for j in range(4)] + ["eps"], writes=["osd"])
                P.add("act", lambda e: e.activation(out=sm[:, 28:32], in_=sm[:, 28:32], func=AF.Exp, scale=-0.5), reads=["osd"], writes=["osd"])
                for j in range(4):
                    mt = mtok[j % 2]
                    P.add("dve", lambda e, j=j, mt=mt: e.scalar_tensor_tensor(out=mt[:], in0=omla[:, j, :], scalar=sm[:, 28 + j:29 + j], in1=ggm[:], op0=ALU.mult, op1=ALU.mult),
                          reads=[f"omla{j}", "osd", "ggm"], writes=[f"mtok{j % 2}"])
                    for c in range(2):
                        P.add("pe", lambda e, mt=mt, c=c, j=j, bv=bv: e.transpose(out=bv[:, c * 512 + j * 128:c * 512 + (j + 1) * 128], in_=mt[:, c * 128:(c + 1) * 128], identity=ident[:]),
                              reads=[f"mtok{j % 2}", "ident"], writes=[br])
                P.add("act", lambda e, bv=bv: e.activation(out=mixedT[:, 2:4, :].rearrange("p c t -> p (c t)"), in_=bv[:, :], func=AF.Copy), reads=[br], writes=["mixedT2", "mixedT3"])

            P.cap = []
            genh[0] = gen1[0]
            for c in range(2):
                gb, gbr = inproj(256 + c * 128, 256 + (c + 1) * 128)
                sg, sgr = F(0)
                P.add("act", lambda e, gb=gb, sg=sg: e.activation(out=sg[:], in_=gb[:, :], func=AF.Exp, scale=-1.0), reads=[gbr], writes=[sgr])
                P.add("act", lambda e, sg=sg: e.activation(out=sg[:], in_=sg[:], func=AF.Ln, bias=onec[:]), reads=[sgr, "eps"], writes=[sgr])
                P.add("act", lambda e, sg=sg: e.activation(out=sg[:], in_=sg[:], func=AF.Exp, scale=-1.0), reads=[sgr], writes=[sgr])
                ab, abr = inproj(c * 128, (c + 1) * 128)
                P.add("dve", lambda e, ab=ab, sg=sg, c=c: e.tensor_tensor(out=ycv[:, c, 30:542], in0=ab[:, :], in1=sg[:], op=ALU.mult), reads=[abr, sgr], writes=[f"ycv{c}"])
            ycf = [F(1), F(2)]
            ycb = [H(0), H(1)]
            ysq = [H(2), H(3)]
            for c in range(2):
                bk, br = genh[0].next()
                for k in range(31):
                    ds = (c * 31 + k) % 8
                    P.add("dve", lambda e, c=c, k=k, ds=ds: e.tensor_scalar(out=dg[:, ds, :], in0=ident[:], scalar1=colp[:, c * 31 + k:c * 31 + k + 1], scalar2=None, op0=ALU.mult),
                          reads=["ident", "colp"], writes=[f"dg{ds}"])
                    P.add("pe", lambda e, bk=bk, c=c, k=k, ds=ds: e.matmul(bk[:, :], lhsT=dg[:, ds, :], rhs=ycv[:, c, k:k + 512], start=(k == 0), stop=(k == 30)),
                          reads=[f"dg{ds}", f"ycv{c}", f"ycvh{c}"], writes=[br])
                ft, fr = ycf[c]
                P.add("dve", lambda e, bk=bk, ft=ft, c=c: e.tensor_scalar(out=ft[:], in0=bk[:, :], scalar1=colp[:, 62 + c:63 + c], scalar2=None, op0=ALU.add), reads=[br, "colp"], writes=[fr])
                P.add("pool", lambda e, c=c: e.tensor_copy(out=ycv[:, c, 0:30], in_=ycv[:, c, 512:542]), reads=[f"ycv{c}"], writes=[f"ycvh{c}"])
                hb_, hbr = ycb[c]
                P.add("pool", lambda e, ft=ft, hb_=hb_: e.tensor_copy(out=hb_[:], in_=ft[:]), reads=[fr], writes=[hbr])
                hq, hqr = ysq[c]
                P.add("pool", lambda e, ft=ft, hq=hq: e.tensor_tensor(out=hq[:], in0=ft[:], in1=ft[:], op=ALU.mult), reads=[fr], writes=[hqr])
            mt_, mtr = F(3)
            m2, m2r = F(4)
            vr, vrr = F(5)
            mb, mbr = genh[0].next()
            mm_group(mb[:, :], mbr, [(ones[:, :], ycb[c][0][:, :], [ycb[c][1], "ones"]) for c in range(2)])
            P.add("dve", lambda e, mb=mb, mt_=mt_: e.tensor_scalar(out=mt_[:], in0=mb[:, :], scalar1=1.0 / 256, scalar2=None, op0=ALU.mult), reads=[mbr], writes=[mtr])
            qb, qbr = genh[0].next()
            mm_group(qb[:, :], qbr, [(ones[:, :], ysq[c][0][:, :], [ysq[c][1], "ones"]) for c in range(2)])
            P.add("pool", lambda e, mt_=mt_, m2=m2: e.tensor_tensor(out=m2[:], in0=mt_[:], in1=mt_[:], op=ALU.mult), reads=[mtr], writes=[m2r])
            P.add("dve", lambda e, qb=qb, m2=m2, vr=vr: e.scalar_tensor_tensor(out=vr[:], in0=qb[:, :], scalar=1.0 / 256, in1=m2[:], op0=ALU.mult, op1=ALU.subtract),
                  reads=[qbr, m2r], writes=[vrr])
            P.add("act", lambda e, vr=vr: e.activation(out=vr[:], in_=vr[:], func=AF.Ln, bias=epsc[:]), reads=[vrr, "eps"], writes=[vrr])
            P.add("act", lambda e, vr=vr: e.activation(out=vr[:], in_=vr[:], func=AF.Exp, scale=-0.5), reads=[vrr], writes=[vrr])
            sact = [H(0), H(1)]
            for c in range(2):
                ft, fr = ycf[c]
                P.add("dve", lambda e, ft=ft, mt_=mt_: e.tensor_tensor(out=ft[:], in0=ft[:], in1=mt_[:], op=ALU.subtract), reads=[fr, mtr], writes=[fr])
                P.add("pool", lambda e, ft=ft, vr=vr: e.tensor_tensor(out=ft[:], in0=ft[:], in1=vr[:], op=ALU.mult), reads=[fr, vrr], writes=[fr])
                ht, hres = sact[c]
                sg, sgr = F(0)
                P.add("dve", lambda e, ft=ft, c=c: e.tensor_scalar(out=ft[:], in0=ft[:], scalar1=colp[:, 64 + c:65 + c], scalar2=colp[:, 66 + c:67 + c], op0=ALU.mult, op1=ALU.add),
                      reads=[fr, "colp"], writes=[fr])
                P.add("act", lambda e, ft=ft, sg=sg: e.activation(out=sg[:], in_=ft[:], func=AF.Exp, scale=-1.0), reads=[fr], writes=[sgr])
                P.add("act", lambda e, sg=sg: e.activation(out=sg[:], in_=sg[:], func=AF.Ln, bias=onec[:]), reads=[sgr, "eps"], writes=[sgr])
                P.add("act", lambda e, sg=sg: e.activation(out=sg[:], in_=sg[:], func=AF.Exp, scale=-1.0), reads=[sgr], writes=[sgr])
                P.add("pool", lambda e, ft=ft, sg=sg, ht=ht: e.tensor_tensor(out=ht[:], in0=ft[:], in1=sg[:], op=ALU.mult), reads=[fr, sgr], writes=[hres])
            of = [F(3), F(4)]
            for co in range(2):
                bk, br = genh[0].next()
                mm_group(bk[:, :], br, [(pw_sb[:, ci, co * 128:(co + 1) * 128], sact[ci][0][:, :], [sact[ci][1], "pw"]) for ci in range(2)])
                ft, fr = of[co]
                P.add("dve", lambda e, bk=bk, ft=ft: e.tensor_copy(out=ft[:], in_=bk[:, :]), reads=[br], writes=[fr])
            group_norm_fm(0, [of[0][0][:], of[1][0][:]], [of[0][1], of[1][1]], [(Hm[2][:], "H2"), (Hm[3][:], "H3")], (Fm[5][:], "F5"))

            streams.append(P.cap)
            P.cap = []
            genh[0] = gen1[1]
            yp = [H(4), H(5)]
            for c in range(2):
                pbk, pbr = inproj(864 + c * 128, 864 + (c + 1) * 128)
                P.add("dve", lambda e, pbk=pbk, c=c: e.tensor_copy(out=zp[:, c, 15:527], in_=pbk[:, :]), reads=[pbr], writes=[f"zp{c}"])
                zc = zp[:, c, :]
                P.add("pool", lambda e, zc=zc: e.tensor_tensor(out=PA[:, 1:527], in0=zc[:, 1:527], in1=zc[:, 0:526], op=ALU.add), reads=[f"zp{c}", f"zph{c}"], writes=["PA"])
                P.add("pool", lambda e: e.tensor_tensor(out=PB[:, 3:527], in0=PA[:, 3:527], in1=PA[:, 1:525], op=ALU.add), reads=["PA"], writes=["PB"])
                if c == 0:
                    lo, hi = PA, PB
                    lor, hir = "PA", "PB"
                else:
                    P.add("pool", lambda e: e.tensor_tensor(out=PA[:, 7:527], in0=PB[:, 7:527], in1=PB[:, 3:523], op=ALU.add), reads=["PB", "PA"], writes=["PA"])
                    P.add("pool", lambda e: e.tensor_tensor(out=PB[:, 15:527], in0=PA[:, 15:527], in1=PA[:, 7:519], op=ALU.add), reads=["PA", "PB"], writes=["PB"])
                    lo, hi = PA, PB
                    lor, hir = "PA", "PB"
                ht, hres = yp[c]
                for (pr, srcT, srcr) in ((slice(0, 64), lo, lor), (slice(64, 128), hi, hir)):
                    P.add("dve", lambda e, pr=pr, srcT=srcT, c=c, ht=ht, zc=zc: e.scalar_tensor_tensor(out=ht[pr, :], in0=srcT[pr, 15:527], scalar=c_sb[pr, 288 + c:289 + c], in1=zc[pr, 15:527],
                                                                                                     op0=ALU.mult, op1=ALU.subtract),
                          reads=[srcr, f"zp{c}", "c"], writes=[hres])
                    if b == 0:
                        tt, ttr = F(6)
                        P.add("dve", lambda e, pr=pr, srcT=srcT, c=c, tt=tt: e.tensor_tensor(out=tt[pr, 0:16], in0=srcT[pr, 15:31], in1=c_sb[pr, 256 + c * 16:272 + c * 16], op=ALU.mult),
                              reads=[srcr, "c"], writes=[ttr])
                        P.add("dve", lambda e, pr=pr, c=c, tt=tt, ht=ht, zc=zc: e.tensor_tensor(out=ht[pr, 0:16], in0=tt[pr, 0:16], in1=zc[pr, 15:31], op=ALU.subtract),
                              reads=[ttr, f"zp{c}", hres], writes=[hres])
                P.add("pool", lambda e, c=c: e.tensor_copy(out=zp[:, c, 0:15], in_=zp[:, c, 512:527]), reads=[f"zp{c}"], writes=[f"zph{c}"])
            of = [(PA[:, 0:512], "PA"), (PB[:, 0:512], "PB")]
            for c in range(2):
                bk, br = genh[0].next()
                mm_group(bk[:, :], br, [(poolw_sb[:, c, :], yp[c][0][:, :], [yp[c][1], "poolw"])])
                ft, fr = of[c]
                P.add("dve", lambda e, bk=bk, ft=ft, c=c: e.tensor_scalar(out=ft, in0=bk[:, :], scalar1=colp[:, 71 + c:72 + c], scalar2=None, op0=ALU.mult), reads=[br, "colp"], writes=[fr])
            group_norm_fm(2, [of[0][0], of[1][0]], [of[0][1], of[1][1]], [(Hm[4][:], "H4"), (Hm[5][:], "H5")], (Fm[6][:], "F6"))

            streams.append(P.cap)
            P.cap = []
            genh[0] = gen1[2]
            for i2 in range(2):
                bk, br = genh[0].next()
                for ii in range(2):
                    i = 2 * i2 + ii
                    for k in range(8):
                        P.add("pe", lambda e, bk=bk, ii=ii, i=i, k=k: e.matmul(bk[:, ii * 256:(ii + 1) * 256], lhsT=xnT[:, k, i * 128:(i + 1) * 128], rhs=w_in_sb[:, k, 1376:1632],
                                                                            start=(k == 0), stop=(k == 7)),
                              reads=XNT + WIN, writes=[br])
                for ii in range(2):
                    i = 2 * i2 + ii
                    src = bk[:, ii * 256:(ii + 1) * 256]
                    P.add("act", lambda e, src=src, i=i: e.activation(out=junk[:, 0:256], in_=src, func=AF.Square, accum_out=sm[:, 32 + i:33 + i]), reads=[br], writes=[f"vss{i}"])
                lo = 2 * i2
                P.add("act", lambda e, lo=lo: e.activation(out=sm[:, 36 + lo:38 + lo], in_=sm[:, 32 + lo:34 + lo], func=AF.Ln, scale=1.0 / 256, bias=epsc[:]), reads=[f"vss{lo}", f"vss{lo + 1}", "eps"], writes=[f"vsdp{i2}"])
                P.add("act", lambda e, lo=lo: e.activation(out=sm[:, 36 + lo:38 + lo], in_=sm[:, 36 + lo:38 + lo], func=AF.Exp, scale=-0.5), reads=[f"vsdp{i2}"], writes=[f"vsdp{i2}"])
                for ii in range(2):
                    i = 2 * i2 + ii
                    src = bk[:, ii * 256:(ii + 1) * 256]
                    P.add("dve", lambda e, src=src, i=i: e.tensor_scalar(out=vn[:, i, :], in0=src, scalar1=sm[:, 36 + i:37 + i], scalar2=None, op0=ALU.mult), reads=[br, f"vsdp{i2}"], writes=[f"vn{i}"])
            of = [F(7), F(8)]
            gate = of
            for c in range(2):
                gb, gbr = genh[0].next()
                for i in range(4):
                    for hh in range(2):
                        h = 2 * c + hh
                        P.add("pe", lambda e, gb=gb, i=i, hh=hh, h=h: e.matmul(gb[hh * 64:(hh + 1) * 64, i * 128:(i + 1) * 128], lhsT=vn[:, i, h * 64:(h + 1) * 64], rhs=wsT_sb[:, h, :],
                                                                            start=True, stop=True),
                              reads=[f"vn{i}", "wsT"], writes=[gbr])
                gt, gtr = gate[c]
                bsb = bsT[:, c, :].unsqueeze(1).broadcast_to([128, 4, 128])
                P.add("dve", lambda e, gb=gb, gt=gt, c=c, bsb=bsb: e.scalar_tensor_tensor(out=gt[:].rearrange("p (i t) -> p i t", i=4), in0=gb[:, :].rearrange("p (i t) -> p i t", i=4),
                                                                                        scalar=colp[:, 73 + c:74 + c], in1=bsb, op0=ALU.mult, op1=ALU.add),
                      reads=[gbr, "colp", "bsT"], writes=[gtr])
                ub, ubr = inproj(1120 + c * 128, 1120 + (c + 1) * 128)
                ft, fr = of[c]
                P.add("dve", lambda e, ub=ub, gt=gt, ft=ft: e.tensor_tensor(out=ft[:], in0=ub[:, :], in1=gt[:], op=ALU.mult), reads=[ubr, gtr], writes=[fr])
            group_norm_fm(3, [of[0][0][:], of[1][0][:]], [of[0][1], of[1][1]],
                          [(vn[:, 0:2, :].rearrange("p a b -> p (a b)"), ["vn0", "vn1"]), (vn[:, 2:4, :].rearrange("p a b -> p (a b)"), ["vn2", "vn3"])], (Fm[9][:], "F9"))

            streams.append(P.cap)
            P.cap = None
            genh[0] = gen3
            main_ops.append((att_units, streams))
            P.cap = []
            genh[0] = genpost
            emit_mla_out()
            MX = [f"mixedT{k}" for k in range(8)]
            for i in range(4):
                t = 4 * b + i
                s = i % 2
                P.add("sp", lambda e, t=t, s=s: e.dma_start(out=hr[s][:], in_=hsrc[t * 128:(t + 1) * 128, :]), reads=[f"hd{t}"], writes=[f"hr{s}"], slot=f"hr{s}")
                for n in range(2):
                    bk, br = genh[0].next()
                    mm_group(bk[:, :], br, [(mixedT[:, k, i * 128:(i + 1) * 128], w_out_sb[:, k, n * 512:(n + 1) * 512], MX + WOUT) for k in range(8)])
                    P.add("dve", lambda e, bk=bk, s=s, n=n: e.tensor_tensor(out=hr[s][:, n * 512:(n + 1) * 512], in0=bk[:, :], in1=hr[s][:, n * 512:(n + 1) * 512], op=ALU.add),
                          reads=[br, f"hr{s}"], writes=[f"hr{s}"])
                P.add("sp", lambda e, t=t, s=s: e.dma_start(out=hbuf[t * 128:(t + 1) * 128, :], in_=hr[s][:]), reads=[f"hr{s}"], writes=[f"hd{t}"], slot=f"st{s}")
            post_ops.append(P.cap)
            P.cap = None
        def merge_main(att_units, streams, extra):
            strs = list(streams) + [extra]
            order = [0, 1, 0, 2, 3]
            totY = sum(len(x) for x in strs)
            per = max(3, -(-totY // max(1, len(att_units))))
            posn = [0] * len(strs)
            st = {"rr": 0}
            out = []

            def emit_y(n):
                done = 0
                while done < n and any(posn[i] < len(strs[i]) for i in range(len(strs))):
                    i = order[st["rr"] % len(order)]
                    st["rr"] += 1
                    if posn[i] < len(strs[i]):
                        out.append(strs[i][posn[i]])
                        posn[i] += 1
                        done += 1

            for u in att_units:
                out.extend(u)
                emit_y(per)
            emit_y(10 ** 9)
            return out

        P.replay(prenorm_ops[0])
        P.replay(pre_ops[0])
        for b in range(NB):
            att_units, streams = main_ops[b]
            P.replay(merge_main(att_units, streams, prenorm_ops[b + 1] if b + 1 < NB else []))
            A_ = post_ops[b]
            B_ = pre_ops[b + 1] if b + 1 < NB else []
            ia = ib = 0
            ra = max(1, len(A_))
            rb = max(1, len(B_))
            while ia < len(A_) or ib < len(B_):
                if ib >= len(B_) or (ia < len(A_) and ia * rb <= ib * ra):
                    P.replay([A_[ia]])
                    ia += 1
                else:
                    P.replay([B_[ib]])
                    ib += 1
        P.emit()


POOL_WINDOWS = (2, 4, 8, 16)


def make_consts():
    c = np.zeros((128, NCST), np.float32)
    c[:, 0:128] = np.eye(128, dtype=np.float32)
    k = np.arange(128)[:, None]
    q = np.arange(128)[None, :]
    c[:, 128:256] = (q >= k).astype(np.float32)
    for ch in range(2):
        for p in range(128):
            w = POOL_WINDOWS[2 * ch + p // 64]
            c[p, 288 + ch] = 1.0 / w
            for t in range(16):
                c[p, 256 + ch * 16 + t] = 1.0 / min(t + 1, w)
    inv_freq = (10000.0 ** (-np.arange(0, 32, 2, dtype=np.float32) / 32)).astype(np.float32)
    for p in range(32):
        c[p, 290] = inv_freq[p % 16]
    return c


def col128(v):
    v = np.asarray(v, np.float32)
    return np.ascontiguousarray(v.reshape(-1, 128).T)


def pack_layer_inputs(inp, layers):
    L = len(layers)
    colp = np.zeros((L, 128, NCOL), np.float32)
    rowp = np.zeros((L, NROW), np.float32)
    poolw = np.zeros((L, 128, 2, 128), np.float32)
    wsT = np.zeros((L, 128, 4, 128), np.float32)
    bsT = np.zeros((L, 128, 2, 128), np.float32)
    for li, l in enumerate(layers):
        dw = inp["conv_dw_w"][l]
        for c in range(2):
            colp[li, :, c * 31:(c + 1) * 31] = dw[:, c * 128:(c + 1) * 128].T
        colp[li, :, 62:64] = col128(inp["conv_dw_b"][l])
        colp[li, :, 64:66] = col128(inp["conv_ln_g"][l])
        colp[li, :, 66:68] = col128(inp["conv_ln_b"][l])
        qg = np.zeros(256, np.float32)
        qg[0:192] = inp["mla_q_norm_g"][l]
        colp[li, :, 68:70] = col128(qg)
        colp[li, :, 70:71] = col128(inp["mla_kv_norm_g"][l])
        colp[li, :, 71:73] = col128(inp["pool_scale"][l])
        colp[li, :, 73:75] = col128(inp["gmlp_norm_g"][l])
        colp[li, :, 75:83] = col128(inp["group_norm_g"][l].reshape(-1))
        rowp[li, 0:1024] = inp["mix_norm_g"][l]
        rowp[li, 1024:2048] = inp["ffn_norm_g"][l]
        rowp[li, 2048:2304] = inp["group_norm_g"][l, 1]
        pw_ = inp["pool_w"][l]
        for c in range(2):
            for hh in range(2):
                poolw[li, hh * 64:(hh + 1) * 64, c, hh * 64:(hh + 1) * 64] = pw_[2 * c + hh]
        wsT[li] = np.transpose(inp["gmlp_ws"][l], (2, 0, 1))
        bs = inp["gmlp_bs"][l]
        for c in range(2):
            for hh in range(2):
                bsT[li, hh * 64:(hh + 1) * 64, c, :] = bs[2 * c + hh][None, :]
    sl = list(layers)
    d = dict(
        w_in=np.ascontiguousarray(inp["w_in"][sl]), w_out=np.ascontiguousarray(inp["w_out"][sl]),
        w_uq=np.ascontiguousarray(inp["mla_w_uq"][sl]), w_ukv=np.ascontiguousarray(inp["mla_w_ukv"][sl]),
        conv_pw=np.ascontiguousarray(inp["conv_pw_w"][sl]), poolw=poolw, wsT=wsT, bsT=bsT, colp=colp, rowp=rowp,
        w_ff1=np.ascontiguousarray(inp["w_ff1"][sl]), w_ff2=np.ascontiguousarray(inp["w_ff2"][sl]),
        fng=np.ascontiguousarray(np.asarray(inp["final_norm_g"], np.float32).reshape(1, D)),
        cst=make_consts(),
    )
    return d


_PROGS = {}


def get_prog(T, n_layers, final):
    key = (T, n_layers, final)
    if key not in _PROGS:
        _PROGS[key] = build_program(T, n_layers, final)
    return _PROGS[key]


def run_layers(inp, hs, positions, layers, final, T):
    shared = pack_layer_inputs(inp, layers)
    nc = get_prog(T, len(layers), final)
    in_maps = []
    for ci in range(len(hs)):
        m = dict(shared)
        m["x"] = np.ascontiguousarray(hs[ci], dtype=np.float32)
        m["pos"] = np.ascontiguousarray(positions[ci].reshape(1, T).astype(np.int32))
        in_maps.append(m)
    res = run_bass_kernel_spmd(nc, in_maps, core_ids=list(range(len(hs))))
    return [np.asarray(r["out"]) for r in res.results]


FUSED = True


def kernel(**inputs):
    inp = {k: np.asarray(v) for k, v in inputs.items()}
    x = inp["x"].astype(np.float32)
    B, T, _ = x.shape
    depth = inp["w_in"].shape[0]
    positions = inp["positions"]
    hs = [x[b] for b in range(B)]
    if FUSED:
        outs = run_layers(inp, hs, positions, list(range(depth)), True, T)
    else:
        for l in range(depth):
            hs = run_layers(inp, hs, positions, [l], l == depth - 1, T)
        outs = hs
    return np.stack(outs, axis=0).astype(np.float32)
```
